# Optimizing a Trainium2 kernel written in Bass

```python
import jax, jax.numpy as jnp
from jax import lax
import numpy as np

D_MODEL = 2048
BATCH = 16
SEQ = 2048
DEPTH = 1

N_META = 16
NORM_EPS = 1e-6
D_FF = ((8 * D_MODEL // 3 + 255) // 256) * 256
RWKV_HEAD = 64
RWKV_HEADS = D_MODEL // RWKV_HEAD
RWKV_WIDTH = RWKV_HEADS * RWKV_HEAD
W_LORA = 96
A_LORA = 96
G_LORA = 256
GN_EPS = RWKV_HEAD * 1e-5
MLA_HEADS = D_MODEL // 128
Q_LORA = 512
KV_LORA = 512
NOPE_DIM = 128
ROPE_DIM = 64
V_DIM = 128
QK_DIM = NOPE_DIM + ROPE_DIM
ROPE_THETA = 10000.0
Q_BLOCK = 128
RWKV_COLS = 3 * RWKV_WIDTH + W_LORA + A_LORA + G_LORA
MLA_COLS = Q_LORA + KV_LORA + ROPE_DIM
GATE_COLS = 2 * D_MODEL
IN_COLS = RWKV_COLS + MLA_COLS + GATE_COLS

kernel_name = "macaron_rwkv7_mla_gated_hybrid"


def _split(x, sizes):
    idx = np.cumsum(sizes)[:-1].tolist()
    return jnp.split(x, idx, axis=-1)


def rms_norm(x, g):
    xf = x.astype(jnp.float32)
    y = xf * lax.rsqrt(jnp.mean(xf * xf, axis=-1, keepdims=True) + NORM_EPS)
    return (y * g.astype(jnp.float32)).astype(x.dtype)


def swiglu(h, w_gate, w_up, w_down):
    return (jax.nn.silu(h @ w_gate) * (h @ w_up)) @ w_down


def rope(x, cos, sin):
    x1, x2 = jnp.split(x.astype(jnp.float32), 2, axis=-1)
    c, s = cos[:, None, :], sin[:, None, :]
    return jnp.concatenate([x1 * c - x2 * s, x1 * s + x2 * c], axis=-1).astype(x.dtype)


def wkv7_scan(r, w, k, v, kk_neg, b):
    bsz, _, h, n = r.shape

    def step(S, inp):
        r_t, w_t, k_t, v_t, kn_t, b_t = inp
        sa = jnp.einsum("bhij,bhj->bhi", S, kn_t)
        S = S * w_t[:, :, None, :] + sa[..., None] * b_t[:, :, None, :] + v_t[..., None] * k_t[:, :, None, :]
        return S, jnp.einsum("bhij,bhj->bhi", S, r_t)

    xs = tuple(jnp.moveaxis(t, 1, 0) for t in (r, w, k, v, kk_neg, b))
    S0 = jnp.zeros((bsz, h, n, n), jnp.float32)
    _, y = lax.scan(step, S0, xs)
    return jnp.moveaxis(y, 0, 1)


def rwkv7_branch(p, mu, w0, w_up, a0, a_up, g_up, k_k, k_a, r_k, gn_w, gn_b):
    bsz, t, _ = p.shape
    prev = jnp.pad(p, ((0, 0), (1, 0), (0, 0)))[:, :-1]
    p = p + mu * (prev - p)
    r, k, v, xw, xa, xg = _split(p, (RWKV_WIDTH, RWKV_WIDTH, RWKV_WIDTH, W_LORA, A_LORA, G_LORA))
    w_pre = -jax.nn.softplus(-(w0 + jnp.tanh(xw) @ w_up)) - 0.5
    decay = jnp.exp(-jnp.exp(w_pre.astype(jnp.float32)))
    a = jax.nn.sigmoid(a0 + xa @ a_up)
    g = jax.nn.sigmoid(xg) @ g_up
    heads = lambda z: z.reshape(bsz, t, RWKV_HEADS, RWKV_HEAD).astype(jnp.float32)
    kk = heads(k * k_k)
    kk = kk * lax.rsqrt(jnp.maximum(jnp.sum(kk * kk, axis=-1, keepdims=True), 1e-24))
    k = k * (1.0 + (a - 1.0) * k_a)
    rh, kh, vh, ah, wh = heads(r), heads(k), heads(v), heads(a), heads(decay)
    y = wkv7_scan(rh, wh, kh, vh, -kk, kk * ah)
    mean = jnp.mean(y, axis=-1, keepdims=True)
    var = jnp.mean(jnp.square(y - mean), axis=-1, keepdims=True)
    y = ((y - mean) * lax.rsqrt(var + GN_EPS)).reshape(bsz, t, RWKV_WIDTH)
    y = y * gn_w.astype(jnp.float32) + gn_b.astype(jnp.float32)
    bonus = jnp.sum(rh * kh * r_k.astype(jnp.float32), axis=-1, keepdims=True) * vh
    y = y + bonus.reshape(bsz, t, RWKV_WIDTH)
    return (y * g.astype(jnp.float32)).astype(p.dtype)


def causal_block_attention(q, k, v):
    bsz, t, h, dq = q.shape
    scale = QK_DIM ** -0.5
    kpos = jnp.arange(t)

    def attend(qb, qpos):
        s = jnp.einsum("bqhd,bkhd->bhqk", qb, k).astype(jnp.float32) * scale
        s = jnp.where(kpos[None, :] <= qpos[:, None], s, -1e30)
        p = jax.nn.softmax(s, axis=-1).astype(v.dtype)
        return jnp.einsum("bhqk,bkhd->bqhd", p, v)

    meta_out = attend(q[:, :N_META], jnp.arange(N_META))
    n_real = t - N_META
    nblk = n_real // Q_BLOCK
    q_real = jnp.moveaxis(q[:, N_META:].reshape(bsz, nblk, Q_BLOCK, h, dq), 1, 0)

    def body(args):
        qb, i = args
        return attend(qb, N_META + i * Q_BLOCK + jnp.arange(Q_BLOCK))

    real = lax.map(body, (q_real, jnp.arange(nblk)))
    real = jnp.moveaxis(real, 0, 1).reshape(bsz, n_real, h, v.shape[-1])
    return jnp.concatenate([meta_out, real], axis=1)


def mla_branch(p, q_norm, w_uq, kv_norm, w_ukv, cos, sin):
    bsz, t, _ = p.shape
    c_q, c_kv, k_pe = _split(p, (Q_LORA, KV_LORA, ROPE_DIM))
    q = (rms_norm(c_q, q_norm) @ w_uq).reshape(bsz, t, MLA_HEADS, QK_DIM)
    q_nope, q_pe = _split(q, (NOPE_DIM, ROPE_DIM))
    kv = (rms_norm(c_kv, kv_norm) @ w_ukv).reshape(bsz, t, MLA_HEADS, NOPE_DIM + V_DIM)
    k_nope, v = _split(kv, (NOPE_DIM, V_DIM))
    q_pe = rope(q_pe, cos, sin)
    k_pe = rope(k_pe[:, :, None, :], cos, sin)
    q = jnp.concatenate([q_nope, q_pe], axis=-1)
    k = jnp.concatenate([k_nope, jnp.broadcast_to(k_pe, (bsz, t, MLA_HEADS, ROPE_DIM))], axis=-1)
    o = causal_block_attention(q, k, v)
    return o.reshape(bsz, t, MLA_HEADS * V_DIM)


def setup_inputs(seed: int = 0) -> dict:
    key = jax.random.key(seed)
    ks = jax.random.split(key, 32)
    L, D = DEPTH, D_MODEL
    nrm = lambda k, shape, s: jax.random.normal(k, shape, jnp.float32) * s
    uni = lambda k, shape: jax.random.uniform(k, shape, jnp.float32)
    return {
        "x": nrm(ks[0], (BATCH, SEQ, D), 1.0),
        "meta_tokens": nrm(ks[1], (N_META, D), 1.0),
        "ffn1_norm": 1.0 + nrm(ks[2], (L, D), 0.02),
        "ffn1_w_gate": nrm(ks[3], (L, D, D_FF), D ** -0.5),
        "ffn1_w_up": nrm(ks[4], (L, D, D_FF), D ** -0.5),
        "ffn1_w_down": nrm(ks[5], (L, D_FF, D), D_FF ** -0.5),
        "mix_norm": 1.0 + nrm(ks[6], (L, D), 0.02),
        "w_in": nrm(ks[7], (L, D, IN_COLS), D ** -0.5),
        "tm_mu": uni(ks[8], (L, RWKV_COLS)),
        "w0": -6.5 + 5.0 * uni(ks[9], (L, RWKV_WIDTH)),
        "w_up": nrm(ks[10], (L, W_LORA, RWKV_WIDTH), W_LORA ** -0.5),
        "a0": nrm(ks[11], (L, RWKV_WIDTH), 0.1),
        "a_up": nrm(ks[12], (L, A_LORA, RWKV_WIDTH), A_LORA ** -0.5),
        "g_up": nrm(ks[13], (L, G_LORA, RWKV_WIDTH), G_LORA ** -0.5),
        "k_k": 0.85 + nrm(ks[14], (L, RWKV_WIDTH), 0.02),
        "k_a": 1.0 + nrm(ks[15], (L, RWKV_WIDTH), 0.02),
        "r_k": nrm(ks[16], (L, RWKV_HEADS, RWKV_HEAD), 0.1),
        "gn_w": 1.0 + nrm(ks[17], (L, RWKV_WIDTH), 0.02),
        "gn_b": nrm(ks[18], (L, RWKV_WIDTH), 0.01),
        "q_norm": 1.0 + nrm(ks[19], (L, Q_LORA), 0.02),
        "w_uq": nrm(ks[20], (L, Q_LORA, MLA_HEADS * QK_DIM), Q_LORA ** -0.5),
        "kv_norm": 1.0 + nrm(ks[21], (L, KV_LORA), 0.02),
        "w_ukv": nrm(ks[22], (L, KV_LORA, MLA_HEADS * (NOPE_DIM + V_DIM)), KV_LORA ** -0.5),
        "w_out": nrm(ks[23], (L, D, D), D ** -0.5),
        "ffn2_norm": 1.0 + nrm(ks[24], (L, D), 0.02),
        "ffn2_w_gate": nrm(ks[25], (L, D, D_FF), D ** -0.5),
        "ffn2_w_up": nrm(ks[26], (L, D, D_FF), D ** -0.5),
        "ffn2_w_down": nrm(ks[27], (L, D_FF, D), D_FF ** -0.5),
        "final_norm": 1.0 + nrm(ks[28], (D,), 0.02),
    }


def reference(x, meta_tokens, ffn1_norm, ffn1_w_gate, ffn1_w_up, ffn1_w_down, mix_norm, w_in,
              tm_mu, w0, w_up, a0, a_up, g_up, k_k, k_a, r_k, gn_w, gn_b, q_norm, w_uq,
              kv_norm, w_ukv, w_out, ffn2_norm, ffn2_w_gate, ffn2_w_up, ffn2_w_down, final_norm):
    bsz = x.shape[0]
    h = jnp.concatenate([jnp.broadcast_to(meta_tokens.astype(x.dtype)[None], (bsz, N_META, D_MODEL)), x], axis=1)
    t = h.shape[1]
    pos = jnp.arange(t, dtype=jnp.float32)
    inv_freq = 1.0 / (ROPE_THETA ** (jnp.arange(0, ROPE_DIM, 2, dtype=jnp.float32) / ROPE_DIM))
    ang = pos[:, None] * inv_freq[None, :]
    cos, sin = jnp.cos(ang), jnp.sin(ang)

    for l in range(DEPTH):
        h = h + 0.5 * swiglu(rms_norm(h, ffn1_norm[l]), ffn1_w_gate[l], ffn1_w_up[l], ffn1_w_down[l])
        u = rms_norm(h, mix_norm[l])
        proj = u @ w_in[l]
        p_rwkv, p_mla, p_gate = _split(proj, (RWKV_COLS, MLA_COLS, GATE_COLS))
        y_a = rwkv7_branch(p_rwkv, tm_mu[l], w0[l], w_up[l], a0[l], a_up[l], g_up[l],
                           k_k[l], k_a[l], r_k[l], gn_w[l], gn_b[l])
        y_b = mla_branch(p_mla, q_norm[l], w_uq[l], kv_norm[l], w_ukv[l], cos, sin)
        g_a, g_b = jnp.split(jax.nn.sigmoid(p_gate), 2, axis=-1)
        h = h + (g_a * y_a + g_b * y_b) @ w_out[l]
        h = h + 0.5 * swiglu(rms_norm(h, ffn2_norm[l]), ffn2_w_gate[l], ffn2_w_up[l], ffn2_w_down[l])

    y = rms_norm(h, final_norm)[:, N_META:]
    return y
```

```python
import math
from contextlib import ExitStack
import numpy as np
import concourse.bass as bass
import concourse.mybir as mybir
from concourse.bass_utils import run_bass_kernel_spmd

F32 = mybir.dt.float32
BF16 = mybir.dt.bfloat16
U8 = mybir.dt.uint8
AF = mybir.ActivationFunctionType
ALU = mybir.AluOpType
AX = mybir.AxisListType

NCORES = 8
D = 2048
T = 2064
NB = 2
DFF = 5632
KT = D // 128
FT = DFF // 128
NT = 688
NTI = T // NT
NSUB = [(0, 344), (344, 344)]
NH = 32
MH = 16
EPS = 1e-6
GN_EPS = 64 * 1e-5
C_R, C_K, C_V = 0, 2048, 4096
C_XW, C_XA, C_XG = 6144, 6240, 6336
C_CQ, C_CKV, C_KPA, C_KPB, C_GA, C_GB = 6592, 7104, 7616, 7680, 7744, 9792
NIN = 11840
PV = {}
_o = 0
for _n, _w in [("ffn1_norm", 16), ("mix_norm", 16), ("mu_r", 16), ("mu_k", 16), ("mu_v", 16), ("w0", 16),
               ("a0", 16), ("k_k", 16), ("k_a", 16), ("r_k", 16), ("gn_w", 16), ("gn_b", 16),
               ("ffn2_norm", 16), ("final_norm", 16), ("q_norm", 4), ("kv_norm", 4),
               ("mu_xw", 1), ("mu_xa", 1), ("mu_xg", 2)]:
    PV[_n] = _o
    _o += _w
NPV = _o


class DSem:
    def __init__(self, name):
        self.name = name
        self.count = 0


class Sched:
    ENG = ("pe", "act", "dve", "pool", "sp")

    def __init__(self):
        self.q = {e: [] for e in self.ENG}
        self.cnt = {e: 0 for e in ("pe", "act", "dve", "pool")}
        self.bufs = {}
        self.waited = {e: {} for e in self.ENG}
        self.dsems = []
        self.barrier_tokens = []

    def dsem(self, name):
        d = DSem("d%d_%s" % (len(self.dsems), name))
        self.dsems.append(d)
        return d

    def barrier(self):
        toks = [(e, c) for e, c in self.cnt.items() if c > 0]
        toks += [(d.name, d.count) for d in self.dsems if d.count > 0]
        self.barrier_tokens = toks

    def op(self, eng, fn, reads=(), writes=(), dsem=None):
        deps = {}

        def add(tok):
            if tok is None:
                return
            s, v = tok
            if deps.get(s, 0) < v:
                deps[s] = v
        for k in reads:
            b = self.bufs.get(k)
            if b:
                add(b[0])
        for k in writes:
            b = self.bufs.get(k)
            if b:
                add(b[0])
                for s, v in b[1].items():
                    add((s, v))
        for tok in self.barrier_tokens:
            add(tok)
        if dsem is not None:
            dsem.count += 16
            token = (dsem.name, dsem.count)
            signal = (dsem.name, 16)
        else:
            self.cnt[eng] += 1
            token = (eng, self.cnt[eng])
            signal = (eng, 1)
        waits = []
        wd = self.waited[eng]
        for s, v in deps.items():
            if wd.get(s, 0) < v:
                wd[s] = v
                waits.append((s, v))
        for k in reads:
            b = self.bufs.setdefault(k, [None, {}])
            if b[1].get(token[0], 0) < token[1]:
                b[1][token[0]] = token[1]
        for k in writes:
            self.bufs[k] = [token, {}]
        self.q[eng].append((waits, fn, signal))
        return token

    def final_wait(self, eng="sp"):
        self.barrier()
        waits = []
        for s, v in self.barrier_tokens:
            if self.waited[eng].get(s, 0) < v:
                waits.append((s, v))
        self.q[eng].append((waits, None, None))

    def emit(self, eng, e, semh):
        for waits, fn, signal in self.q[eng]:
            for s, v in waits:
                e.wait_ge(semh[s], v)
            if fn is None:
                continue
            inst = fn(e)
            inst.then_inc(semh[signal[0]], signal[1])


class Arena:
    def __init__(self, ap, nbytes):
        self.ap = ap
        self.nbytes = nbytes
        self.off = 0

    def reset(self, off=0):
        self.off = off

    def alloc(self, shape, dtype, parts=128):
        esz = 4 if dtype == F32 else 2
        n = 1
        for s in shape:
            n *= s
        nb = (n * esz + 63) // 64 * 64
        assert self.off + nb <= self.nbytes, ("arena overflow", self.off, nb, self.nbytes)
        v = self.ap[0:parts, self.off:self.off + nb]
        self.off += nb
        v = v[:, 0:n * esz].bitcast(dtype)
        if len(shape) == 2:
            v = v.rearrange("p (a b) -> p a b", a=shape[0])
        elif len(shape) == 3:
            v = v.rearrange("p (a b c) -> p a b c", a=shape[0], b=shape[1])
        return v


def build_program(debug=()):
    nc = bass.Bass("TRN2", target_bir_lowering=False)
    S = Sched()
    NB1, NTI1 = (1, 1) if "one_tile" in debug else ((1, NTI) if "one_b" in debug else (NB, NTI))

    def din(name, shape, dt=F32):
        return nc.dram_tensor(name, list(shape), dt, kind="ExternalInput").ap()

    def dscr(name, shape, dt=F32):
        kind = "ExternalOutput" if name in debug else "Internal"
        return nc.dram_tensor(name, list(shape), dt, kind=kind).ap()

    hT_in = din("hT", [NB, D, T])
    pvec_in = din("pvec", [128, NPV])
    consts_in = din("consts", [128, 3 * 128])
    rope_in = din("rope", [64, 2 * T])
    wsrc = {
        "wg1": din("wg1", [D, DFF]), "wu1": din("wu1", [D, DFF]), "wd1": din("wd1", [DFF, D]),
        "win": din("win", [D, NIN]),
        "wup": din("wup", [96, D]), "aup": din("aup", [96, D]), "gup": din("gup", [256, D]),
        "wuq": din("wuq", [512, 4096]), "wk": din("wk", [512, D]), "wv": din("wv", [512, D]),
        "wout": din("wout", [D, D]),
        "wg2": din("wg2", [D, DFF]), "wu2": din("wu2", [D, DFF]), "wd2": din("wd2", [DFF, D]),
    }
    out_T = nc.dram_tensor("outT", [NB, D, T - 16], F32, kind="ExternalOutput").ap()
    wbf = {k: dscr("b_" + k, v.shape, BF16) for k, v in wsrc.items()}
    H1 = dscr("H1", [NB, D, T])
    PROJ = dscr("PROJ", [NB, NIN, T])
    SC = {k: dscr("SC_" + k, [NB, D, T]) for k in ("r", "w", "k", "v", "kn", "b")}
    GOUT = dscr("GOUT", [NB, D, T])
    BONUS = dscr("BONUS", [NB, D, T])
    YA = dscr("YA", [NB, D, T])
    YB = dscr("YB", [NB, D, T])

    ARENA_BYTES = 190 * 1024
    with ExitStack() as es:
        arena_t = es.enter_context(nc.sbuf_tensor("arena", [128, ARENA_BYTES], U8))
        psum_t = es.enter_context(nc.psum_tensor("psum", [128, 8, 512], F32))
        A = Arena(arena_t, ARENA_BYTES)

        def bank(i, n=512, parts=128):
            return psum_t[0:parts, i, 0:n]

        pvec = A.alloc([NPV], F32)
        consts = A.alloc([3 * 128], F32)
        cbf = A.alloc([2 * 128], BF16)
        base_off = A.off
        blk1 = consts[:, 0:128]
        ident_f = consts[:, 128:256]
        ident_b = cbf[:, 0:128]
        mask_b = cbf[:, 128:256]

        def pcol(name, c=0, parts=128):
            return pvec[0:parts, PV[name] + c:PV[name] + c + 1]

        d_pv = S.dsem("pvec")
        S.op("sp", lambda e: e.dma_start(out=pvec, in_=pvec_in), writes=["pvec"], dsem=d_pv)
        d_cs = S.dsem("consts")
        S.op("sp", lambda e: e.dma_start(out=consts, in_=consts_in), writes=["consts"], dsem=d_cs)
        S.op("dve", lambda e: e.tensor_copy(out=cbf, in_=consts[:, 128:384]), reads=["consts"], writes=["cbf"])

        def phase0():
            A.reset(base_off)
            CH = 4096
            NS = 3
            st_f = [A.alloc([CH], F32) for _ in range(NS)]
            st_b = [A.alloc([CH], BF16) for _ in range(NS)]
            ds = [S.dsem("cv%d" % i) for i in range(NS)]
            ds2 = [S.dsem("cvb%d" % i) for i in range(NS)]
            engs = ["act", "dve", "pool"]
            it = 0
            for name, src in wsrc.items():
                R, C = src.shape
                dst = wbf[name]
                nchunk = (C + CH - 1) // CH
                cw = (C + nchunk - 1) // nchunk
                for r0 in range(0, R, 128):
                    rp = min(128, R - r0)
                    for c0 in range(0, C, cw):
                        w = min(cw, C - c0)
                        s = it % NS
                        eng = engs[it % 3]
                        it += 1
                        f_ap = st_f[s][0:rp, 0:w]
                        b_ap = st_b[s][0:rp, 0:w]
                        S.op("sp", lambda e, f_ap=f_ap, src=src, r0=r0, rp=rp, c0=c0, w=w:
                             e.dma_start(out=f_ap, in_=src[r0:r0 + rp, c0:c0 + w]),
                             writes=["cvf%d" % s], dsem=ds[s])
                        if eng == "act":
                            fn = lambda e, f_ap=f_ap, b_ap=b_ap: e.activation(out=b_ap, in_=f_ap, func=AF.Copy)
                        else:
                            fn = lambda e, f_ap=f_ap, b_ap=b_ap: e.tensor_copy(out=b_ap, in_=f_ap)
                        S.op(eng, fn, reads=["cvf%d" % s], writes=["cvb%d" % s])
                        S.op("sp", lambda e, b_ap=b_ap, dst=dst, r0=r0, rp=rp, c0=c0, w=w:
                             e.dma_start(out=dst[r0:r0 + rp, c0:c0 + w], in_=b_ap),
                             reads=["cvb%d" % s], writes=["W_" + name], dsem=ds2[s])

        class Ctx:
            pass

        def rmsnorm(hT, uT, gname, hkey, ukey, sqs, rstd, pb0, ones_ap, nchunk=KT, dim=D, nt=NT, nsub=NSUB):
            for c in range(nchunk):
                sq = sqs[c % 2]
                sk = "sq%d" % (c % 2)
                S.op("act", lambda e, c=c, sq=sq: e.activation(out=sq[:, 0:nt], in_=hT[:, c, 0:nt], func=AF.Square),
                     reads=[hkey], writes=[sk])

                def mm(e, c=c, sq=sq):
                    i = None
                    for si, (n0, nw) in enumerate(nsub):
                        i = e.matmul(bank(pb0 + si, nw), ones_ap, sq[:, n0:n0 + nw],
                                     start=(c == 0), stop=(c == nchunk - 1))
                    return i
                S.op("pe", mm, reads=[sk, "consts", "ones"], writes=["ps%d" % (pb0 + si) for si in range(len(nsub))])
            for si, (n0, nw) in enumerate(nsub):
                S.op("dve", lambda e, si=si, n0=n0, nw=nw: e.tensor_scalar(
                    out=rstd[:, n0:n0 + nw], in0=bank(pb0 + si, nw), scalar1=1.0 / dim, scalar2=EPS,
                    op0=ALU.mult, op1=ALU.add), reads=["ps%d" % (pb0 + si)], writes=["rstd"])
            S.op("act", lambda e: e.activation(out=rstd[:, 0:nt], in_=rstd[:, 0:nt], func=AF.Sqrt),
                 reads=["rstd"], writes=["rstd"])
            S.op("dve", lambda e: e.reciprocal(out=rstd[:, 0:nt], in_=rstd[:, 0:nt]), reads=["rstd"], writes=["rstd"])
            for c in range(nchunk):
                S.op("dve", lambda e, c=c: e.scalar_tensor_tensor(
                    out=uT[:, c, 0:nt], in0=hT[:, c, 0:nt], scalar=pcol(gname, c), in1=rstd[:, 0:nt],
                    op0=ALU.mult, op1=ALU.mult), reads=[hkey, "rstd", "pvec"], writes=[ukey])

        psrr = [0]

        def gemm(wlist, kt, kp, panels, act, actkey, slots, evac, nsub, banks, tag):
            nw_ = len(wlist)
            npan = len(panels)

            def load(pi):
                c0, pw, _ = panels[pi]
                sl_ap, sl_key, sl_ds = slots[pi % len(slots)]
                for wi, (wname, wap) in enumerate(wlist):
                    if kt > 1:
                        src = wap.rearrange("(c p) m -> p c m", p=kp)[:, :, c0:c0 + pw]
                    else:
                        src = wap[:, c0:c0 + pw].unsqueeze(1)
                    S.op("sp", lambda e, sl_ap=sl_ap, wi=wi, src=src, pw=pw: e.dma_start(
                        out=sl_ap[0:kp, wi, 0:kt, 0:pw], in_=src),
                        reads=["W_" + wname], writes=[sl_key], dsem=sl_ds)
            load(0)
            for pi in range(npan):
                if pi + 1 < npan:
                    load(pi + 1)
                c0, pw, chunks = panels[pi]
                sl_ap, sl_key, sl_ds = slots[pi % len(slots)]
                for (cc0, cw) in chunks:
                    for si, (n0, nw) in enumerate(nsub):
                        bks = []
                        for wi in range(nw_):
                            bk = banks[psrr[0] % len(banks)]
                            psrr[0] += 1
                            bks.append(bk)

                            def mm(e, wi=wi, bk=bk, cc0=cc0, cw=cw, n0=n0, nw=nw, sl_ap=sl_ap, c0=c0):
                                i = None
                                for k in range(kt):
                                    i = e.matmul(bank(bk, nw, cw), sl_ap[0:kp, wi, k, cc0 - c0:cc0 - c0 + cw],
                                                 act[0:kp, k, n0:n0 + nw], start=(k == 0), stop=(k == kt - 1))
                                return i
                            S.op("pe", mm, reads=[sl_key, actkey], writes=["ps%d" % bk])
                        evac(cc0, cw, si, n0, nw, bks)

        def mk_panels(chunks, pw=256):
            panels = []
            cur = []
            for (c0, w) in chunks:
                if cur and (c0 + w - cur[0][0] > pw or cur[-1][0] + cur[-1][1] != c0):
                    panels.append((cur[0][0], cur[-1][0] + cur[-1][1] - cur[0][0], cur))
                    cur = []
                cur.append((c0, w))
            if cur:
                panels.append((cur[0][0], cur[-1][0] + cur[-1][1] - cur[0][0], cur))
            return panels

        def ffn(X, wg, wu, wd):
            pan = mk_panels([(c, 128) for c in range(0, DFF, 128)])

            def ev_gu(cc0, cw, si, n0, nw, bks):
                f = cc0 // 128
                tm = X.tmp[si % 2]
                tk = "tmp%d" % (si % 2)
                S.op("act", lambda e: e.activation(out=tm[:, 0:nw], in_=bank(bks[0], nw), func=AF.Silu),
                     reads=["ps%d" % bks[0]], writes=[tk])
                S.op("dve", lambda e: e.tensor_tensor(out=X.aT[:, f, n0:n0 + nw], in0=bank(bks[1], nw),
                                                      in1=tm[:, 0:nw], op=ALU.mult),
                     reads=["ps%d" % bks[1], tk], writes=["aT"])
            gemm([(wg, wbf[wg]), (wu, wbf[wu])], KT, 128, pan, X.uT, "uT", X.slots_gu, ev_gu, NSUB,
                 [0, 1, 2, 3, 4, 5], "gu")
            pan_d = mk_panels([(c, 128) for c in range(0, D, 128)], pw=128)

            def ev_d(cc0, cw, si, n0, nw, bks):
                m = cc0 // 128
                S.op("dve", lambda e: e.scalar_tensor_tensor(
                    out=X.hT[:, m, n0:n0 + nw], in0=bank(bks[0], nw), scalar=0.5, in1=X.hT[:, m, n0:n0 + nw],
                    op0=ALU.mult, op1=ALU.add), reads=["ps%d" % bks[0], "hT"], writes=["hT"])
            gemm([(wd, wbf[wd])], FT, 128, pan_d, X.aT, "aT", X.slots_d, ev_d, NSUB, [0, 1, 2, 3, 4, 5], "dn")

        def ffn_arena():
            A.reset(base_off)
            X = Ctx()
            X.hT = A.alloc([KT, NT], F32)
            X.uT = A.alloc([KT, NT], BF16)
            X.aT = A.alloc([FT, NT], BF16)
            X.sqs = [A.alloc([NT], F32) for _ in range(2)]
            X.rstd = A.alloc([NT], F32)
            X.tmp = [A.alloc([344], F32) for _ in range(2)]
            X.d_h = S.dsem("hT")
            slot_bytes = 2 * KT * 256 * 2
            X.slots_gu = []
            X.slots_d = []
            X.slots_in = []
            for i in range(2):
                off = A.off
                raw = A.alloc([slot_bytes // 2], BF16)
                ds = S.dsem("wslot%d_%d" % (i, len(S.dsems)))
                key = "wslot%d" % i
                X.slots_gu.append((raw[:, 0:2 * KT * 256].rearrange("p (w k m) -> p w k m", w=2, k=KT), key, ds))
                X.slots_d.append((raw[:, 0:FT * 128].rearrange("p (w k m) -> p w k m", w=1, k=FT), key, ds))
                X.slots_in.append((raw[:, 0:KT * 256].rearrange("p (w k m) -> p w k m", w=1, k=KT), key, ds))
            return X

        def phase1():
            X = ffn_arena()
            NST = 4
            stg = [A.alloc([NT], F32) for _ in range(NST)]
            dst = [S.dsem("stg%d_%d" % (i, len(S.dsems))) for i in range(NST)]
            chunks = [(c, 128) for c in range(0, C_XW, 128)]
            chunks += [(C_XW, 96), (C_XA, 96), (C_XG, 128), (C_XG + 128, 128)]
            chunks += [(c, 128) for c in range(C_CQ, C_KPA, 128)]
            chunks += [(C_KPA, 64), (C_KPB, 64)]
            chunks += [(c, 128) for c in range(C_GA, NIN, 128)]
            pan_in = mk_panels(chunks)
            ones_f = None
            for b in range(NB1):
                for ti in range(NTI1):
                    t0 = ti * NT
                    S.op("sp", lambda e, b=b, t0=t0: e.dma_start(
                        out=X.hT, in_=hT_in[b].rearrange("(c p) t -> p c t", p=128)[:, :, t0:t0 + NT]),
                        writes=["hT"], dsem=X.d_h)
                    rmsnorm(X.hT, X.uT, "ffn1_norm", "hT", "uT", X.sqs, X.rstd, 6, ones_all)
                    ffn(X, "wg1", "wu1", "wd1")
                    S.op("sp", lambda e, b=b, t0=t0: e.dma_start(
                        out=H1[b].rearrange("(c p) t -> p c t", p=128)[:, :, t0:t0 + NT], in_=X.hT),
                        reads=["hT"], writes=["H1"], dsem=X.d_h)
                    rmsnorm(X.hT, X.uT, "mix_norm", "hT", "uT", X.sqs, X.rstd, 6, ones_all)
                    cnt = [0]

                    def ev_in(cc0, cw, si, n0, nw, bks, b=b, t0=t0):
                        s = cnt[0] % NST
                        sk = "stg%d" % s
                        gate = cc0 >= C_GA
                        if gate:
                            S.op("act", lambda e: e.activation(out=stg[s][0:cw, n0:n0 + nw], in_=bank(bks[0], nw, cw),
                                                               func=AF.Sigmoid),
                                 reads=["ps%d" % bks[0]], writes=[sk])
                        else:
                            eng = "act" if (cnt[0] % 2 == 0) else "dve"
                            if eng == "act":
                                fn = lambda e: e.activation(out=stg[s][0:cw, n0:n0 + nw], in_=bank(bks[0], nw, cw),
                                                            func=AF.Copy)
                            else:
                                fn = lambda e: e.tensor_copy(out=stg[s][0:cw, n0:n0 + nw], in_=bank(bks[0], nw, cw))
                            S.op(eng, fn, reads=["ps%d" % bks[0]], writes=[sk])
                        if si == len(NSUB) - 1:
                            S.op("sp", lambda e: e.dma_start(out=PROJ[b, cc0:cc0 + cw, t0:t0 + NT],
                                                             in_=stg[s][0:cw, 0:NT]),
                                 reads=[sk], writes=["PROJ"], dsem=dst[s])
                            cnt[0] += 1
                    gemm([("win", wbf["win"])], KT, 128, pan_in, X.uT, "uT", X.slots_in, ev_in, NSUB,
                         [0, 1, 2, 3, 4, 5], "in")


        def phase2():
            A.reset(base_off)
            NP1 = NT + 1
            wup_s = A.alloc([D], BF16)
            aup_s = A.alloc([D], BF16)
            gup_s = A.alloc([2, D], BF16)
            d_l = S.dsem("lora")
            S.op("sp", lambda e: e.dma_start(out=wup_s[0:96, :], in_=wbf["wup"]), reads=["W_wup"], writes=["lw"], dsem=d_l)
            S.op("sp", lambda e: e.dma_start(out=aup_s[0:96, :], in_=wbf["aup"]), reads=["W_aup"], writes=["lw"], dsem=d_l)
            S.op("sp", lambda e: e.dma_start(out=gup_s, in_=wbf["gup"].rearrange("(c p) m -> p c m", p=128)),
                 reads=["W_gup"], writes=["lw"], dsem=d_l)
            raw = {n: [(A.alloc([NP1], F32), "raw_%s%d" % (n, i), S.dsem("raw_%s%d" % (n, i))) for i in range(2)]
                   for n in ("r", "k", "v")}
            xs = A.alloc([4, NP1], F32)
            d_xs = S.dsem("xs")
            txw = A.alloc([NT], BF16)
            txa = A.alloc([NT], BF16)
            tsg = A.alloc([2, NT], BF16)
            names = ["dtmp", "tmpf", "sh_r", "sh_k", "sh_v", "kk", "a_t", "wdec", "tmpA", "tmpB", "kn", "bt", "tq", "k2", "rk", "bon", "gsb"]
            W_ = {n: A.alloc([NT], F32) for n in names}
            DS = {n: S.dsem("o_" + n) for n in ("sh_r", "wdec", "k2", "sh_v", "kn", "bt", "bon", "gsb")}
            rr = [0]

            def nb_():
                b_ = [0, 1, 2, 3, 4, 5][rr[0] % 6]
                rr[0] += 1
                return b_

            def shift(dst, dkey, src, skey, mucol, parts=128):
                S.op("dve", lambda e: e.tensor_tensor(out=W_["dtmp"][0:parts, :], in0=src[0:parts, 0:NT], in1=src[0:parts, 1:NP1],
                                                      op=ALU.subtract), reads=[skey], writes=["dtmp"])
                S.op("dve", lambda e: e.scalar_tensor_tensor(out=dst, in0=W_["dtmp"][0:parts, :], scalar=mucol,
                                                             in1=src[0:parts, 1:NP1], op0=ALU.mult, op1=ALU.add),
                     reads=["dtmp", skey, "pvec"], writes=[dkey])

            def load_halo(dst, key, ds, b, row0, nrows, t0):
                if t0 == 0:
                    S.op("pool", lambda e: e.memset(dst[0:nrows, 0:1], 0.0), writes=[key])
                    S.op("sp", lambda e: e.dma_start(out=dst[0:nrows, 1:NP1], in_=PROJ[b, row0:row0 + nrows, 0:NT]),
                         reads=["PROJ"], writes=[key], dsem=ds)
                else:
                    S.op("sp", lambda e: e.dma_start(out=dst[0:nrows, 0:NP1], in_=PROJ[b, row0:row0 + nrows, t0 - 1:t0 + NT]),
                         reads=["PROJ"], writes=[key], dsem=ds)

            def mm_ev(mms, parts, evac):
                for (n0, nw) in NSUB:
                    bk = nb_()

                    def f(e, bk=bk, n0=n0, nw=nw):
                        i = None
                        for j, (l, rf) in enumerate(mms):
                            i = e.matmul(bank(bk, nw, parts), l, rf(n0, nw), start=(j == 0), stop=(j == len(mms) - 1))
                        return i
                    S.op("pe", f, reads=["lw", "txw", "txa", "tsg", "consts", "tmpB", "rk"], writes=["ps%d" % bk])
                    evac(bank(bk, nw, parts), n0, nw, "ps%d" % bk)

            def store(name, dst_ap):
                S.op("sp", lambda e: e.dma_start(out=dst_ap, in_=W_[name]), reads=[name], writes=["SCR"], dsem=DS[name])

            for b in range(NB1):
                for ti in range(NTI1):
                    t0 = ti * NT
                    for i, (r0, nr) in enumerate([(C_XW, 96), (C_XA, 96), (C_XG, 128), (C_XG + 128, 128)]):
                        load_halo(xs[:, i, :], "xs", d_xs, b, r0, nr, t0)
                    shift(W_["tmpf"][0:96, :], "tmpf", xs[:, 0, :], "xs", pcol("mu_xw", 0, 96), 96)
                    S.op("act", lambda e: e.activation(out=txw[0:96, :], in_=W_["tmpf"][0:96, :], func=AF.Tanh),
                         reads=["tmpf"], writes=["txw"])
                    shift(txa[0:96, :], "txa", xs[:, 1, :], "xs", pcol("mu_xa", 0, 96), 96)
                    for c in range(2):
                        shift(W_["tmpf"], "tmpf", xs[:, 2 + c, :], "xs", pcol("mu_xg", c))
                        S.op("act", lambda e, c=c: e.activation(out=tsg[:, c, :], in_=W_["tmpf"], func=AF.Sigmoid),
                             reads=["tmpf"], writes=["tsg"])
                    for m in range(KT):
                        ms = slice(m * 128, (m + 1) * 128)
                        for n, c0 in (("r", C_R), ("k", C_K), ("v", C_V)):
                            ap_, key, ds = raw[n][m % 2]
                            load_halo(ap_, key, ds, b, c0 + m * 128, 128, t0)
                            shift(W_["sh_" + n], "sh_" + n, ap_, key, pcol("mu_" + n, m))
                        mm_ev([(wup_s[0:96, ms], lambda n0, nw: txw[0:96, n0:n0 + nw])], 128,
                              lambda bp, n0, nw, bk, m=m: S.op("act", lambda e: e.activation(
                                  out=W_["tmpA"][:, n0:n0 + nw], in_=bp, func=AF.Sigmoid, bias=pcol("w0", m)),
                                  reads=[bk, "pvec"], writes=["tmpA"]))
                        S.op("act", lambda e: e.activation(out=W_["wdec"], in_=W_["tmpA"], func=AF.Exp, scale=-math.exp(-0.5)),
                             reads=["tmpA"], writes=["wdec"])
                        mm_ev([(aup_s[0:96, ms], lambda n0, nw: txa[0:96, n0:n0 + nw])], 128,
                              lambda bp, n0, nw, bk, m=m: S.op("act", lambda e: e.activation(
                                  out=W_["a_t"][:, n0:n0 + nw], in_=bp, func=AF.Sigmoid, bias=pcol("a0", m)),
                                  reads=[bk, "pvec"], writes=["a_t"]))
                        mm_ev([(gup_s[:, 0, ms], lambda n0, nw: tsg[:, 0, n0:n0 + nw]),
                               (gup_s[:, 1, ms], lambda n0, nw: tsg[:, 1, n0:n0 + nw])], 128,
                              lambda bp, n0, nw, bk: S.op("act", lambda e: e.activation(
                                  out=W_["gsb"][:, n0:n0 + nw], in_=bp, func=AF.Copy), reads=[bk], writes=["gsb"]))
                        S.op("dve", lambda e, m=m: e.tensor_scalar(out=W_["kk"], in0=W_["sh_k"], scalar1=pcol("k_k", m), scalar2=None,
                                                                   op0=ALU.mult), reads=["sh_k", "pvec"], writes=["kk"])
                        S.op("act", lambda e: e.activation(out=W_["tmpB"], in_=W_["kk"], func=AF.Square), reads=["kk"], writes=["tmpB"])
                        mm_ev([(blk1, lambda n0, nw: W_["tmpB"][:, n0:n0 + nw])], 128,
                              lambda bp, n0, nw, bk: S.op("dve", lambda e: e.tensor_scalar(
                                  out=W_["tq"][:, n0:n0 + nw], in0=bp, scalar1=1e-24, scalar2=None, op0=ALU.max),
                                  reads=[bk], writes=["tq"]))
                        S.op("act", lambda e: e.activation(out=W_["tq"], in_=W_["tq"], func=AF.Sqrt), reads=["tq"], writes=["tq"])
                        S.op("dve", lambda e: e.reciprocal(out=W_["tq"], in_=W_["tq"]), reads=["tq"], writes=["tq"])
                        S.op("dve", lambda e: e.scalar_tensor_tensor(out=W_["kn"], in0=W_["kk"], scalar=-1.0, in1=W_["tq"],
                                                                     op0=ALU.mult, op1=ALU.mult), reads=["kk", "tq"], writes=["kn"])
                        S.op("dve", lambda e: e.scalar_tensor_tensor(out=W_["bt"], in0=W_["kn"], scalar=-1.0, in1=W_["a_t"],
                                                                     op0=ALU.mult, op1=ALU.mult), reads=["kn", "a_t"], writes=["bt"])
                        S.op("dve", lambda e, m=m: e.tensor_scalar(out=W_["tq"], in0=W_["a_t"], scalar1=-1.0, scalar2=pcol("k_a", m),
                                                                   op0=ALU.add, op1=ALU.mult), reads=["a_t", "pvec"], writes=["tq"])
                        S.op("dve", lambda e: e.scalar_tensor_tensor(out=W_["k2"], in0=W_["tq"], scalar=1.0, in1=W_["sh_k"],
                                                                     op0=ALU.add, op1=ALU.mult), reads=["tq", "sh_k"], writes=["k2"])
                        S.op("dve", lambda e, m=m: e.scalar_tensor_tensor(out=W_["rk"], in0=W_["sh_r"], scalar=pcol("r_k", m), in1=W_["k2"],
                                                                          op0=ALU.mult, op1=ALU.mult), reads=["sh_r", "k2", "pvec"], writes=["rk"])
                        mm_ev([(blk1, lambda n0, nw: W_["rk"][:, n0:n0 + nw])], 128,
                              lambda bp, n0, nw, bk: S.op("dve", lambda e: e.tensor_tensor(
                                  out=W_["bon"][:, n0:n0 + nw], in0=bp, in1=W_["sh_v"][:, n0:n0 + nw], op=ALU.mult),
                                  reads=[bk, "sh_v"], writes=["bon"]))
                        tsl = slice(t0, t0 + NT)
                        store("sh_r", SC["r"][b, ms, tsl]); store("wdec", SC["w"][b, ms, tsl]); store("k2", SC["k"][b, ms, tsl])
                        store("sh_v", SC["v"][b, ms, tsl]); store("kn", SC["kn"][b, ms, tsl]); store("bt", SC["b"][b, ms, tsl])
                        store("bon", BONUS[b, ms, tsl]); store("gsb", GOUT[b, ms, tsl])

        def phase4():
            A.reset(base_off)
            TC = 32
            PH = NB1 * 32
            P2_ = 2 * PH
            sets = []
            for i in range(2):
                tl = {n: A.alloc([64, TC], F32) for n in ("r", "w", "k", "kn", "b")}
                tl["v"] = A.alloc([32, TC], F32)
                sets.append((tl, S.dsem("scin%d" % i), "scin%d" % i))
            ysets = [(A.alloc([32, TC], F32), S.dsem("yout%d" % i), "yout%d" % i) for i in range(2)]
            St = A.alloc([32, 64], F32)
            Sw = [A.alloc([32, 64], F32) for _ in range(2)]
            T1 = A.alloc([32, 64], F32)
            T2 = A.alloc([32, 64], F32)
            T3 = A.alloc([32, 64], F32)
            sa = A.alloc([32], F32)
            S.op("dve", lambda e: e.memset(St[0:P2_], 0.0), writes=["St"])
            nch = (T + TC - 1) // TC

            def load(c):
                tl, ds, key = sets[c % 2]
                t0 = c * TC
                tc = min(TC, T - t0)
                for n in ("r", "w", "k", "kn", "b"):
                    src = SC[n][0:NB1].rearrange("b (h j) t -> (b h) j t", j=64)
                    for half in range(2):
                        for jh in range(2):
                            S.op("sp", lambda e, n=n, half=half, jh=jh, src=src, tl=tl, t0=t0, tc=tc: e.dma_start(
                                out=tl[n][half * PH:(half + 1) * PH, jh * 32:(jh + 1) * 32, 0:tc],
                                in_=src[:, jh * 32:(jh + 1) * 32, t0:t0 + tc]), reads=["SCR"], writes=[key], dsem=ds)
                srcv = SC["v"][0:NB1].rearrange("b (h x i) t -> (b h) x i t", x=2, i=32)
                for half in range(2):
                    S.op("sp", lambda e, half=half, tl=tl, t0=t0, tc=tc: e.dma_start(
                        out=tl["v"][half * PH:(half + 1) * PH, :, 0:tc], in_=srcv[:, half, :, t0:t0 + tc]),
                        reads=["SCR"], writes=[key], dsem=ds)
            load(0)
            dsty = YA[0:NB1].rearrange("b (h x i) t -> (b h) x i t", x=2, i=32)
            step = 0
            for c in range(nch):
                if c + 1 < nch:
                    load(c + 1)
                tl, ds, key = sets[c % 2]
                yt, yds, ykey = ysets[c % 2]
                t0 = c * TC
                tc = min(TC, T - t0)
                for tt in range(tc):
                    def bj(n):
                        return tl[n][0:P2_, :, tt].unsqueeze(1).broadcast_to([P2_, 32, 64])
                    vb = tl["v"][0:P2_, :, tt].unsqueeze(2).broadcast_to([P2_, 32, 64])
                    sw = Sw[step % 2]
                    swk = "Sw%d" % (step % 2)
                    S.op("pool", lambda e, vb=vb, kb=bj("k"): e.tensor_tensor(out=T3[0:P2_], in0=vb, in1=kb, op=ALU.mult),
                         reads=[key], writes=["T3"])
                    S.op("pool", lambda e, sw=sw, wb=bj("w"): e.tensor_tensor(out=sw[0:P2_], in0=St[0:P2_], in1=wb, op=ALU.mult),
                         reads=[key, "St"], writes=[swk])
                    S.op("pool", lambda e, sw=sw: e.tensor_tensor(out=sw[0:P2_], in0=sw[0:P2_], in1=T3[0:P2_], op=ALU.add),
                         reads=[swk, "T3"], writes=[swk])
                    S.op("dve", lambda e, knb=bj("kn"): e.tensor_tensor(out=T1[0:P2_], in0=St[0:P2_], in1=knb, op=ALU.mult),
                         reads=[key, "St"], writes=["T1"])
                    S.op("dve", lambda e: e.tensor_reduce(out=sa[0:P2_], in_=T1[0:P2_], axis=AX.X, op=ALU.add),
                         reads=["T1"], writes=["sa"])
                    S.op("dve", lambda e, bb=bj("b"): e.tensor_tensor(
                        out=T2[0:P2_], in0=sa[0:P2_].unsqueeze(2).broadcast_to([P2_, 32, 64]), in1=bb, op=ALU.mult),
                        reads=[key, "sa"], writes=["T2"])
                    S.op("dve", lambda e, sw=sw: e.tensor_tensor(out=St[0:P2_], in0=sw[0:P2_], in1=T2[0:P2_], op=ALU.add),
                         reads=[swk, "T2"], writes=["St"])
                    S.op("dve", lambda e, rb=bj("r"): e.tensor_tensor(out=T1[0:P2_], in0=St[0:P2_], in1=rb, op=ALU.mult),
                         reads=[key, "St"], writes=["T1"])
                    S.op("dve", lambda e, yt=yt, tt=tt: e.tensor_reduce(out=yt[0:P2_, :, tt], in_=T1[0:P2_], axis=AX.X, op=ALU.add),
                         reads=["T1"], writes=[ykey])
                    step += 1
                for half in range(2):
                    S.op("sp", lambda e, half=half, yt=yt, t0=t0, tc=tc: e.dma_start(
                        out=dsty[:, half, :, t0:t0 + tc], in_=yt[half * PH:(half + 1) * PH, :, 0:tc]),
                        reads=[ykey], writes=["YA"], dsem=yds)

        def phase3():
            A.reset(base_off)
            SUBS = [(0, 512), (512, 512), (1024, 512), (1536, 512), (2048, 16)]
            SCALE = 192.0 ** -0.5
            cqn = A.alloc([4, T], BF16)
            ckvn = A.alloc([4, T], BF16)
            kpeT = A.alloc([T], BF16)
            ropeT = A.alloc([2, T], F32)
            wuq_s = A.alloc([4, 4096], BF16)
            wk_s = A.alloc([4, D], BF16)
            wv_s = A.alloc([4, D], BF16)
            d_w = S.dsem("mlaw")
            S.op("sp", lambda e: e.dma_start(out=ropeT[0:64], in_=rope_in.rearrange("p (a t) -> p a t", a=2)),
                 writes=["rope"], dsem=S.dsem("rope"))
            S.op("sp", lambda e: e.dma_start(out=wuq_s, in_=wbf["wuq"].rearrange("(c p) m -> p c m", p=128)),
                 reads=["W_wuq"], writes=["mlaw"], dsem=d_w)
            S.op("sp", lambda e: e.dma_start(out=wk_s, in_=wbf["wk"].rearrange("(c p) m -> p c m", p=128)),
                 reads=["W_wk"], writes=["mlaw"], dsem=d_w)
            S.op("sp", lambda e: e.dma_start(out=wv_s, in_=wbf["wv"].rearrange("(c p) m -> p c m", p=128)),
                 reads=["W_wv"], writes=["mlaw"], dsem=d_w)
            rawc = A.alloc([4, 512], F32)
            d_rc = S.dsem("rawc")
            rawk = A.alloc([2, 512], F32)
            d_rk = S.dsem("rawk")
            sq2 = [A.alloc([512], F32) for _ in range(2)]
            rs = A.alloc([512], F32)
            t1 = A.alloc([512], F32)
            t2 = A.alloc([512], F32)
            qnT = A.alloc([T], BF16)
            qpeT = A.alloc([T], BF16)
            knT = A.alloc([T], BF16)
            Vh = A.alloc([17, 128], BF16)
            Pt = A.alloc([16 + 2048], BF16)
            PTs = [A.alloc([128], BF16) for _ in range(4)]
            ybh = A.alloc([T], F32)
            d_yb = S.dsem("ybh")
            st_ = A.alloc([8], F32)
            pt_bank = [psum_t[:, 5, 0:256].bitcast(BF16), psum_t[:, 7, 0:256].bitcast(BF16)]

            def norm_lat(b, row0, dstT, gname, dkey):
                for (n0, nw) in SUBS:
                    S.op("sp", lambda e, n0=n0, nw=nw: e.dma_start(
                        out=rawc[:, :, 0:nw], in_=PROJ[b, row0:row0 + 512, n0:n0 + nw].rearrange("(c p) t -> p c t", p=128)),
                        reads=["PROJ"], writes=["rawc"], dsem=d_rc)
                    for c in range(4):
                        S.op("act", lambda e, c=c, nw=nw: e.activation(out=sq2[c % 2][:, 0:nw], in_=rawc[:, c, 0:nw], func=AF.Square),
                             reads=["rawc"], writes=["sqm%d" % (c % 2)])
                        S.op("pe", lambda e, c=c, nw=nw: e.matmul(bank(6, nw), ones_all, sq2[c % 2][:, 0:nw], start=(c == 0), stop=(c == 3)),
                             reads=["sqm%d" % (c % 2), "ones"], writes=["ps6"])
                    S.op("dve", lambda e, nw=nw: e.tensor_scalar(out=rs[:, 0:nw], in0=bank(6, nw), scalar1=1.0 / 512, scalar2=EPS,
                                                                 op0=ALU.mult, op1=ALU.add), reads=["ps6"], writes=["rs"])
                    S.op("act", lambda e, nw=nw: e.activation(out=rs[:, 0:nw], in_=rs[:, 0:nw], func=AF.Sqrt), reads=["rs"], writes=["rs"])
                    S.op("dve", lambda e, nw=nw: e.reciprocal(out=rs[:, 0:nw], in_=rs[:, 0:nw]), reads=["rs"], writes=["rs"])
                    for c in range(4):
                        S.op("dve", lambda e, c=c, n0=n0, nw=nw: e.scalar_tensor_tensor(
                            out=dstT[:, c, n0:n0 + nw], in0=rawc[:, c, 0:nw], scalar=pcol(gname, c), in1=rs[:, 0:nw],
                            op0=ALU.mult, op1=ALU.mult), reads=["rawc", "rs", "pvec"], writes=[dkey])

            def rope_comb(dst, dkey, a_ap, b_ap, n0, nw, rkeys):
                S.op("dve", lambda e: e.tensor_tensor(out=t1[0:64, 0:nw], in0=a_ap, in1=ropeT[0:64, 0, n0:n0 + nw], op=ALU.mult),
                     reads=rkeys + ["rope"], writes=["t1"])
                S.op("dve", lambda e: e.tensor_tensor(out=t2[0:64, 0:nw], in0=b_ap, in1=ropeT[0:64, 1, n0:n0 + nw], op=ALU.mult),
                     reads=rkeys + ["rope"], writes=["t2"])
                S.op("dve", lambda e: e.tensor_tensor(out=dst[0:64, n0:n0 + nw], in0=t1[0:64, 0:nw], in1=t2[0:64, 0:nw], op=ALU.add),
                     reads=["t1", "t2"], writes=[dkey])

            brr = [0]

            def proj(lhs_fn, M, src, skey, evac):
                for (n0, nw) in SUBS:
                    bk = [0, 1, 2, 3][brr[0] % 4]
                    brr[0] += 1

                    def f(e, bk=bk, n0=n0, nw=nw):
                        i = None
                        for k in range(4):
                            i = e.matmul(bank(bk, nw, M), lhs_fn(k), src[:, k, n0:n0 + nw], start=(k == 0), stop=(k == 3))
                        return i
                    S.op("pe", f, reads=["mlaw", skey], writes=["ps%d" % bk])
                    evac(bank(bk, nw, M), "ps%d" % bk, n0, nw)

            for b in range(NB1):
                norm_lat(b, C_CQ, cqn, "q_norm", "cqn")
                norm_lat(b, C_CKV, ckvn, "kv_norm", "ckvn")
                for (n0, nw) in SUBS:
                    S.op("sp", lambda e, n0=n0, nw=nw, b=b: e.dma_start(out=rawk[0:64, 0, 0:nw], in_=PROJ[b, C_KPA:C_KPA + 64, n0:n0 + nw]),
                         reads=["PROJ"], writes=["rawk"], dsem=d_rk)
                    S.op("sp", lambda e, n0=n0, nw=nw, b=b: e.dma_start(out=rawk[0:64, 1, 0:nw], in_=PROJ[b, C_KPB:C_KPB + 64, n0:n0 + nw]),
                         reads=["PROJ"], writes=["rawk"], dsem=d_rk)
                    rope_comb(kpeT, "kpeT", rawk[0:64, 0, 0:nw], rawk[0:64, 1, 0:nw], n0, nw, ["rawk"])
                for h in range(MH):
                    proj(lambda k, h=h: wuq_s[:, k, h * 256:h * 256 + 128], 128, cqn, "cqn",
                         lambda bp, bk, n0, nw: S.op("act", lambda e: e.activation(out=qnT[:, n0:n0 + nw], in_=bp, func=AF.Copy),
                                                     reads=[bk], writes=["qnT"]))
                    for (n0, nw) in SUBS:
                        bka, bkb = 0, 1

                        def fa(e, n0=n0, nw=nw, h=h):
                            i = None
                            for k in range(4):
                                i = e.matmul(bank(0, nw, 64), wuq_s[:, k, h * 256 + 128:h * 256 + 192], cqn[:, k, n0:n0 + nw],
                                             start=(k == 0), stop=(k == 3))
                            for k in range(4):
                                i = e.matmul(bank(1, nw, 64), wuq_s[:, k, h * 256 + 192:h * 256 + 256], cqn[:, k, n0:n0 + nw],
                                             start=(k == 0), stop=(k == 3))
                            return i
                        S.op("pe", fa, reads=["mlaw", "cqn"], writes=["ps0", "ps1"])
                        rope_comb(qpeT, "qpeT", bank(0, nw, 64), bank(1, nw, 64), n0, nw, ["ps0", "ps1"])
                    proj(lambda k, h=h: wk_s[:, k, h * 128:(h + 1) * 128], 128, ckvn, "ckvn",
                         lambda bp, bk, n0, nw: S.op("act", lambda e: e.activation(out=knT[:, n0:n0 + nw], in_=bp, func=AF.Copy),
                                                     reads=[bk], writes=["knT"]))
                    for kb in range(17):
                        tk0, tw = (0, 16) if kb == 0 else (16 + 128 * (kb - 1), 128)
                        bk = [2, 3][kb % 2]

                        def fv(e, bk=bk, tk0=tk0, tw=tw, h=h):
                            i = None
                            for k in range(4):
                                i = e.matmul(bank(bk, 128, tw), ckvn[:, k, tk0:tk0 + tw], wv_s[:, k, h * 128:(h + 1) * 128],
                                             start=(k == 0), stop=(k == 3))
                            return i
                        S.op("pe", fv, reads=["mlaw", "ckvn"], writes=["ps%d" % bk])
                        S.op("dve", lambda e, bk=bk, kb=kb, tw=tw: e.tensor_copy(out=Vh[0:tw, kb, :], in_=bank(bk, 128, tw)),
                             reads=["ps%d" % bk], writes=["Vh"])
                    pti = 0
                    for qb in range(17):
                        q0, qw = (0, 16) if qb == 0 else (16 + 128 * (qb - 1), 128)
                        nreal = 128 * qb
                        nkc = (nreal + 511) // 512

                        def fs(e, q0=q0, qw=qw, qb=qb, nreal=nreal, nkc=nkc):
                            i = e.matmul(bank(4, 16, qw), qnT[:, q0:q0 + qw], knT[:, 0:16], start=True, stop=False)
                            i = e.matmul(bank(4, 16, qw), qpeT[0:64, q0:q0 + qw], kpeT[0:64, 0:16], start=False, stop=(qb != 0))
                            if qb == 0:
                                i = e.matmul(bank(4, 16, 16), ident_b[0:16, 0:16], mask_b[0:16, 0:16], start=False, stop=True)
                            for kc in range(nkc):
                                k0 = 16 + 512 * kc
                                kw = min(512, nreal - 512 * kc)
                                last = (kc == nkc - 1)
                                i = e.matmul(bank(kc, kw, qw), qnT[:, q0:q0 + qw], knT[:, k0:k0 + kw], start=True, stop=False)
                                i = e.matmul(bank(kc, kw, qw), qpeT[0:64, q0:q0 + qw], kpeT[0:64, k0:k0 + kw], start=False, stop=not last)
                                if last:
                                    off = kw - 128
                                    i = e.matmul(psum_t[0:qw, kc, off:off + 128], ident_b, mask_b, start=False, stop=True)
                            return i
                        S.op("pe", fs, reads=["qnT", "qpeT", "knT", "kpeT", "cbf"], writes=["ps0", "ps1", "ps2", "ps3", "ps4"])
                        sreal = psum_t[0:qw, 0:4, :].rearrange("p a b -> p (a b)")[:, 0:max(nreal, 1)]
                        S.op("dve", lambda e, qw=qw: e.tensor_reduce(out=st_[0:qw, 1:2], in_=bank(4, 16, qw), axis=AX.X, op=ALU.max),
                             reads=["ps4"], writes=["st"])
                        if qb > 0:
                            S.op("dve", lambda e, qw=qw, sreal=sreal: e.tensor_reduce(out=st_[0:qw, 0:1], in_=sreal, axis=AX.X, op=ALU.max),
                                 reads=["ps0", "ps1", "ps2", "ps3"], writes=["st"])
                            S.op("dve", lambda e, qw=qw: e.tensor_tensor(out=st_[0:qw, 2:3], in0=st_[0:qw, 0:1], in1=st_[0:qw, 1:2], op=ALU.max),
                                 reads=["st"], writes=["st"])
                        else:
                            S.op("dve", lambda e, qw=qw: e.tensor_copy(out=st_[0:qw, 2:3], in_=st_[0:qw, 1:2]), reads=["st"], writes=["st"])
                        S.op("dve", lambda e, qw=qw: e.tensor_scalar(out=st_[0:qw, 3:4], in0=st_[0:qw, 2:3], scalar1=-SCALE, scalar2=None,
                                                                     op0=ALU.mult), reads=["st"], writes=["st"])
                        S.op("act", lambda e, qw=qw: e.activation(out=Pt[0:qw, 0:16], in_=bank(4, 16, qw), func=AF.Exp, bias=st_[0:qw, 3:4],
                                                                   scale=SCALE, accum_out=st_[0:qw, 5:6]),
                             reads=["ps4", "st"], writes=["Pt", "st"])
                        if qb > 0:
                            S.op("act", lambda e, qw=qw, sreal=sreal, nreal=nreal: e.activation(
                                out=Pt[0:qw, 16:16 + nreal], in_=sreal, func=AF.Exp, bias=st_[0:qw, 3:4], scale=SCALE,
                                accum_out=st_[0:qw, 4:5]), reads=["ps0", "ps1", "ps2", "ps3", "st"], writes=["Pt", "st"])
                            S.op("dve", lambda e, qw=qw: e.tensor_tensor(out=st_[0:qw, 6:7], in0=st_[0:qw, 4:5], in1=st_[0:qw, 5:6], op=ALU.add),
                                 reads=["st"], writes=["st"])
                        else:
                            S.op("dve", lambda e, qw=qw: e.tensor_copy(out=st_[0:qw, 6:7], in_=st_[0:qw, 5:6]), reads=["st"], writes=["st"])
                        S.op("dve", lambda e, qw=qw: e.reciprocal(out=st_[0:qw, 7:8], in_=st_[0:qw, 6:7]), reads=["st"], writes=["st"])
                        S.op("dve", lambda e, qw=qw, nreal=nreal: e.tensor_scalar(
                            out=Pt[0:qw, 0:16 + nreal], in0=Pt[0:qw, 0:16 + nreal], scalar1=st_[0:qw, 7:8], scalar2=None, op0=ALU.mult),
                            reads=["Pt", "st"], writes=["Pt"])
                        nkb = qb + 1
                        for kb in range(nkb):
                            c0, kw = (0, 16) if kb == 0 else (16 + 128 * (kb - 1), 128)
                            s4 = pti % 4
                            pti += 1
                            ptp = pt_bank[s4 // 2][:, (s4 % 2) * 256:(s4 % 2) * 256 + 128]
                            S.op("pe", lambda e, ptp=ptp, c0=c0, kw=kw, qw=qw: e.transpose(ptp[0:kw, 0:qw], Pt[0:qw, c0:c0 + kw], ident_b[0:qw, 0:qw]),
                                 reads=["Pt", "cbf"], writes=["ptp%d" % s4])
                            eng = "act" if kb % 2 == 0 else "dve"
                            if eng == "act":
                                fn = lambda e, ptp=ptp, s4=s4, kw=kw, qw=qw: e.activation(out=PTs[s4][0:kw, 0:qw], in_=ptp[0:kw, 0:qw], func=AF.Copy)
                            else:
                                fn = lambda e, ptp=ptp, s4=s4, kw=kw, qw=qw: e.tensor_copy(out=PTs[s4][0:kw, 0:qw], in_=ptp[0:kw, 0:qw])
                            S.op(eng, fn, reads=["ptp%d" % s4], writes=["PTs%d" % s4])
                            S.op("pe", lambda e, kb=kb, kw=kw, qw=qw, s4=s4, nkb=nkb: e.matmul(
                                bank(6, qw), Vh[0:kw, kb, :], PTs[s4][0:kw, 0:qw], start=(kb == 0), stop=(kb == nkb - 1)),
                                reads=["Vh", "PTs%d" % s4], writes=["ps6"])
                        S.op("act", lambda e, q0=q0, qw=qw: e.activation(out=ybh[:, q0:q0 + qw], in_=bank(6, qw), func=AF.Copy),
                             reads=["ps6"], writes=["ybh"])
                    S.op("sp", lambda e, b=b, h=h: e.dma_start(out=YB[b, h * 128:(h + 1) * 128, :], in_=ybh),
                         reads=["ybh"], writes=["YB"], dsem=d_yb)
                if b == 0 and "DBG_kpe" in debug:
                    for nm, ap_, key_, parts in (("DBG_kpe", kpeT, "kpeT", 64), ("DBG_kn", knT, "knT", 128),
                                                 ("DBG_qpe", qpeT, "qpeT", 64), ("DBG_qn", qnT, "qnT", 128)):
                        dd = dscr(nm, [128, T], BF16)
                        S.op("sp", lambda e, dd=dd, ap_=ap_, parts=parts: e.dma_start(out=dd[0:parts, :], in_=ap_[0:parts, :]),
                             reads=[key_], writes=[nm], dsem=S.dsem(nm))

        def phase5():
            X = ffn_arena()
            NL = 6
            ld = [[(A.alloc([NT], F32), "ld%d_%d" % (i, j), S.dsem("ld%d_%d" % (i, j))) for j in range(NL)] for i in range(1)]
            wk_ = {"sq0": X.sqs[0], "sq1": X.sqs[1], "rstd": X.rstd}
            zT = X.aT[:, 0:KT, :]
            outf = X.aT[:, 0:32, :].rearrange("p a n -> p (a n)").bitcast(F32).rearrange("p (a n) -> p a n", a=KT)
            d_o = S.dsem("outf")
            pan_o = mk_panels([(c, 128) for c in range(0, D, 128)])
            rr = [0]
            for b in range(NB1):
                for ti in range(NTI1):
                    t0 = ti * NT
                    tsl = slice(t0, t0 + NT)
                    S.op("sp", lambda e, b=b, t0=t0: e.dma_start(
                        out=X.hT, in_=H1[b].rearrange("(c p) t -> p c t", p=128)[:, :, t0:t0 + NT]),
                        reads=["H1"], writes=["hT"], dsem=X.d_h)
                    for m in range(KT):
                        ms = slice(m * 128, (m + 1) * 128)
                        srcs = [YA[b, ms, tsl], BONUS[b, ms, tsl], GOUT[b, ms, tsl],
                                PROJ[b, C_GA + m * 128:C_GA + (m + 1) * 128, tsl], PROJ[b, C_GB + m * 128:C_GB + (m + 1) * 128, tsl],
                                YB[b, ms, tsl]]
                        L = ld[0]
                        for j, sap in enumerate(srcs):
                            S.op("sp", lambda e, j=j, sap=sap: e.dma_start(out=L[j][0], in_=sap),
                                 reads=["YA", "SCR", "PROJ", "YB"], writes=[L[j][1]], dsem=L[j][2])
                        y_, bo_, g_, ga_, gb_, yb_ = [L[j][0] for j in range(6)]
                        yk, bok, gk, gak, gbk, ybk = [L[j][1] for j in range(6)]

                        def blkmm(src_ap, skey, evac):
                            for (n0, nw) in NSUB:
                                bk = [6, 7][rr[0] % 2]
                                rr[0] += 1
                                S.op("pe", lambda e, bk=bk, n0=n0, nw=nw: e.matmul(bank(bk, nw), blk1, src_ap[:, n0:n0 + nw], start=True, stop=True),
                                     reads=[skey, "consts"], writes=["ps%d" % bk])
                                evac(bank(bk, nw), "ps%d" % bk, n0, nw)
                        blkmm(y_, yk, lambda bp, bk, n0, nw: S.op("dve", lambda e: e.scalar_tensor_tensor(
                            out=wk_["sq0"][:, n0:n0 + nw], in0=bp, scalar=-1.0 / 64, in1=y_[:, n0:n0 + nw], op0=ALU.mult, op1=ALU.add),
                            reads=[bk, yk], writes=["sq0"]))
                        S.op("act", lambda e: e.activation(out=wk_["sq1"], in_=wk_["sq0"], func=AF.Square), reads=["sq0"], writes=["sq1"])
                        blkmm(wk_["sq1"], "sq1", lambda bp, bk, n0, nw: S.op("dve", lambda e: e.tensor_scalar(
                            out=wk_["rstd"][:, n0:n0 + nw], in0=bp, scalar1=1.0 / 64, scalar2=GN_EPS, op0=ALU.mult, op1=ALU.add),
                            reads=[bk], writes=["rstd"]))
                        S.op("act", lambda e: e.activation(out=wk_["rstd"], in_=wk_["rstd"], func=AF.Sqrt), reads=["rstd"], writes=["rstd"])
                        S.op("dve", lambda e: e.reciprocal(out=wk_["rstd"], in_=wk_["rstd"]), reads=["rstd"], writes=["rstd"])
                        S.op("dve", lambda e: e.tensor_tensor(out=wk_["sq0"], in0=wk_["sq0"], in1=wk_["rstd"], op=ALU.mult),
                             reads=["sq0", "rstd"], writes=["sq0"])
                        S.op("dve", lambda e, m=m: e.tensor_scalar(out=wk_["sq0"], in0=wk_["sq0"], scalar1=pcol("gn_w", m), scalar2=pcol("gn_b", m),
                                                                   op0=ALU.mult, op1=ALU.add), reads=["sq0", "pvec"], writes=["sq0"])
                        S.op("pool", lambda e: e.tensor_tensor(out=wk_["sq0"], in0=wk_["sq0"], in1=bo_, op=ALU.add), reads=["sq0", bok], writes=["sq0"])
                        S.op("pool", lambda e: e.tensor_tensor(out=wk_["sq0"], in0=wk_["sq0"], in1=g_, op=ALU.mult), reads=["sq0", gk], writes=["sq0"])
                        S.op("pool", lambda e: e.tensor_tensor(out=wk_["sq0"], in0=wk_["sq0"], in1=ga_, op=ALU.mult), reads=["sq0", gak], writes=["sq0"])
                        S.op("pool", lambda e: e.tensor_tensor(out=wk_["sq1"], in0=yb_, in1=gb_, op=ALU.mult), reads=[ybk, gbk], writes=["sq1"])
                        S.op("dve", lambda e, m=m: e.tensor_tensor(out=zT[:, m, :], in0=wk_["sq0"], in1=wk_["sq1"], op=ALU.add),
                             reads=["sq0", "sq1"], writes=["aT"])

                    def ev_o(cc0, cw, si, n0, nw, bks):
                        m = cc0 // 128
                        S.op("dve", lambda e: e.tensor_tensor(out=X.hT[:, m, n0:n0 + nw], in0=bank(bks[0], nw), in1=X.hT[:, m, n0:n0 + nw],
                                                              op=ALU.add), reads=["ps%d" % bks[0], "hT"], writes=["hT"])
                    gemm([("wout", wbf["wout"])], KT, 128, pan_o, zT, "aT", X.slots_in, ev_o, NSUB, [0, 1, 2, 3, 4, 5], "wo")
                    rmsnorm(X.hT, X.uT, "ffn2_norm", "hT", "uT", X.sqs, X.rstd, 6, ones_all)
                    ffn(X, "wg2", "wu2", "wd2")
                    rmsnorm(X.hT, outf, "final_norm", "hT", "aT", X.sqs, X.rstd, 6, ones_all)
                    if ti == 0:
                        S.op("sp", lambda e, b=b: e.dma_start(out=out_T[b].rearrange("(c p) t -> p c t", p=128)[:, :, 0:NT - 16],
                                                              in_=outf[:, :, 16:NT]), reads=["aT"], writes=["OUT"], dsem=d_o)
                    else:
                        S.op("sp", lambda e, b=b, t0=t0: e.dma_start(out=out_T[b].rearrange("(c p) t -> p c t", p=128)[:, :, t0 - 16:t0 - 16 + NT],
                                                                     in_=outf), reads=["aT"], writes=["OUT"], dsem=d_o)

        ones_all = None
        ones_all = A.alloc([128], F32)
        base_off = A.off
        S.op("pool", lambda e: e.memset(ones_all, 1.0), writes=["ones"])

        phase0()
        S.barrier()
        if "stop0" not in debug:
            phase1()
            S.barrier()
        if "stop1" not in debug and "stop0" not in debug:
            phase2()
            S.barrier()
            if "no3" not in debug:
                phase3()
                S.barrier()
            if "no4" not in debug:
                phase4()
                S.barrier()
            if "no5" not in debug:
                phase5()
                S.barrier()

        S.final_wait("sp")

        semh = {}
        for e_ in ("pe", "act", "dve", "pool"):
            semh[e_] = es.enter_context(nc.semaphore("s_" + e_))
        for d in S.dsems:
            semh[d.name] = es.enter_context(nc.semaphore(d.name))
        block = es.enter_context(nc.Block())

        @block.tensor
        def _(e):
            S.emit("pe", e, semh)

        @block.scalar
        def _(e):
            S.emit("act", e, semh)

        @block.vector
        def _(e):
            S.emit("dve", e, semh)

        @block.gpsimd
        def _(e):
            S.emit("pool", e, semh)

        @block.sync
        def _(e):
            S.emit("sp", e, semh)
    return nc


def host_prep(inp):
    f = lambda a: np.ascontiguousarray(np.asarray(a, dtype=np.float32))
    x = f(inp["x"])
    meta = f(inp["meta_tokens"])
    w_in = f(inp["w_in"])[0]
    kpe = w_in[:, 7616:7680]
    win_ext = np.concatenate([w_in[:, :7616], kpe[:, :32], kpe[:, :32], kpe[:, 32:], kpe[:, 32:], w_in[:, 7680:]], axis=1)
    wuq = f(inp["w_uq"])[0].reshape(512, 16, 192)
    wuq_ext = np.concatenate([wuq[:, :, :128], wuq[:, :, 128:160], wuq[:, :, 128:160], wuq[:, :, 160:192],
                              wuq[:, :, 160:192]], axis=2).reshape(512, 4096)
    wukv = f(inp["w_ukv"])[0].reshape(512, 16, 256)
    wk = np.ascontiguousarray(wukv[:, :, :128].reshape(512, 2048))
    wv = np.ascontiguousarray(wukv[:, :, 128:].reshape(512, 2048))
    pv = np.zeros((128, NPV), np.float32)

    def put(name, vec, n):
        v = f(vec).reshape(-1)
        if v.size >= 128:
            pv[:, PV[name]:PV[name] + n] = v.reshape(n, 128).T
        else:
            pv[:v.size, PV[name]] = v
    mu = f(inp["tm_mu"])[0]
    put("ffn1_norm", inp["ffn1_norm"], 16); put("mix_norm", inp["mix_norm"], 16)
    put("mu_r", mu[0:2048], 16); put("mu_k", mu[2048:4096], 16); put("mu_v", mu[4096:6144], 16)
    put("w0", inp["w0"], 16); put("a0", inp["a0"], 16); put("k_k", inp["k_k"], 16); put("k_a", inp["k_a"], 16)
    put("r_k", inp["r_k"], 16); put("gn_w", inp["gn_w"], 16); put("gn_b", inp["gn_b"], 16)
    put("ffn2_norm", inp["ffn2_norm"], 16); put("final_norm", inp["final_norm"], 16)
    put("q_norm", inp["q_norm"], 4); put("kv_norm", inp["kv_norm"], 4)
    put("mu_xw", mu[6144:6240], 1); put("mu_xa", mu[6240:6336], 1); put("mu_xg", mu[6336:6592], 2)
    consts = np.zeros((128, 384), np.float32)
    consts[:64, :64] = 1.0
    consts[64:, 64:128] = 1.0
    consts[:, 128:256] = np.eye(128, dtype=np.float32)
    qi = np.arange(128)[:, None]
    ki = np.arange(128)[None, :]
    consts[:, 256:384] = np.where(ki <= qi, 0.0, -30000.0).astype(np.float32)
    pos = np.arange(T, dtype=np.float32)
    inv_freq = (1.0 / (np.float32(10000.0) ** (np.arange(0, 64, 2, dtype=np.float32) / np.float32(64)))).astype(np.float32)
    ang = (pos[None, :] * inv_freq[:, None]).astype(np.float32)
    cs, sn = np.cos(ang).astype(np.float32), np.sin(ang).astype(np.float32)
    rope = np.concatenate([np.concatenate([cs, sn], 0), np.concatenate([-sn, cs], 0)], axis=1)
    shared = {
        "pvec": pv, "consts": consts, "rope": np.ascontiguousarray(rope),
        "wg1": f(inp["ffn1_w_gate"])[0], "wu1": f(inp["ffn1_w_up"])[0], "wd1": f(inp["ffn1_w_down"])[0],
        "win": np.ascontiguousarray(win_ext),
        "wup": f(inp["w_up"])[0], "aup": f(inp["a_up"])[0], "gup": f(inp["g_up"])[0],
        "wuq": np.ascontiguousarray(wuq_ext), "wk": wk, "wv": wv, "wout": f(inp["w_out"])[0],
        "wg2": f(inp["ffn2_w_gate"])[0], "wu2": f(inp["ffn2_w_up"])[0], "wd2": f(inp["ffn2_w_down"])[0],
    }
    in_maps = []
    for c in range(NCORES):
        hT = np.empty((NB, D, T), np.float32)
        for j in range(NB):
            b = c * NB + j
            hT[j, :, :16] = meta.T
            hT[j, :, 16:] = x[b].T
        m = dict(shared)
        m["hT"] = hT
        in_maps.append(m)
    return in_maps


def kernel(**inputs):
    in_maps = host_prep(inputs)
    nc = build_program()
    res = run_bass_kernel_spmd(nc, in_maps, core_ids=list(range(NCORES)))
    out = np.empty((NCORES * NB, T - 16, D), np.float32)
    for c in range(NCORES):
        oT = np.asarray(res.results[c]["outT"])
        for j in range(NB):
            out[c * NB + j] = oT[j].T
    return out
```

```python
import math
from contextlib import ExitStack
import numpy as np
import concourse.bass as bass
import concourse.mybir as mybir
from concourse.bass_utils import run_bass_kernel_spmd

F32 = mybir.dt.float32
BF16 = mybir.dt.bfloat16
U8 = mybir.dt.uint8
AF = mybir.ActivationFunctionType
ALU = mybir.AluOpType
AX = mybir.AxisListType

NCORES = 8
D = 2048
T = 2064
NB = 2
DFF = 5632
KT = D // 128
FT = DFF // 128
NT = 688
NTI = T // NT
NSUB = [(0, 344), (344, 344)]
NH = 32
MH = 16
EPS = 1e-6
GN_EPS = 64 * 1e-5
C_R, C_K, C_V = 0, 2048, 4096
C_XW, C_XA, C_XG = 6144, 6240, 6336
C_CQ, C_CKV, C_KPA, C_KPB, C_GA, C_GB = 6592, 7104, 7616, 7680, 7744, 9792
NIN = 11840
PV = {}
_o = 0
for _n, _w in [("ffn1_norm", 16), ("mix_norm", 16), ("mu_r", 16), ("mu_k", 16), ("mu_v", 16), ("w0", 16),
               ("a0", 16), ("k_k", 16), ("k_a", 16), ("r_k", 16), ("gn_w", 16), ("gn_b", 16),
               ("ffn2_norm", 16), ("final_norm", 16), ("q_norm", 4), ("kv_norm", 4),
               ("mu_xw", 1), ("mu_xa", 1), ("mu_xg", 2)]:
    PV[_n] = _o
    _o += _w
NPV = _o


class DSem:
    def __init__(self, name):
        self.name = name
        self.count = 0


class Sched:
    ENG = ("pe", "act", "dve", "pool", "sp")

    def __init__(self):
        self.q = {e: [] for e in self.ENG}
        self.cnt = {e: 0 for e in ("pe", "act", "dve", "pool")}
        self.bufs = {}
        self.waited = {e: {} for e in self.ENG}
        self.dsems = []
        self.barrier_tokens = []

    def dsem(self, name):
        d = DSem("d%d_%s" % (len(self.dsems), name))
        self.dsems.append(d)
        return d

    def barrier(self):
        toks = [(e, c) for e, c in self.cnt.items() if c > 0]
        toks += [(d.name, d.count) for d in self.dsems if d.count > 0]
        self.barrier_tokens = toks

    def op(self, eng, fn, reads=(), writes=(), dsem=None):
        deps = {}

        def add(tok):
            if tok is None:
                return
            s, v = tok
            if deps.get(s, 0) < v:
                deps[s] = v
        for k in reads:
            b = self.bufs.get(k)
            if b:
                add(b[0])
        for k in writes:
            b = self.bufs.get(k)
            if b:
                add(b[0])
                for s, v in b[1].items():
                    add((s, v))
        for tok in self.barrier_tokens:
            add(tok)
        if dsem is not None:
            dsem.count += 16
            token = (dsem.name, dsem.count)
            signal = (dsem.name, 16)
        else:
            self.cnt[eng] += 1
            token = (eng, self.cnt[eng])
            signal = (eng, 1)
        waits = []
        wd = self.waited[eng]
        for s, v in deps.items():
            if wd.get(s, 0) < v:
                wd[s] = v
                waits.append((s, v))
        for k in reads:
            b = self.bufs.setdefault(k, [None, {}])
            if b[1].get(token[0], 0) < token[1]:
                b[1][token[0]] = token[1]
        for k in writes:
            self.bufs[k] = [token, {}]
        self.q[eng].append((waits, fn, signal))
        return token

    def final_wait(self, eng="sp"):
        self.barrier()
        waits = []
        for s, v in self.barrier_tokens:
            if self.waited[eng].get(s, 0) < v:
                waits.append((s, v))
        self.q[eng].append((waits, None, None))

    def emit(self, eng, e, semh):
        for waits, fn, signal in self.q[eng]:
            for s, v in waits:
                e.wait_ge(semh[s], v)
            if fn is None:
                continue
            inst = fn(e)
            inst.then_inc(semh[signal[0]], signal[1])


class Arena:
    def __init__(self, ap, nbytes):
        self.ap = ap
        self.nbytes = nbytes
        self.off = 0

    def reset(self, off=0):
        self.off = off

    def alloc(self, shape, dtype, parts=128):
        esz = 4 if dtype == F32 else 2
        n = 1
        for s in shape:
            n *= s
        nb = (n * esz + 63) // 64 * 64
        assert self.off + nb <= self.nbytes, ("arena overflow", self.off, nb, self.nbytes)
        v = self.ap[0:parts, self.off:self.off + nb]
        self.off += nb
        v = v[:, 0:n * esz].bitcast(dtype)
        if len(shape) == 2:
            v = v.rearrange("p (a b) -> p a b", a=shape[0])
        elif len(shape) == 3:
            v = v.rearrange("p (a b c) -> p a b c", a=shape[0], b=shape[1])
        return v


def build_program(debug=()):
    nc = bass.Bass("TRN2", target_bir_lowering=False)
    S = Sched()
    NB1, NTI1 = (1, 1) if "one_tile" in debug else ((1, NTI) if "one_b" in debug else (NB, NTI))

    def din(name, shape, dt=F32):
        return nc.dram_tensor(name, list(shape), dt, kind="ExternalInput").ap()

    def dscr(name, shape, dt=F32):
        kind = "ExternalOutput" if name in debug else "Internal"
        return nc.dram_tensor(name, list(shape), dt, kind=kind).ap()

    hT_in = din("hT", [NB, D, T])
    pvec_in = din("pvec", [128, NPV])
    consts_in = din("consts", [128, 3 * 128])
    rope_in = din("rope", [64, 2 * T])
    wsrc = {
        "wg1": din("wg1", [D, DFF]), "wu1": din("wu1", [D, DFF]), "wd1": din("wd1", [DFF, D]),
        "win": din("win", [D, NIN]),
        "wup": din("wup", [96, D]), "aup": din("aup", [96, D]), "gup": din("gup", [256, D]),
        "wuq": din("wuq", [512, 4096]), "wk": din("wk", [512, D]), "wv": din("wv", [512, D]),
        "wout": din("wout", [D, D]),
        "wg2": din("wg2", [D, DFF]), "wu2": din("wu2", [D, DFF]), "wd2": din("wd2", [DFF, D]),
    }
    out_T = nc.dram_tensor("outT", [NB, D, T - 16], F32, kind="ExternalOutput").ap()
    wbf = {k: dscr("b_" + k, v.shape, BF16) for k, v in wsrc.items()}
    H1 = dscr("H1", [NB, D, T])
    PROJ = dscr("PROJ", [NB, NIN, T])
    SC = {k: dscr("SC_" + k, [NB, D, T]) for k in ("r", "w", "k", "v", "kn", "b")}
    GOUT = dscr("GOUT", [NB, D, T])
    BONUS = dscr("BONUS", [NB, D, T])
    YA = dscr("YA", [NB, D, T])
    YB = dscr("YB", [NB, D, T])

    ARENA_BYTES = 190 * 1024
    with ExitStack() as es:
        arena_t = es.enter_context(nc.sbuf_tensor("arena", [128, ARENA_BYTES], U8))
        psum_t = es.enter_context(nc.psum_tensor("psum", [128, 8, 512], F32))
        A = Arena(arena_t, ARENA_BYTES)

        def bank(i, n=512, parts=128):
            return psum_t[0:parts, i, 0:n]

        pvec = A.alloc([NPV], F32)
        consts = A.alloc([3 * 128], F32)
        cbf = A.alloc([2 * 128], BF16)
        base_off = A.off
        blk1 = consts[:, 0:128]
        ident_f = consts[:, 128:256]
        ident_b = cbf[:, 0:128]
        mask_b = cbf[:, 128:256]

        def pcol(name, c=0, parts=128):
            return pvec[0:parts, PV[name] + c:PV[name] + c + 1]

        d_pv = S.dsem("pvec")
        S.op("sp", lambda e: e.dma_start(out=pvec, in_=pvec_in), writes=["pvec"], dsem=d_pv)
        d_cs = S.dsem("consts")
        S.op("sp", lambda e: e.dma_start(out=consts, in_=consts_in), writes=["consts"], dsem=d_cs)
        S.op("dve", lambda e: e.tensor_copy(out=cbf, in_=consts[:, 128:384]), reads=["consts"], writes=["cbf"])

        def phase0():
            A.reset(base_off)
            CH = 4096
            NS = 3
            st_f = [A.alloc([CH], F32) for _ in range(NS)]
            st_b = [A.alloc([CH], BF16) for _ in range(NS)]
            ds = [S.dsem("cv%d" % i) for i in range(NS)]
            ds2 = [S.dsem("cvb%d" % i) for i in range(NS)]
            engs = ["act", "dve", "pool"]
            it = 0
            for name, src in wsrc.items():
                R, C = src.shape
                dst = wbf[name]
                nchunk = (C + CH - 1) // CH
                cw = (C + nchunk - 1) // nchunk
                for r0 in range(0, R, 128):
                    rp = min(128, R - r0)
                    for c0 in range(0, C, cw):
                        w = min(cw, C - c0)
                        s = it % NS
                        eng = engs[it % 3]
                        it += 1
                        f_ap = st_f[s][0:rp, 0:w]
                        b_ap = st_b[s][0:rp, 0:w]
                        S.op("sp", lambda e, f_ap=f_ap, src=src, r0=r0, rp=rp, c0=c0, w=w:
                             e.dma_start(out=f_ap, in_=src[r0:r0 + rp, c0:c0 + w]),
                             writes=["cvf%d" % s], dsem=ds[s])
                        if eng == "act":
                            fn = lambda e, f_ap=f_ap, b_ap=b_ap: e.activation(out=b_ap, in_=f_ap, func=AF.Copy)
                        else:
                            fn = lambda e, f_ap=f_ap, b_ap=b_ap: e.tensor_copy(out=b_ap, in_=f_ap)
                        S.op(eng, fn, reads=["cvf%d" % s], writes=["cvb%d" % s])
                        S.op("sp", lambda e, b_ap=b_ap, dst=dst, r0=r0, rp=rp, c0=c0, w=w:
                             e.dma_start(out=dst[r0:r0 + rp, c0:c0 + w], in_=b_ap),
                             reads=["cvb%d" % s], writes=["W_" + name], dsem=ds2[s])

        class Ctx:
            pass

        def rmsnorm(hT, uT, gname, hkey, ukey, sqs, rstd, pb0, ones_ap, nchunk=KT, dim=D, nt=NT, nsub=NSUB):
            for c in range(nchunk):
                sq = sqs[c % 2]
                sk = "sq%d" % (c % 2)
                S.op("act", lambda e, c=c, sq=sq: e.activation(out=sq[:, 0:nt], in_=hT[:, c, 0:nt], func=AF.Square),
                     reads=[hkey], writes=[sk])

                def mm(e, c=c, sq=sq):
                    i = None
                    for si, (n0, nw) in enumerate(nsub):
                        i = e.matmul(bank(pb0 + si, nw), ones_ap, sq[:, n0:n0 + nw],
                                     start=(c == 0), stop=(c == nchunk - 1))
                    return i
                S.op("pe", mm, reads=[sk, "consts", "ones"], writes=["ps%d" % (pb0 + si) for si in range(len(nsub))])
            for si, (n0, nw) in enumerate(nsub):
                S.op("dve", lambda e, si=si, n0=n0, nw=nw: e.tensor_scalar(
                    out=rstd[:, n0:n0 + nw], in0=bank(pb0 + si, nw), scalar1=1.0 / dim, scalar2=EPS,
                    op0=ALU.mult, op1=ALU.add), reads=["ps%d" % (pb0 + si)], writes=["rstd"])
            S.op("act", lambda e: e.activation(out=rstd[:, 0:nt], in_=rstd[:, 0:nt], func=AF.Sqrt),
                 reads=["rstd"], writes=["rstd"])
            S.op("dve", lambda e: e.reciprocal(out=rstd[:, 0:nt], in_=rstd[:, 0:nt]), reads=["rstd"], writes=["rstd"])
            for c in range(nchunk):
                S.op("dve", lambda e, c=c: e.scalar_tensor_tensor(
                    out=uT[:, c, 0:nt], in0=hT[:, c, 0:nt], scalar=pcol(gname, c), in1=rstd[:, 0:nt],
                    op0=ALU.mult, op1=ALU.mult), reads=[hkey, "rstd", "pvec"], writes=[ukey])

        psrr = [0]

        def gemm(wlist, kt, kp, panels, act, actkey, slots, evac, nsub, banks, tag):
            nw_ = len(wlist)
            npan = len(panels)

            def load(pi):
                c0, pw, _ = panels[pi]
                sl_ap, sl_key, sl_ds = slots[pi % len(slots)]
                for wi, (wname, wap) in enumerate(wlist):
                    if kt > 1:
                        src = wap.rearrange("(c p) m -> p c m", p=kp)[:, :, c0:c0 + pw]
                    else:
                        src = wap[:, c0:c0 + pw].unsqueeze(1)
                    S.op("sp", lambda e, sl_ap=sl_ap, wi=wi, src=src, pw=pw: e.dma_start(
                        out=sl_ap[0:kp, wi, 0:kt, 0:pw], in_=src),
                        reads=["W_" + wname], writes=[sl_key], dsem=sl_ds)
            load(0)
            for pi in range(npan):
                if pi + 1 < npan:
                    load(pi + 1)
                c0, pw, chunks = panels[pi]
                sl_ap, sl_key, sl_ds = slots[pi % len(slots)]
                for (cc0, cw) in chunks:
                    for si, (n0, nw) in enumerate(nsub):
                        bks = []
                        for wi in range(nw_):
                            bk = banks[psrr[0] % len(banks)]
                            psrr[0] += 1
                            bks.append(bk)

                            def mm(e, wi=wi, bk=bk, cc0=cc0, cw=cw, n0=n0, nw=nw, sl_ap=sl_ap, c0=c0):
                                i = None
                                for k in range(kt):
                                    i = e.matmul(bank(bk, nw, cw), sl_ap[0:kp, wi, k, cc0 - c0:cc0 - c0 + cw],
                                                 act[0:kp, k, n0:n0 + nw], start=(k == 0), stop=(k == kt - 1))
                                return i
                            S.op("pe", mm, reads=[sl_key, actkey], writes=["ps%d" % bk])
                        evac(cc0, cw, si, n0, nw, bks)

        def mk_panels(chunks, pw=256):
            panels = []
            cur = []
            for (c0, w) in chunks:
                if cur and (c0 + w - cur[0][0] > pw or cur[-1][0] + cur[-1][1] != c0):
                    panels.append((cur[0][0], cur[-1][0] + cur[-1][1] - cur[0][0], cur))
                    cur = []
                cur.append((c0, w))
            if cur:
                panels.append((cur[0][0], cur[-1][0] + cur[-1][1] - cur[0][0], cur))
            return panels

        def ffn(X, wg, wu, wd):
            pan = mk_panels([(c, 128) for c in range(0, DFF, 128)])

            def ev_gu(cc0, cw, si, n0, nw, bks):
                f = cc0 // 128
                tm = X.tmp[si % 2]
                tk = "tmp%d" % (si % 2)
                S.op("act", lambda e: e.activation(out=tm[:, 0:nw], in_=bank(bks[0], nw), func=AF.Silu),
                     reads=["ps%d" % bks[0]], writes=[tk])
                S.op("dve", lambda e: e.tensor_tensor(out=X.aT[:, f, n0:n0 + nw], in0=bank(bks[1], nw),
                                                      in1=tm[:, 0:nw], op=ALU.mult),
                     reads=["ps%d" % bks[1], tk], writes=["aT"])
            gemm([(wg, wbf[wg]), (wu, wbf[wu])], KT, 128, pan, X.uT, "uT", X.slots_gu, ev_gu, NSUB,
                 [0, 1, 2, 3, 4, 5], "gu")
            pan_d = mk_panels([(c, 128) for c in range(0, D, 128)], pw=128)

            def ev_d(cc0, cw, si, n0, nw, bks):
                m = cc0 // 128
                S.op("dve", lambda e: e.scalar_tensor_tensor(
                    out=X.hT[:, m, n0:n0 + nw], in0=bank(bks[0], nw), scalar=0.5, in1=X.hT[:, m, n0:n0 + nw],
                    op0=ALU.mult, op1=ALU.add), reads=["ps%d" % bks[0], "hT"], writes=["hT"])
            gemm([(wd, wbf[wd])], FT, 128, pan_d, X.aT, "aT", X.slots_d, ev_d, NSUB, [0, 1, 2, 3, 4, 5], "dn")

        def ffn_arena():
            A.reset(base_off)
            X = Ctx()
            X.hT = A.alloc([KT, NT], F32)
            X.uT = A.alloc([KT, NT], BF16)
            X.aT = A.alloc([FT, NT], BF16)
            X.sqs = [A.alloc([NT], F32) for _ in range(2)]
            X.rstd = A.alloc([NT], F32)
            X.tmp = [A.alloc([344], F32) for _ in range(2)]
            X.d_h = S.dsem("hT")
            slot_bytes = 2 * KT * 256 * 2
            X.slots_gu = []
            X.slots_d = []
            X.slots_in = []
            for i in range(2):
                off = A.off
                raw = A.alloc([slot_bytes // 2], BF16)
                ds = S.dsem("wslot%d_%d" % (i, len(S.dsems)))
                key = "wslot%d" % i
                X.slots_gu.append((raw[:, 0:2 * KT * 256].rearrange("p (w k m) -> p w k m", w=2, k=KT), key, ds))
                X.slots_d.append((raw[:, 0:FT * 128].rearrange("p (w k m) -> p w k m", w=1, k=FT), key, ds))
                X.slots_in.append((raw[:, 0:KT * 256].rearrange("p (w k m) -> p w k m", w=1, k=KT), key, ds))
            return X

        def phase1():
            X = ffn_arena()
            NST = 4
            stg = [A.alloc([NT], F32) for _ in range(NST)]
            dst = [S.dsem("stg%d_%d" % (i, len(S.dsems))) for i in range(NST)]
            chunks = [(c, 128) for c in range(0, C_XW, 128)]
            chunks += [(C_XW, 96), (C_XA, 96), (C_XG, 128), (C_XG + 128, 128)]
            chunks += [(c, 128) for c in range(C_CQ, C_KPA, 128)]
            chunks += [(C_KPA, 64), (C_KPB, 64)]
            chunks += [(c, 128) for c in range(C_GA, NIN, 128)]
            pan_in = mk_panels(chunks)
            ones_f = None
            for b in range(NB1):
                for ti in range(NTI1):
                    t0 = ti * NT
                    S.op("sp", lambda e, b=b, t0=t0: e.dma_start(
                        out=X.hT, in_=hT_in[b].rearrange("(c p) t -> p c t", p=128)[:, :, t0:t0 + NT]),
                        writes=["hT"], dsem=X.d_h)
                    rmsnorm(X.hT, X.uT, "ffn1_norm", "hT", "uT", X.sqs, X.rstd, 6, ones_all)
                    ffn(X, "wg1", "wu1", "wd1")
                    S.op("sp", lambda e, b=b, t0=t0: e.dma_start(
                        out=H1[b].rearrange("(c p) t -> p c t", p=128)[:, :, t0:t0 + NT], in_=X.hT),
                        reads=["hT"], writes=["H1"], dsem=X.d_h)
                    rmsnorm(X.hT, X.uT, "mix_norm", "hT", "uT", X.sqs, X.rstd, 6, ones_all)
                    cnt = [0]

                    def ev_in(cc0, cw, si, n0, nw, bks, b=b, t0=t0):
                        s = cnt[0] % NST
                        sk = "stg%d" % s
                        gate = cc0 >= C_GA
                        if gate:
                            S.op("act", lambda e: e.activation(out=stg[s][0:cw, n0:n0 + nw], in_=bank(bks[0], nw, cw),
                                                               func=AF.Sigmoid),
                                 reads=["ps%d" % bks[0]], writes=[sk])
                        else:
                            eng = "act" if (cnt[0] % 2 == 0) else "dve"
                            if eng == "act":
                                fn = lambda e: e.activation(out=stg[s][0:cw, n0:n0 + nw], in_=bank(bks[0], nw, cw),
                                                            func=AF.Copy)
                            else:
                                fn = lambda e: e.tensor_copy(out=stg[s][0:cw, n0:n0 + nw], in_=bank(bks[0], nw, cw))
                            S.op(eng, fn, reads=["ps%d" % bks[0]], writes=[sk])
                        if si == len(NSUB) - 1:
                            S.op("sp", lambda e: e.dma_start(out=PROJ[b, cc0:cc0 + cw, t0:t0 + NT],
                                                             in_=stg[s][0:cw, 0:NT]),
                                 reads=[sk], writes=["PROJ"], dsem=dst[s])
                            cnt[0] += 1
                    gemm([("win", wbf["win"])], KT, 128, pan_in, X.uT, "uT", X.slots_in, ev_in, NSUB,
                         [0, 1, 2, 3, 4, 5], "in")


        def phase2():
            A.reset(base_off)
            NP1 = NT + 1
            wup_s = A.alloc([D], BF16)
            aup_s = A.alloc([D], BF16)
            gup_s = A.alloc([2, D], BF16)
            d_l = S.dsem("lora")
            S.op("sp", lambda e: e.dma_start(out=wup_s[0:96, :], in_=wbf["wup"]), reads=["W_wup"], writes=["lw"], dsem=d_l)
            S.op("sp", lambda e: e.dma_start(out=aup_s[0:96, :], in_=wbf["aup"]), reads=["W_aup"], writes=["lw"], dsem=d_l)
            S.op("sp", lambda e: e.dma_start(out=gup_s, in_=wbf["gup"].rearrange("(c p) m -> p c m", p=128)),
                 reads=["W_gup"], writes=["lw"], dsem=d_l)
            raw = {n: [(A.alloc([NP1], F32), "raw_%s%d" % (n, i), S.dsem("raw_%s%d" % (n, i))) for i in range(2)]
                   for n in ("r", "k", "v")}
            xs = A.alloc([4, NP1], F32)
            d_xs = S.dsem("xs")
            txw = A.alloc([NT], BF16)
            txa = A.alloc([NT], BF16)
            tsg = A.alloc([2, NT], BF16)
            names = ["dtmp", "tmpf", "sh_r", "sh_k", "sh_v", "kk", "a_t", "wdec", "tmpA", "tmpB", "kn", "bt", "tq", "k2", "rk", "bon", "gsb"]
            W_ = {n: A.alloc([NT], F32) for n in names}
            DS = {n: S.dsem("o_" + n) for n in ("sh_r", "wdec", "k2", "sh_v", "kn", "bt", "bon", "gsb")}
            rr = [0]

            def nb_():
                b_ = [0, 1, 2, 3, 4, 5][rr[0] % 6]
                rr[0] += 1
                return b_

            def shift(dst, dkey, src, skey, mucol, parts=128):
                S.op("dve", lambda e: e.tensor_tensor(out=W_["dtmp"][0:parts, :], in0=src[0:parts, 0:NT], in1=src[0:parts, 1:NP1],
                                                      op=ALU.subtract), reads=[skey], writes=["dtmp"])
                S.op("dve", lambda e: e.scalar_tensor_tensor(out=dst, in0=W_["dtmp"][0:parts, :], scalar=mucol,
                                                             in1=src[0:parts, 1:NP1], op0=ALU.mult, op1=ALU.add),
                     reads=["dtmp", skey, "pvec"], writes=[dkey])

            def load_halo(dst, key, ds, b, row0, nrows, t0):
                if t0 == 0:
                    S.op("pool", lambda e: e.memset(dst[0:nrows, 0:1], 0.0), writes=[key])
                    S.op("sp", lambda e: e.dma_start(out=dst[0:nrows, 1:NP1], in_=PROJ[b, row0:row0 + nrows, 0:NT]),
                         reads=["PROJ"], writes=[key], dsem=ds)
                else:
                    S.op("sp", lambda e: e.dma_start(out=dst[0:nrows, 0:NP1], in_=PROJ[b, row0:row0 + nrows, t0 - 1:t0 + NT]),
                         reads=["PROJ"], writes=[key], dsem=ds)

            def mm_ev(mms, parts, evac):
                for (n0, nw) in NSUB:
                    bk = nb_()

                    def f(e, bk=bk, n0=n0, nw=nw):
                        i = None
                        for j, (l, rf) in enumerate(mms):
                            i = e.matmul(bank(bk, nw, parts), l, rf(n0, nw), start=(j == 0), stop=(j == len(mms) - 1))
                        return i
                    S.op("pe", f, reads=["lw", "txw", "txa", "tsg", "consts", "tmpB", "rk"], writes=["ps%d" % bk])
                    evac(bank(bk, nw, parts), n0, nw, "ps%d" % bk)

            def store(name, dst_ap):
                S.op("sp", lambda e: e.dma_start(out=dst_ap, in_=W_[name]), reads=[name], writes=["SCR"], dsem=DS[name])

            for b in range(NB1):
                for ti in range(NTI1):
                    t0 = ti * NT
                    for i, (r0, nr) in enumerate([(C_XW, 96), (C_XA, 96), (C_XG, 128), (C_XG + 128, 128)]):
                        load_halo(xs[:, i, :], "xs", d_xs, b, r0, nr, t0)
                    shift(W_["tmpf"][0:96, :], "tmpf", xs[:, 0, :], "xs", pcol("mu_xw", 0, 96), 96)
                    S.op("act", lambda e: e.activation(out=txw[0:96, :], in_=W_["tmpf"][0:96, :], func=AF.Tanh),
                         reads=["tmpf"], writes=["txw"])
                    shift(txa[0:96, :], "txa", xs[:, 1, :], "xs", pcol("mu_xa", 0, 96), 96)
                    for c in range(2):
                        shift(W_["tmpf"], "tmpf", xs[:, 2 + c, :], "xs", pcol("mu_xg", c))
                        S.op("act", lambda e, c=c: e.activation(out=tsg[:, c, :], in_=W_["tmpf"], func=AF.Sigmoid),
                             reads=["tmpf"], writes=["tsg"])
                    for m in range(KT):
                        ms = slice(m * 128, (m + 1) * 128)
                        for n, c0 in (("r", C_R), ("k", C_K), ("v", C_V)):
                            ap_, key, ds = raw[n][m % 2]
                            load_halo(ap_, key, ds, b, c0 + m * 128, 128, t0)
                            shift(W_["sh_" + n], "sh_" + n, ap_, key, pcol("mu_" + n, m))
                        mm_ev([(wup_s[0:96, ms], lambda n0, nw: txw[0:96, n0:n0 + nw])], 128,
                              lambda bp, n0, nw, bk, m=m: S.op("act", lambda e: e.activation(
                                  out=W_["tmpA"][:, n0:n0 + nw], in_=bp, func=AF.Sigmoid, bias=pcol("w0", m)),
                                  reads=[bk, "pvec"], writes=["tmpA"]))
                        S.op("act", lambda e: e.activation(out=W_["wdec"], in_=W_["tmpA"], func=AF.Exp, scale=-math.exp(-0.5)),
                             reads=["tmpA"], writes=["wdec"])
                        mm_ev([(aup_s[0:96, ms], lambda n0, nw: txa[0:96, n0:n0 + nw])], 128,
                              lambda bp, n0, nw, bk, m=m: S.op("act", lambda e: e.activation(
                                  out=W_["a_t"][:, n0:n0 + nw], in_=bp, func=AF.Sigmoid, bias=pcol("a0", m)),
                                  reads=[bk, "pvec"], writes=["a_t"]))
                        mm_ev([(gup_s[:, 0, ms], lambda n0, nw: tsg[:, 0, n0:n0 + nw]),
                               (gup_s[:, 1, ms], lambda n0, nw: tsg[:, 1, n0:n0 + nw])], 128,
                              lambda bp, n0, nw, bk: S.op("act", lambda e: e.activation(
                                  out=W_["gsb"][:, n0:n0 + nw], in_=bp, func=AF.Copy), reads=[bk], writes=["gsb"]))
                        S.op("dve", lambda e, m=m: e.tensor_scalar(out=W_["kk"], in0=W_["sh_k"], scalar1=pcol("k_k", m), scalar2=None,
                                                                   op0=ALU.mult), reads=["sh_k", "pvec"], writes=["kk"])
                        S.op("act", lambda e: e.activation(out=W_["tmpB"], in_=W_["kk"], func=AF.Square), reads=["kk"], writes=["tmpB"])
                        mm_ev([(blk1, lambda n0, nw: W_["tmpB"][:, n0:n0 + nw])], 128,
                              lambda bp, n0, nw, bk: S.op("dve", lambda e: e.tensor_scalar(
                                  out=W_["tq"][:, n0:n0 + nw], in0=bp, scalar1=1e-24, scalar2=None, op0=ALU.max),
                                  reads=[bk], writes=["tq"]))
                        S.op("act", lambda e: e.activation(out=W_["tq"], in_=W_["tq"], func=AF.Sqrt), reads=["tq"], writes=["tq"])
                        S.op("dve", lambda e: e.reciprocal(out=W_["tq"], in_=W_["tq"]), reads=["tq"], writes=["tq"])
                        S.op("dve", lambda e: e.scalar_tensor_tensor(out=W_["kn"], in0=W_["kk"], scalar=-1.0, in1=W_["tq"],
                                                                     op0=ALU.mult, op1=ALU.mult), reads=["kk", "tq"], writes=["kn"])
                        S.op("dve", lambda e: e.scalar_tensor_tensor(out=W_["bt"], in0=W_["kn"], scalar=-1.0, in1=W_["a_t"],
                                                                     op0=ALU.mult, op1=ALU.mult), reads=["kn", "a_t"], writes=["bt"])
                        S.op("dve", lambda e, m=m: e.tensor_scalar(out=W_["tq"], in0=W_["a_t"], scalar1=-1.0, scalar2=pcol("k_a", m),
                                                                   op0=ALU.add, op1=ALU.mult), reads=["a_t", "pvec"], writes=["tq"])
                        S.op("dve", lambda e: e.scalar_tensor_tensor(out=W_["k2"], in0=W_["tq"], scalar=1.0, in1=W_["sh_k"],
                                                                     op0=ALU.add, op1=ALU.mult), reads=["tq", "sh_k"], writes=["k2"])
                        S.op("dve", lambda e, m=m: e.scalar_tensor_tensor(out=W_["rk"], in0=W_["sh_r"], scalar=pcol("r_k", m), in1=W_["k2"],
                                                                          op0=ALU.mult, op1=ALU.mult), reads=["sh_r", "k2", "pvec"], writes=["rk"])
                        mm_ev([(blk1, lambda n0, nw: W_["rk"][:, n0:n0 + nw])], 128,
                              lambda bp, n0, nw, bk: S.op("dve", lambda e: e.tensor_tensor(
                                  out=W_["bon"][:, n0:n0 + nw], in0=bp, in1=W_["sh_v"][:, n0:n0 + nw], op=ALU.mult),
                                  reads=[bk, "sh_v"], writes=["bon"]))
                        tsl = slice(t0, t0 + NT)
                        store("sh_r", SC["r"][b, ms, tsl]); store("wdec", SC["w"][b, ms, tsl]); store("k2", SC["k"][b, ms, tsl])
                        store("sh_v", SC["v"][b, ms, tsl]); store("kn", SC["kn"][b, ms, tsl]); store("bt", SC["b"][b, ms, tsl])
                        store("bon", BONUS[b, ms, tsl]); store("gsb", GOUT[b, ms, tsl])

        def phase4():
            A.reset(base_off)
            TC = 32
            PH = NB1 * 32
            P2_ = 2 * PH
            sets = []
            for i in range(2):
                tl = {n: A.alloc([64, TC], F32) for n in ("r", "w", "k", "kn", "b")}
                tl["v"] = A.alloc([32, TC], F32)
                sets.append((tl, S.dsem("scin%d" % i), "scin%d" % i))
            ysets = [(A.alloc([32, TC], F32), S.dsem("yout%d" % i), "yout%d" % i) for i in range(2)]
            St = A.alloc([32, 64], F32)
            Sw = [A.alloc([32, 64], F32) for _ in range(2)]
            T1 = A.alloc([32, 64], F32)
            T2 = A.alloc([32, 64], F32)
            T3 = A.alloc([32, 64], F32)
            sa = A.alloc([32], F32)
            S.op("dve", lambda e: e.memset(St[0:P2_], 0.0), writes=["St"])
            nch = (T + TC - 1) // TC

            def load(c):
                tl, ds, key = sets[c % 2]
                t0 = c * TC
                tc = min(TC, T - t0)
                for n in ("r", "w", "k", "kn", "b"):
                    src = SC[n][0:NB1].rearrange("b (h j) t -> (b h) j t", j=64)
                    for half in range(2):
                        for jh in range(2):
                            S.op("sp", lambda e, n=n, half=half, jh=jh, src=src, tl=tl, t0=t0, tc=tc: e.dma_start(
                                out=tl[n][half * PH:(half + 1) * PH, jh * 32:(jh + 1) * 32, 0:tc],
                                in_=src[:, jh * 32:(jh + 1) * 32, t0:t0 + tc]), reads=["SCR"], writes=[key], dsem=ds)
                srcv = SC["v"][0:NB1].rearrange("b (h x i) t -> (b h) x i t", x=2, i=32)
                for half in range(2):
                    S.op("sp", lambda e, half=half, tl=tl, t0=t0, tc=tc: e.dma_start(
                        out=tl["v"][half * PH:(half + 1) * PH, :, 0:tc], in_=srcv[:, half, :, t0:t0 + tc]),
                        reads=["SCR"], writes=[key], dsem=ds)
            load(0)
            dsty = YA[0:NB1].rearrange("b (h x i) t -> (b h) x i t", x=2, i=32)
            step = 0
            for c in range(nch):
                if c + 1 < nch:
                    load(c + 1)
                tl, ds, key = sets[c % 2]
                yt, yds, ykey = ysets[c % 2]
                t0 = c * TC
                tc = min(TC, T - t0)
                for tt in range(tc):
                    def bj(n):
                        return tl[n][0:P2_, :, tt].unsqueeze(1).broadcast_to([P2_, 32, 64])
                    vb = tl["v"][0:P2_, :, tt].unsqueeze(2).broadcast_to([P2_, 32, 64])
                    sw = Sw[step % 2]
                    swk = "Sw%d" % (step % 2)
                    S.op("pool", lambda e, vb=vb, kb=bj("k"): e.tensor_tensor(out=T3[0:P2_], in0=vb, in1=kb, op=ALU.mult),
                         reads=[key], writes=["T3"])
                    S.op("pool", lambda e, sw=sw, wb=bj("w"): e.tensor_tensor(out=sw[0:P2_], in0=St[0:P2_], in1=wb, op=ALU.mult),
                         reads=[key, "St"], writes=[swk])
                    S.op("pool", lambda e, sw=sw: e.tensor_tensor(out=sw[0:P2_], in0=sw[0:P2_], in1=T3[0:P2_], op=ALU.add),
                         reads=[swk, "T3"], writes=[swk])
                    S.op("dve", lambda e, knb=bj("kn"): e.tensor_tensor(out=T1[0:P2_], in0=St[0:P2_], in1=knb, op=ALU.mult),
                         reads=[key, "St"], writes=["T1"])
                    S.op("dve", lambda e: e.tensor_reduce(out=sa[0:P2_], in_=T1[0:P2_], axis=AX.X, op=ALU.add),
                         reads=["T1"], writes=["sa"])
                    S.op("dve", lambda e, bb=bj("b"): e.tensor_tensor(
                        out=T2[0:P2_], in0=sa[0:P2_].unsqueeze(2).broadcast_to([P2_, 32, 64]), in1=bb, op=ALU.mult),
                        reads=[key, "sa"], writes=["T2"])
                    S.op("dve", lambda e, sw=sw: e.tensor_tensor(out=St[0:P2_], in0=sw[0:P2_], in1=T2[0:P2_], op=ALU.add),
                         reads=[swk, "T2"], writes=["St"])
                    S.op("dve", lambda e, rb=bj("r"): e.tensor_tensor(out=T1[0:P2_], in0=St[0:P2_], in1=rb, op=ALU.mult),
                         reads=[key, "St"], writes=["T1"])
                    S.op("dve", lambda e, yt=yt, tt=tt: e.tensor_reduce(out=yt[0:P2_, :, tt], in_=T1[0:P2_], axis=AX.X, op=ALU.add),
                         reads=["T1"], writes=[ykey])
                    step += 1
                for half in range(2):
                    S.op("sp", lambda e, half=half, yt=yt, t0=t0, tc=tc: e.dma_start(
                        out=dsty[:, half, :, t0:t0 + tc], in_=yt[half * PH:(half + 1) * PH, :, 0:tc]),
                        reads=[ykey], writes=["YA"], dsem=yds)

        def phase3():
            A.reset(base_off)
            SUBS = [(0, 512), (512, 512), (1024, 512), (1536, 512), (2048, 16)]
            SCALE = 192.0 ** -0.5
            cqn = A.alloc([4, T], BF16)
            ckvn = A.alloc([4, T], BF16)
            kpeT = A.alloc([T], BF16)
            ropeT = A.alloc([2, T], F32)
            wuq_s = A.alloc([4, 4096], BF16)
            wk_s = A.alloc([4, D], BF16)
            wv_s = A.alloc([4, D], BF16)
            d_w = S.dsem("mlaw")
            S.op("sp", lambda e: e.dma_start(out=ropeT[0:64], in_=rope_in.rearrange("p (a t) -> p a t", a=2)),
                 writes=["rope"], dsem=S.dsem("rope"))
            S.op("sp", lambda e: e.dma_start(out=wuq_s, in_=wbf["wuq"].rearrange("(c p) m -> p c m", p=128)),
                 reads=["W_wuq"], writes=["mlaw"], dsem=d_w)
            S.op("sp", lambda e: e.dma_start(out=wk_s, in_=wbf["wk"].rearrange("(c p) m -> p c m", p=128)),
                 reads=["W_wk"], writes=["mlaw"], dsem=d_w)
            S.op("sp", lambda e: e.dma_start(out=wv_s, in_=wbf["wv"].rearrange("(c p) m -> p c m", p=128)),
                 reads=["W_wv"], writes=["mlaw"], dsem=d_w)
            rawc = A.alloc([4, 512], F32)
            d_rc = S.dsem("rawc")
            rawk = A.alloc([2, 512], F32)
            d_rk = S.dsem("rawk")
            sq2 = [A.alloc([512], F32) for _ in range(2)]
            rs = A.alloc([512], F32)
            t1 = A.alloc([512], F32)
            t2 = A.alloc([512], F32)
            qnT = A.alloc([T], BF16)
            qpeT = A.alloc([T], BF16)
            knT = A.alloc([T], BF16)
            Vh = A.alloc([17, 128], BF16)
            Pts = [A.alloc([16 + 2048], BF16) for _ in range(2)]
            pti = [0]
            PTs = [A.alloc([128], BF16) for _ in range(4)]
            ybh = A.alloc([T], F32)
            d_yb = S.dsem("ybh")
            st_ = A.alloc([16], F32)
            pt_bank = [psum_t[:, 5, 0:256].bitcast(BF16), psum_t[:, 7, 0:256].bitcast(BF16)]

            def norm_lat(b, row0, dstT, gname, dkey):
                for (n0, nw) in SUBS:
                    S.op("sp", lambda e, n0=n0, nw=nw: e.dma_start(
                        out=rawc[:, :, 0:nw], in_=PROJ[b, row0:row0 + 512, n0:n0 + nw].rearrange("(c p) t -> p c t", p=128)),
                        reads=["PROJ"], writes=["rawc"], dsem=d_rc)
                    for c in range(4):
                        S.op("act", lambda e, c=c, nw=nw: e.activation(out=sq2[c % 2][:, 0:nw], in_=rawc[:, c, 0:nw], func=AF.Square),
                             reads=["rawc"], writes=["sqm%d" % (c % 2)])
                        S.op("pe", lambda e, c=c, nw=nw: e.matmul(bank(6, nw), ones_all, sq2[c % 2][:, 0:nw], start=(c == 0), stop=(c == 3)),
                             reads=["sqm%d" % (c % 2), "ones"], writes=["ps6"])
                    S.op("dve", lambda e, nw=nw: e.tensor_scalar(out=rs[:, 0:nw], in0=bank(6, nw), scalar1=1.0 / 512, scalar2=EPS,
                                                                 op0=ALU.mult, op1=ALU.add), reads=["ps6"], writes=["rs"])
                    S.op("act", lambda e, nw=nw: e.activation(out=rs[:, 0:nw], in_=rs[:, 0:nw], func=AF.Sqrt), reads=["rs"], writes=["rs"])
                    S.op("dve", lambda e, nw=nw: e.reciprocal(out=rs[:, 0:nw], in_=rs[:, 0:nw]), reads=["rs"], writes=["rs"])
                    for c in range(4):
                        S.op("dve", lambda e, c=c, n0=n0, nw=nw: e.scalar_tensor_tensor(
                            out=dstT[:, c, n0:n0 + nw], in0=rawc[:, c, 0:nw], scalar=pcol(gname, c), in1=rs[:, 0:nw],
                            op0=ALU.mult, op1=ALU.mult), reads=["rawc", "rs", "pvec"], writes=[dkey])

            def rope_comb(dst, dkey, a_ap, b_ap, n0, nw, rkeys):
                S.op("dve", lambda e: e.tensor_tensor(out=t1[0:64, 0:nw], in0=a_ap, in1=ropeT[0:64, 0, n0:n0 + nw], op=ALU.mult),
                     reads=rkeys + ["rope"], writes=["t1"])
                S.op("dve", lambda e: e.tensor_tensor(out=t2[0:64, 0:nw], in0=b_ap, in1=ropeT[0:64, 1, n0:n0 + nw], op=ALU.mult),
                     reads=rkeys + ["rope"], writes=["t2"])
                S.op("dve", lambda e: e.tensor_tensor(out=dst[0:64, n0:n0 + nw], in0=t1[0:64, 0:nw], in1=t2[0:64, 0:nw], op=ALU.add),
                     reads=["t1", "t2"], writes=[dkey])

            brr = [0]

            def proj(lhs_fn, M, src, skey, evac):
                for (n0, nw) in SUBS:
                    bk = [0, 1, 2, 3][brr[0] % 4]
                    brr[0] += 1

                    def f(e, bk=bk, n0=n0, nw=nw):
                        i = None
                        for k in range(4):
                            i = e.matmul(bank(bk, nw, M), lhs_fn(k), src[:, k, n0:n0 + nw], start=(k == 0), stop=(k == 3))
                        return i
                    S.op("pe", f, reads=["mlaw", skey], writes=["ps%d" % bk])
                    evac(bank(bk, nw, M), "ps%d" % bk, n0, nw)

            for b in range(NB1):
                norm_lat(b, C_CQ, cqn, "q_norm", "cqn")
                norm_lat(b, C_CKV, ckvn, "kv_norm", "ckvn")
                for (n0, nw) in SUBS:
                    S.op("sp", lambda e, n0=n0, nw=nw, b=b: e.dma_start(out=rawk[0:64, 0, 0:nw], in_=PROJ[b, C_KPA:C_KPA + 64, n0:n0 + nw]),
                         reads=["PROJ"], writes=["rawk"], dsem=d_rk)
                    S.op("sp", lambda e, n0=n0, nw=nw, b=b: e.dma_start(out=rawk[0:64, 1, 0:nw], in_=PROJ[b, C_KPB:C_KPB + 64, n0:n0 + nw]),
                         reads=["PROJ"], writes=["rawk"], dsem=d_rk)
                    rope_comb(kpeT, "kpeT", rawk[0:64, 0, 0:nw], rawk[0:64, 1, 0:nw], n0, nw, ["rawk"])
                for h in range(MH):
                    proj(lambda k, h=h: wuq_s[:, k, h * 256:h * 256 + 128], 128, cqn, "cqn",
                         lambda bp, bk, n0, nw: S.op("act", lambda e: e.activation(out=qnT[:, n0:n0 + nw], in_=bp, func=AF.Copy),
                                                     reads=[bk], writes=["qnT"]))
                    for (n0, nw) in SUBS:
                        bka, bkb = 0, 1

                        def fa(e, n0=n0, nw=nw, h=h):
                            i = None
                            for k in range(4):
                                i = e.matmul(bank(0, nw, 64), wuq_s[:, k, h * 256 + 128:h * 256 + 192], cqn[:, k, n0:n0 + nw],
                                             start=(k == 0), stop=(k == 3))
                            for k in range(4):
                                i = e.matmul(bank(1, nw, 64), wuq_s[:, k, h * 256 + 192:h * 256 + 256], cqn[:, k, n0:n0 + nw],
                                             start=(k == 0), stop=(k == 3))
                            return i
                        S.op("pe", fa, reads=["mlaw", "cqn"], writes=["ps0", "ps1"])
                        rope_comb(qpeT, "qpeT", bank(0, nw, 64), bank(1, nw, 64), n0, nw, ["ps0", "ps1"])
                    proj(lambda k, h=h: wk_s[:, k, h * 128:(h + 1) * 128], 128, ckvn, "ckvn",
                         lambda bp, bk, n0, nw: S.op("act", lambda e: e.activation(out=knT[:, n0:n0 + nw], in_=bp, func=AF.Copy),
                                                     reads=[bk], writes=["knT"]))
                    for kb in range(17):
                        tk0, tw = (0, 16) if kb == 0 else (16 + 128 * (kb - 1), 128)
                        bk = [2, 3][kb % 2]

                        def fv(e, bk=bk, tk0=tk0, tw=tw, h=h):
                            i = None
                            for k in range(4):
                                i = e.matmul(bank(bk, 128, tw), ckvn[:, k, tk0:tk0 + tw], wv_s[:, k, h * 128:(h + 1) * 128],
                                             start=(k == 0), stop=(k == 3))
                            return i
                        S.op("pe", fv, reads=["mlaw", "ckvn"], writes=["ps%d" % bk])
                        S.op("dve", lambda e, bk=bk, kb=kb, tw=tw: e.tensor_copy(out=Vh[0:tw, kb, :], in_=bank(bk, 128, tw)),
                             reads=["ps%d" % bk], writes=["Vh"])
                    def qinfo(qb):
                        q0, qw = (0, 16) if qb == 0 else (16 + 128 * (qb - 1), 128)
                        return q0, qw, 128 * qb

                    def emit_S(qb):
                        q0, qw, nreal = qinfo(qb)
                        nkc = (nreal + 511) // 512

                        def fs(e, q0=q0, qw=qw, qb=qb, nreal=nreal, nkc=nkc):
                            i = e.matmul(bank(4, 16, qw), qnT[:, q0:q0 + qw], knT[:, 0:16], start=True, stop=False)
                            i = e.matmul(bank(4, 16, qw), qpeT[0:64, q0:q0 + qw], kpeT[0:64, 0:16], start=False, stop=(qb != 0))
                            if qb == 0:
                                i = e.matmul(bank(4, 16, 16), ident_b[0:16, 0:16], mask_b[0:16, 0:16], start=False, stop=True)
                            for kc in range(nkc):
                                k0 = 16 + 512 * kc
                                kw = min(512, nreal - 512 * kc)
                                last = (kc == nkc - 1)
                                i = e.matmul(bank(kc, kw, qw), qnT[:, q0:q0 + qw], knT[:, k0:k0 + kw], start=True, stop=False)
                                i = e.matmul(bank(kc, kw, qw), qpeT[0:64, q0:q0 + qw], kpeT[0:64, k0:k0 + kw], start=False, stop=not last)
                                if last:
                                    off = kw - 128
                                    i = e.matmul(psum_t[0:qw, kc, off:off + 128], ident_b, mask_b, start=False, stop=True)
                            return i
                        S.op("pe", fs, reads=["qnT", "qpeT", "knT", "kpeT", "cbf"], writes=["ps0", "ps1", "ps2", "ps3", "ps4"])

                    def emit_softmax(qb):
                        q0, qw, nreal = qinfo(qb)
                        par = qb % 2
                        P_ = Pts[par]
                        pk = "Pt%d" % par
                        sk_ = "st%d" % par
                        so = 8 * par
                        sc = lambda j: st_[0:qw, so + j:so + j + 1]
                        sreal = psum_t[0:qw, 0:4, :].rearrange("p a b -> p (a b)")[:, 0:max(nreal, 1)]
                        S.op("dve", lambda e: e.tensor_reduce(out=sc(1), in_=bank(4, 16, qw), axis=AX.X, op=ALU.max),
                             reads=["ps4"], writes=[sk_])
                        if qb > 0:
                            S.op("dve", lambda e: e.tensor_reduce(out=sc(0), in_=sreal, axis=AX.X, op=ALU.max),
                                 reads=["ps0", "ps1", "ps2", "ps3"], writes=[sk_])
                            S.op("dve", lambda e: e.tensor_tensor(out=sc(2), in0=sc(0), in1=sc(1), op=ALU.max),
                                 reads=[sk_], writes=[sk_])
                        else:
                            S.op("dve", lambda e: e.tensor_copy(out=sc(2), in_=sc(1)), reads=[sk_], writes=[sk_])
                        S.op("dve", lambda e: e.tensor_scalar(out=sc(3), in0=sc(2), scalar1=-SCALE, scalar2=None, op0=ALU.mult),
                             reads=[sk_], writes=[sk_])
                        S.op("act", lambda e: e.activation(out=P_[0:qw, 0:16], in_=bank(4, 16, qw), func=AF.Exp, bias=sc(3),
                                                           scale=SCALE, accum_out=sc(5)),
                             reads=["ps4", sk_], writes=[pk, sk_])
                        if qb > 0:
                            S.op("act", lambda e: e.activation(out=P_[0:qw, 16:16 + nreal], in_=sreal, func=AF.Exp, bias=sc(3),
                                                               scale=SCALE, accum_out=sc(4)),
                                 reads=["ps0", "ps1", "ps2", "ps3", sk_], writes=[pk, sk_])
                            S.op("dve", lambda e: e.tensor_tensor(out=sc(6), in0=sc(4), in1=sc(5), op=ALU.add),
                                 reads=[sk_], writes=[sk_])
                        else:
                            S.op("dve", lambda e: e.tensor_copy(out=sc(6), in_=sc(5)), reads=[sk_], writes=[sk_])
                        S.op("dve", lambda e: e.reciprocal(out=sc(7), in_=sc(6)), reads=[sk_], writes=[sk_])
                        S.op("dve", lambda e: e.tensor_scalar(out=P_[0:qw, 0:16 + nreal], in0=P_[0:qw, 0:16 + nreal],
                                                              scalar1=sc(7), scalar2=None, op0=ALU.mult),
                             reads=[pk, sk_], writes=[pk])

                    def emit_PV(qb):
                        q0, qw, nreal = qinfo(qb)
                        par = qb % 2
                        P_ = Pts[par]
                        pk = "Pt%d" % par
                        nkb = qb + 1
                        slots = {}

                        def emit_T(kb):
                            c0, kw = (0, 16) if kb == 0 else (16 + 128 * (kb - 1), 128)
                            s4 = pti[0] % 2
                            pti[0] += 1
                            slots[kb] = s4
                            ptp = pt_bank[s4][:, 0:128]
                            S.op("pe", lambda e: e.transpose(ptp[0:kw, 0:qw], P_[0:qw, c0:c0 + kw], ident_b[0:qw, 0:qw]),
                                 reads=[pk, "cbf"], writes=["ptp%d" % s4])
                            if kb % 2 == 0:
                                fn = lambda e: e.activation(out=PTs[s4][0:kw, 0:qw], in_=ptp[0:kw, 0:qw], func=AF.Copy)
                                eng = "act"
                            else:
                                fn = lambda e: e.tensor_copy(out=PTs[s4][0:kw, 0:qw], in_=ptp[0:kw, 0:qw])
                                eng = "dve"
                            S.op(eng, fn, reads=["ptp%d" % s4], writes=["PTs%d" % s4])
                        LA = 1
                        for kb in range(min(LA, nkb)):
                            emit_T(kb)
                        for kb in range(nkb):
                            if kb + LA < nkb:
                                emit_T(kb + LA)
                            kw = 16 if kb == 0 else 128
                            s4 = slots[kb]
                            S.op("pe", lambda e, kb=kb, kw=kw, s4=s4: e.matmul(
                                bank(6, qw), Vh[0:kw, kb, :], PTs[s4][0:kw, 0:qw], start=(kb == 0), stop=(kb == nkb - 1)),
                                reads=["Vh", "PTs%d" % s4], writes=["ps6"])
                        S.op("act", lambda e: e.activation(out=ybh[:, q0:q0 + qw], in_=bank(6, qw), func=AF.Copy),
                             reads=["ps6"], writes=["ybh"])

                    emit_S(0)
                    for qb in range(17):
                        emit_softmax(qb)
                        if qb + 1 < 17:
                            emit_S(qb + 1)
                        emit_PV(qb)
                    S.op("sp", lambda e, b=b, h=h: e.dma_start(out=YB[b, h * 128:(h + 1) * 128, :], in_=ybh),
                         reads=["ybh"], writes=["YB"], dsem=d_yb)
                if b == 0 and "DBG_kpe" in debug:
                    for nm, ap_, key_, parts in (("DBG_kpe", kpeT, "kpeT", 64), ("DBG_kn", knT, "knT", 128),
                                                 ("DBG_qpe", qpeT, "qpeT", 64), ("DBG_qn", qnT, "qnT", 128)):
                        dd = dscr(nm, [128, T], BF16)
                        S.op("sp", lambda e, dd=dd, ap_=ap_, parts=parts: e.dma_start(out=dd[0:parts, :], in_=ap_[0:parts, :]),
                             reads=[key_], writes=[nm], dsem=S.dsem(nm))

        def phase5():
            X = ffn_arena()
            NL = 6
            ld = [[(A.alloc([NT], F32), "ld%d_%d" % (i, j), S.dsem("ld%d_%d" % (i, j))) for j in range(NL)] for i in range(1)]
            wk_ = {"sq0": X.sqs[0], "sq1": X.sqs[1], "rstd": X.rstd}
            zT = X.aT[:, 0:KT, :]
            outf = X.aT[:, 0:32, :].rearrange("p a n -> p (a n)").bitcast(F32).rearrange("p (a n) -> p a n", a=KT)
            d_o = S.dsem("outf")
            pan_o = mk_panels([(c, 128) for c in range(0, D, 128)])
            rr = [0]
            for b in range(NB1):
                for ti in range(NTI1):
                    t0 = ti * NT
                    tsl = slice(t0, t0 + NT)
                    S.op("sp", lambda e, b=b, t0=t0: e.dma_start(
                        out=X.hT, in_=H1[b].rearrange("(c p) t -> p c t", p=128)[:, :, t0:t0 + NT]),
                        reads=["H1"], writes=["hT"], dsem=X.d_h)
                    for m in range(KT):
                        ms = slice(m * 128, (m + 1) * 128)
                        srcs = [YA[b, ms, tsl], BONUS[b, ms, tsl], GOUT[b, ms, tsl],
                                PROJ[b, C_GA + m * 128:C_GA + (m + 1) * 128, tsl], PROJ[b, C_GB + m * 128:C_GB + (m + 1) * 128, tsl],
                                YB[b, ms, tsl]]
                        L = ld[0]
                        for j, sap in enumerate(srcs):
                            S.op("sp", lambda e, j=j, sap=sap: e.dma_start(out=L[j][0], in_=sap),
                                 reads=["YA", "SCR", "PROJ", "YB"], writes=[L[j][1]], dsem=L[j][2])
                        y_, bo_, g_, ga_, gb_, yb_ = [L[j][0] for j in range(6)]
                        yk, bok, gk, gak, gbk, ybk = [L[j][1] for j in range(6)]

                        def blkmm(src_ap, skey, evac):
                            for (n0, nw) in NSUB:
                                bk = [6, 7][rr[0] % 2]
                                rr[0] += 1
                                S.op("pe", lambda e, bk=bk, n0=n0, nw=nw: e.matmul(bank(bk, nw), blk1, src_ap[:, n0:n0 + nw], start=True, stop=True),
                                     reads=[skey, "consts"], writes=["ps%d" % bk])
                                evac(bank(bk, nw), "ps%d" % bk, n0, nw)
                        blkmm(y_, yk, lambda bp, bk, n0, nw: S.op("dve", lambda e: e.scalar_tensor_tensor(
                            out=wk_["sq0"][:, n0:n0 + nw], in0=bp, scalar=-1.0 / 64, in1=y_[:, n0:n0 + nw], op0=ALU.mult, op1=ALU.add),
                            reads=[bk, yk], writes=["sq0"]))
                        S.op("act", lambda e: e.activation(out=wk_["sq1"], in_=wk_["sq0"], func=AF.Square), reads=["sq0"], writes=["sq1"])
                        blkmm(wk_["sq1"], "sq1", lambda bp, bk, n0, nw: S.op("dve", lambda e: e.tensor_scalar(
                            out=wk_["rstd"][:, n0:n0 + nw], in0=bp, scalar1=1.0 / 64, scalar2=GN_EPS, op0=ALU.mult, op1=ALU.add),
                            reads=[bk], writes=["rstd"]))
                        S.op("act", lambda e: e.activation(out=wk_["rstd"], in_=wk_["rstd"], func=AF.Sqrt), reads=["rstd"], writes=["rstd"])
                        S.op("dve", lambda e: e.reciprocal(out=wk_["rstd"], in_=wk_["rstd"]), reads=["rstd"], writes=["rstd"])
                        S.op("dve", lambda e: e.tensor_tensor(out=wk_["sq0"], in0=wk_["sq0"], in1=wk_["rstd"], op=ALU.mult),
                             reads=["sq0", "rstd"], writes=["sq0"])
                        S.op("dve", lambda e, m=m: e.tensor_scalar(out=wk_["sq0"], in0=wk_["sq0"], scalar1=pcol("gn_w", m), scalar2=pcol("gn_b", m),
                                                                   op0=ALU.mult, op1=ALU.add), reads=["sq0", "pvec"], writes=["sq0"])
                        S.op("pool", lambda e: e.tensor_tensor(out=wk_["sq0"], in0=wk_["sq0"], in1=bo_, op=ALU.add), reads=["sq0", bok], writes=["sq0"])
                        S.op("pool", lambda e: e.tensor_tensor(out=wk_["sq0"], in0=wk_["sq0"], in1=g_, op=ALU.mult), reads=["sq0", gk], writes=["sq0"])
                        S.op("pool", lambda e: e.tensor_tensor(out=wk_["sq0"], in0=wk_["sq0"], in1=ga_, op=ALU.mult), reads=["sq0", gak], writes=["sq0"])
                        S.op("pool", lambda e: e.tensor_tensor(out=wk_["sq1"], in0=yb_, in1=gb_, op=ALU.mult), reads=[ybk, gbk], writes=["sq1"])
                        S.op("dve", lambda e, m=m: e.tensor_tensor(out=zT[:, m, :], in0=wk_["sq0"], in1=wk_["sq1"], op=ALU.add),
                             reads=["sq0", "sq1"], writes=["aT"])

                    def ev_o(cc0, cw, si, n0, nw, bks):
                        m = cc0 // 128
                        S.op("dve", lambda e: e.tensor_tensor(out=X.hT[:, m, n0:n0 + nw], in0=bank(bks[0], nw), in1=X.hT[:, m, n0:n0 + nw],
                                                              op=ALU.add), reads=["ps%d" % bks[0], "hT"], writes=["hT"])
                    gemm([("wout", wbf["wout"])], KT, 128, pan_o, zT, "aT", X.slots_in, ev_o, NSUB, [0, 1, 2, 3, 4, 5], "wo")
                    rmsnorm(X.hT, X.uT, "ffn2_norm", "hT", "uT", X.sqs, X.rstd, 6, ones_all)
                    ffn(X, "wg2", "wu2", "wd2")
                    rmsnorm(X.hT, outf, "final_norm", "hT", "aT", X.sqs, X.rstd, 6, ones_all)
                    if ti == 0:
                        S.op("sp", lambda e, b=b: e.dma_start(out=out_T[b].rearrange("(c p) t -> p c t", p=128)[:, :, 0:NT - 16],
                                                              in_=outf[:, :, 16:NT]), reads=["aT"], writes=["OUT"], dsem=d_o)
                    else:
                        S.op("sp", lambda e, b=b, t0=t0: e.dma_start(out=out_T[b].rearrange("(c p) t -> p c t", p=128)[:, :, t0 - 16:t0 - 16 + NT],
                                                                     in_=outf), reads=["aT"], writes=["OUT"], dsem=d_o)

        ones_all = None
        ones_all = A.alloc([128], F32)
        base_off = A.off
        S.op("pool", lambda e: e.memset(ones_all, 1.0), writes=["ones"])

        phase0()
        S.barrier()
        if "stop0" not in debug:
            phase1()
            S.barrier()
        if "stop1" not in debug and "stop0" not in debug:
            phase2()
            S.barrier()
            if "no3" not in debug:
                phase3()
                S.barrier()
            if "no4" not in debug:
                phase4()
                S.barrier()
            if "no5" not in debug:
                phase5()
                S.barrier()

        S.final_wait("sp")

        semh = {}
        for e_ in ("pe", "act", "dve", "pool"):
            semh[e_] = es.enter_context(nc.semaphore("s_" + e_))
        for d in S.dsems:
            semh[d.name] = es.enter_context(nc.semaphore(d.name))
        block = es.enter_context(nc.Block())

        @block.tensor
        def _(e):
            S.emit("pe", e, semh)

        @block.scalar
        def _(e):
            S.emit("act", e, semh)

        @block.vector
        def _(e):
            S.emit("dve", e, semh)

        @block.gpsimd
        def _(e):
            S.emit("pool", e, semh)

        @block.sync
        def _(e):
            S.emit("sp", e, semh)
    return nc


def host_prep(inp):
    f = lambda a: np.ascontiguousarray(np.asarray(a, dtype=np.float32))
    x = f(inp["x"])
    meta = f(inp["meta_tokens"])
    w_in = f(inp["w_in"])[0]
    kpe = w_in[:, 7616:7680]
    win_ext = np.concatenate([w_in[:, :7616], kpe[:, :32], kpe[:, :32], kpe[:, 32:], kpe[:, 32:], w_in[:, 7680:]], axis=1)
    wuq = f(inp["w_uq"])[0].reshape(512, 16, 192)
    wuq_ext = np.concatenate([wuq[:, :, :128], wuq[:, :, 128:160], wuq[:, :, 128:160], wuq[:, :, 160:192],
                              wuq[:, :, 160:192]], axis=2).reshape(512, 4096)
    wukv = f(inp["w_ukv"])[0].reshape(512, 16, 256)
    wk = np.ascontiguousarray(wukv[:, :, :128].reshape(512, 2048))
    wv = np.ascontiguousarray(wukv[:, :, 128:].reshape(512, 2048))
    pv = np.zeros((128, NPV), np.float32)

    def put(name, vec, n):
        v = f(vec).reshape(-1)
        if v.size >= 128:
            pv[:, PV[name]:PV[name] + n] = v.reshape(n, 128).T
        else:
            pv[:v.size, PV[name]] = v
    mu = f(inp["tm_mu"])[0]
    put("ffn1_norm", inp["ffn1_norm"], 16); put("mix_norm", inp["mix_norm"], 16)
    put("mu_r", mu[0:2048], 16); put("mu_k", mu[2048:4096], 16); put("mu_v", mu[4096:6144], 16)
    put("w0", inp["w0"], 16); put("a0", inp["a0"], 16); put("k_k", inp["k_k"], 16); put("k_a", inp["k_a"], 16)
    put("r_k", inp["r_k"], 16); put("gn_w", inp["gn_w"], 16); put("gn_b", inp["gn_b"], 16)
    put("ffn2_norm", inp["ffn2_norm"], 16); put("final_norm", inp["final_norm"], 16)
    put("q_norm", inp["q_norm"], 4); put("kv_norm", inp["kv_norm"], 4)
    put("mu_xw", mu[6144:6240], 1); put("mu_xa", mu[6240:6336], 1); put("mu_xg", mu[6336:6592], 2)
    consts = np.zeros((128, 384), np.float32)
    consts[:64, :64] = 1.0
    consts[64:, 64:128] = 1.0
    consts[:, 128:256] = np.eye(128, dtype=np.float32)
    qi = np.arange(128)[:, None]
    ki = np.arange(128)[None, :]
    consts[:, 256:384] = np.where(ki <= qi, 0.0, -30000.0).astype(np.float32)
    pos = np.arange(T, dtype=np.float32)
    inv_freq = (1.0 / (np.float32(10000.0) ** (np.arange(0, 64, 2, dtype=np.float32) / np.float32(64)))).astype(np.float32)
    ang = (pos[None, :] * inv_freq[:, None]).astype(np.float32)
    cs, sn = np.cos(ang).astype(np.float32), np.sin(ang).astype(np.float32)
    rope = np.concatenate([np.concatenate([cs, sn], 0), np.concatenate([-sn, cs], 0)], axis=1)
    shared = {
        "pvec": pv, "consts": consts, "rope": np.ascontiguousarray(rope),
        "wg1": f(inp["ffn1_w_gate"])[0], "wu1": f(inp["ffn1_w_up"])[0], "wd1": f(inp["ffn1_w_down"])[0],
        "win": np.ascontiguousarray(win_ext),
        "wup": f(inp["w_up"])[0], "aup": f(inp["a_up"])[0], "gup": f(inp["g_up"])[0],
        "wuq": np.ascontiguousarray(wuq_ext), "wk": wk, "wv": wv, "wout": f(inp["w_out"])[0],
        "wg2": f(inp["ffn2_w_gate"])[0], "wu2": f(inp["ffn2_w_up"])[0], "wd2": f(inp["ffn2_w_down"])[0],
    }
    in_maps = []
    for c in range(NCORES):
        hT = np.empty((NB, D, T), np.float32)
        for j in range(NB):
            b = c * NB + j
            hT[j, :, :16] = meta.T
            hT[j, :, 16:] = x[b].T
        m = dict(shared)
        m["hT"] = hT
        in_maps.append(m)
    return in_maps


def kernel(**inputs):
    in_maps = host_prep(inputs)
    nc = build_program()
    res = run_bass_kernel_spmd(nc, in_maps, core_ids=list(range(NCORES)))
    out = np.empty((NCORES * NB, T - 16, D), np.float32)
    for c in range(NCORES):
        oT = np.asarray(res.results[c]["outT"])
        for j in range(NB):
            out[c * NB + j] = oT[j].T
    return out
```

```python
import math
from contextlib import ExitStack
import numpy as np
import concourse.bass as bass
import concourse.mybir as mybir
from concourse.bass_utils import run_bass_kernel_spmd

F32 = mybir.dt.float32
BF16 = mybir.dt.bfloat16
U8 = mybir.dt.uint8
AF = mybir.ActivationFunctionType
ALU = mybir.AluOpType
AX = mybir.AxisListType

NCORES = 8
D = 2048
T = 2064
NB = 2
DFF = 5632
KT = D // 128
FT = DFF // 128
NT = 688
NTI = T // NT
NSUB = [(0, 344), (344, 344)]
NH = 32
MH = 16
EPS = 1e-6
GN_EPS = 64 * 1e-5
C_R, C_K, C_V = 0, 2048, 4096
C_XW, C_XA, C_XG = 6144, 6240, 6336
C_CQ, C_CKV, C_KPA, C_KPB, C_GA, C_GB = 6592, 7104, 7616, 7680, 7744, 9792
NIN = 11840
PV = {}
_o = 0
for _n, _w in [("ffn1_norm", 16), ("mix_norm", 16), ("mu_r", 16), ("mu_k", 16), ("mu_v", 16), ("w0", 16),
               ("a0", 16), ("k_k", 16), ("k_a", 16), ("r_k", 16), ("gn_w", 16), ("gn_b", 16),
               ("ffn2_norm", 16), ("final_norm", 16), ("q_norm", 4), ("kv_norm", 4),
               ("mu_xw", 1), ("mu_xa", 1), ("mu_xg", 2)]:
    PV[_n] = _o
    _o += _w
NPV = _o


class DSem:
    def __init__(self, name):
        self.name = name
        self.count = 0


class Sched:
    ENG = ("pe", "act", "dve", "pool", "sp")

    def __init__(self):
        self.q = {e: [] for e in self.ENG}
        self.cnt = {e: 0 for e in ("pe", "act", "dve", "pool")}
        self.bufs = {}
        self.waited = {e: {} for e in self.ENG}
        self.dsems = []
        self.barrier_tokens = []

    def dsem(self, name):
        d = DSem("d%d_%s" % (len(self.dsems), name))
        self.dsems.append(d)
        return d

    def barrier(self):
        toks = [(e, c) for e, c in self.cnt.items() if c > 0]
        toks += [(d.name, d.count) for d in self.dsems if d.count > 0]
        self.barrier_tokens = toks

    def op(self, eng, fn, reads=(), writes=(), dsem=None):
        deps = {}

        def add(tok):
            if tok is None:
                return
            s, v = tok
            if deps.get(s, 0) < v:
                deps[s] = v
        for k in reads:
            b = self.bufs.get(k)
            if b:
                add(b[0])
        for k in writes:
            b = self.bufs.get(k)
            if b:
                add(b[0])
                for s, v in b[1].items():
                    add((s, v))
        for tok in self.barrier_tokens:
            add(tok)
        if dsem is not None:
            dsem.count += 16
            token = (dsem.name, dsem.count)
            signal = (dsem.name, 16)
        else:
            self.cnt[eng] += 1
            token = (eng, self.cnt[eng])
            signal = (eng, 1)
        waits = []
        wd = self.waited[eng]
        for s, v in deps.items():
            if wd.get(s, 0) < v:
                wd[s] = v
                waits.append((s, v))
        for k in reads:
            b = self.bufs.setdefault(k, [None, {}])
            if b[1].get(token[0], 0) < token[1]:
                b[1][token[0]] = token[1]
        for k in writes:
            self.bufs[k] = [token, {}]
        self.q[eng].append((waits, fn, signal))
        return token

    def final_wait(self, eng="sp"):
        self.barrier()
        waits = []
        for s, v in self.barrier_tokens:
            if self.waited[eng].get(s, 0) < v:
                waits.append((s, v))
        self.q[eng].append((waits, None, None))

    def emit(self, eng, e, semh):
        for waits, fn, signal in self.q[eng]:
            for s, v in waits:
                e.wait_ge(semh[s], v)
            if fn is None:
                continue
            inst = fn(e)
            inst.then_inc(semh[signal[0]], signal[1])


class Arena:
    def __init__(self, ap, nbytes):
        self.ap = ap
        self.nbytes = nbytes
        self.off = 0

    def reset(self, off=0):
        self.off = off

    def alloc(self, shape, dtype, parts=128):
        esz = 4 if dtype == F32 else 2
        n = 1
        for s in shape:
            n *= s
        nb = (n * esz + 63) // 64 * 64
        assert self.off + nb <= self.nbytes, ("arena overflow", self.off, nb, self.nbytes)
        v = self.ap[0:parts, self.off:self.off + nb]
        self.off += nb
        v = v[:, 0:n * esz].bitcast(dtype)
        if len(shape) == 2:
            v = v.rearrange("p (a b) -> p a b", a=shape[0])
        elif len(shape) == 3:
            v = v.rearrange("p (a b c) -> p a b c", a=shape[0], b=shape[1])
        return v


def build_program(debug=()):
    nc = bass.Bass("TRN2", target_bir_lowering=False)
    S = Sched()
    NB1, NTI1 = (1, 1) if "one_tile" in debug else ((1, NTI) if "one_b" in debug else (NB, NTI))

    def din(name, shape, dt=F32):
        return nc.dram_tensor(name, list(shape), dt, kind="ExternalInput").ap()

    def dscr(name, shape, dt=F32):
        kind = "ExternalOutput" if name in debug else "Internal"
        return nc.dram_tensor(name, list(shape), dt, kind=kind).ap()

    hT_in = din("hT", [NB, D, T])
    pvec_in = din("pvec", [128, NPV])
    consts_in = din("consts", [128, 3 * 128])
    wsrc = {
        "wg1": din("wg1", [D, DFF]), "wu1": din("wu1", [D, DFF]), "wd1": din("wd1", [DFF, D]),
        "win": din("win", [D, NIN]),
        "wup": din("wup", [96, D]), "aup": din("aup", [96, D]), "gup": din("gup", [256, D]),
        "wuq": din("wuq", [512, 4096]), "wk": din("wk", [512, D]), "wv": din("wv", [512, D]),
        "wout": din("wout", [D, D]),
        "wg2": din("wg2", [D, DFF]), "wu2": din("wu2", [D, DFF]), "wd2": din("wd2", [DFF, D]),
    }
    out_T = nc.dram_tensor("outT", [NB, D, T - 16], F32, kind="ExternalOutput").ap()
    wbf = {k: dscr("b_" + k, v.shape, BF16) for k, v in wsrc.items()}
    H1 = dscr("H1", [NB, D, T])
    PROJ = dscr("PROJ", [NB, NIN, T])
    SC = {k: dscr("SC_" + k, [NB, D, T]) for k in ("r", "w", "k", "v", "kn", "b")}
    GOUT = dscr("GOUT", [NB, D, T])
    BONUS = dscr("BONUS", [NB, D, T])
    YA = dscr("YA", [NB, D, T])
    YB = dscr("YB", [NB, D, T])

    ARENA_BYTES = 190 * 1024
    with ExitStack() as es:
        arena_t = es.enter_context(nc.sbuf_tensor("arena", [128, ARENA_BYTES], U8))
        psum_t = es.enter_context(nc.psum_tensor("psum", [128, 8, 512], F32))
        A = Arena(arena_t, ARENA_BYTES)

        def bank(i, n=512, parts=128):
            return psum_t[0:parts, i, 0:n]

        pvec = A.alloc([NPV], F32)
        consts = A.alloc([3 * 128], F32)
        cbf = A.alloc([2 * 128], BF16)
        base_off = A.off
        blk1 = consts[:, 0:128]
        ident_f = consts[:, 128:256]
        ident_b = cbf[:, 0:128]
        mask_b = cbf[:, 128:256]

        def pcol(name, c=0, parts=128):
            return pvec[0:parts, PV[name] + c:PV[name] + c + 1]

        d_pv = S.dsem("pvec")
        S.op("sp", lambda e: e.dma_start(out=pvec, in_=pvec_in), writes=["pvec"], dsem=d_pv)
        d_cs = S.dsem("consts")
        S.op("sp", lambda e: e.dma_start(out=consts, in_=consts_in), writes=["consts"], dsem=d_cs)
        S.op("dve", lambda e: e.tensor_copy(out=cbf, in_=consts[:, 128:384]), reads=["consts"], writes=["cbf"])

        def phase0():
            A.reset(base_off)
            CH = 4096
            NS = 3
            st_f = [A.alloc([CH], F32) for _ in range(NS)]
            st_b = [A.alloc([CH], BF16) for _ in range(NS)]
            ds = [S.dsem("cv%d" % i) for i in range(NS)]
            ds2 = [S.dsem("cvb%d" % i) for i in range(NS)]
            engs = ["act", "dve", "pool"]
            it = 0
            for name, src in wsrc.items():
                R, C = src.shape
                dst = wbf[name]
                nchunk = (C + CH - 1) // CH
                cw = (C + nchunk - 1) // nchunk
                for r0 in range(0, R, 128):
                    rp = min(128, R - r0)
                    for c0 in range(0, C, cw):
                        w = min(cw, C - c0)
                        s = it % NS
                        eng = engs[it % 3]
                        it += 1
                        f_ap = st_f[s][0:rp, 0:w]
                        b_ap = st_b[s][0:rp, 0:w]
                        S.op("sp", lambda e, f_ap=f_ap, src=src, r0=r0, rp=rp, c0=c0, w=w:
                             e.dma_start(out=f_ap, in_=src[r0:r0 + rp, c0:c0 + w]),
                             writes=["cvf%d" % s], dsem=ds[s])
                        if eng == "act":
                            fn = lambda e, f_ap=f_ap, b_ap=b_ap: e.activation(out=b_ap, in_=f_ap, func=AF.Copy)
                        else:
                            fn = lambda e, f_ap=f_ap, b_ap=b_ap: e.tensor_copy(out=b_ap, in_=f_ap)
                        S.op(eng, fn, reads=["cvf%d" % s], writes=["cvb%d" % s])
                        S.op("sp", lambda e, b_ap=b_ap, dst=dst, r0=r0, rp=rp, c0=c0, w=w:
                             e.dma_start(out=dst[r0:r0 + rp, c0:c0 + w], in_=b_ap),
                             reads=["cvb%d" % s], writes=["W_" + name], dsem=ds2[s])

        class Ctx:
            pass

        def rmsnorm(hT, uT, gname, hkey, ukey, sqs, rstd, pb0, ones_ap, nchunk=KT, dim=D, nt=NT, nsub=NSUB):
            for c in range(nchunk):
                sq = sqs[c % 2]
                sk = "sq%d" % (c % 2)
                S.op("act", lambda e, c=c, sq=sq: e.activation(out=sq[:, 0:nt], in_=hT[:, c, 0:nt], func=AF.Square),
                     reads=[hkey], writes=[sk])

                def mm(e, c=c, sq=sq):
                    i = None
                    for si, (n0, nw) in enumerate(nsub):
                        i = e.matmul(bank(pb0 + si, nw), ones_ap, sq[:, n0:n0 + nw],
                                     start=(c == 0), stop=(c == nchunk - 1))
                    return i
                S.op("pe", mm, reads=[sk, "consts", "ones"], writes=["ps%d" % (pb0 + si) for si in range(len(nsub))])
            for si, (n0, nw) in enumerate(nsub):
                S.op("dve", lambda e, si=si, n0=n0, nw=nw: e.tensor_scalar(
                    out=rstd[:, n0:n0 + nw], in0=bank(pb0 + si, nw), scalar1=1.0 / dim, scalar2=EPS,
                    op0=ALU.mult, op1=ALU.add), reads=["ps%d" % (pb0 + si)], writes=["rstd"])
            S.op("act", lambda e: e.activation(out=rstd[:, 0:nt], in_=rstd[:, 0:nt], func=AF.Sqrt),
                 reads=["rstd"], writes=["rstd"])
            S.op("dve", lambda e: e.reciprocal(out=rstd[:, 0:nt], in_=rstd[:, 0:nt]), reads=["rstd"], writes=["rstd"])
            for c in range(nchunk):
                S.op("dve", lambda e, c=c: e.scalar_tensor_tensor(
                    out=uT[:, c, 0:nt], in0=hT[:, c, 0:nt], scalar=pcol(gname, c), in1=rstd[:, 0:nt],
                    op0=ALU.mult, op1=ALU.mult), reads=[hkey, "rstd", "pvec"], writes=[ukey])

        psrr = [0]

        def gemm(wlist, kt, kp, panels, act, actkey, slots, evac, nsub, banks, tag):
            nw_ = len(wlist)
            npan = len(panels)

            def load(pi):
                c0, pw, _ = panels[pi]
                sl_ap, sl_key, sl_ds = slots[pi % len(slots)]
                for wi, (wname, wap) in enumerate(wlist):
                    if kt > 1:
                        src = wap.rearrange("(c p) m -> p c m", p=kp)[:, :, c0:c0 + pw]
                    else:
                        src = wap[:, c0:c0 + pw].unsqueeze(1)
                    S.op("sp", lambda e, sl_ap=sl_ap, wi=wi, src=src, pw=pw: e.dma_start(
                        out=sl_ap[0:kp, wi, 0:kt, 0:pw], in_=src),
                        reads=["W_" + wname], writes=[sl_key], dsem=sl_ds)
            load(0)
            for pi in range(npan):
                if pi + 1 < npan:
                    load(pi + 1)
                c0, pw, chunks = panels[pi]
                sl_ap, sl_key, sl_ds = slots[pi % len(slots)]
                for (cc0, cw) in chunks:
                    for si, (n0, nw) in enumerate(nsub):
                        bks = []
                        for wi in range(nw_):
                            bk = banks[psrr[0] % len(banks)]
                            psrr[0] += 1
                            bks.append(bk)

                            def mm(e, wi=wi, bk=bk, cc0=cc0, cw=cw, n0=n0, nw=nw, sl_ap=sl_ap, c0=c0):
                                i = None
                                for k in range(kt):
                                    i = e.matmul(bank(bk, nw, cw), sl_ap[0:kp, wi, k, cc0 - c0:cc0 - c0 + cw],
                                                 act[0:kp, k, n0:n0 + nw], start=(k == 0), stop=(k == kt - 1))
                                return i
                            S.op("pe", mm, reads=[sl_key, actkey], writes=["ps%d" % bk])
                        evac(cc0, cw, si, n0, nw, bks)

        def mk_panels(chunks, pw=256):
            panels = []
            cur = []
            for (c0, w) in chunks:
                if cur and (c0 + w - cur[0][0] > pw or cur[-1][0] + cur[-1][1] != c0):
                    panels.append((cur[0][0], cur[-1][0] + cur[-1][1] - cur[0][0], cur))
                    cur = []
                cur.append((c0, w))
            if cur:
                panels.append((cur[0][0], cur[-1][0] + cur[-1][1] - cur[0][0], cur))
            return panels

        def ffn(X, wg, wu, wd):
            pan = mk_panels([(c, 128) for c in range(0, DFF, 128)])

            def ev_gu(cc0, cw, si, n0, nw, bks):
                f = cc0 // 128
                tm = X.tmp[si % 2]
                tk = "tmp%d" % (si % 2)
                S.op("act", lambda e: e.activation(out=tm[:, 0:nw], in_=bank(bks[0], nw), func=AF.Silu),
                     reads=["ps%d" % bks[0]], writes=[tk])
                S.op("dve", lambda e: e.tensor_tensor(out=X.aT[:, f, n0:n0 + nw], in0=bank(bks[1], nw),
                                                      in1=tm[:, 0:nw], op=ALU.mult),
                     reads=["ps%d" % bks[1], tk], writes=["aT"])
            gemm([(wg, wbf[wg]), (wu, wbf[wu])], KT, 128, pan, X.uT, "uT", X.slots_gu, ev_gu, NSUB,
                 [0, 1, 2, 3, 4, 5], "gu")
            pan_d = mk_panels([(c, 128) for c in range(0, D, 128)], pw=128)

            def ev_d(cc0, cw, si, n0, nw, bks):
                m = cc0 // 128
                S.op("dve", lambda e: e.scalar_tensor_tensor(
                    out=X.hT[:, m, n0:n0 + nw], in0=bank(bks[0], nw), scalar=0.5, in1=X.hT[:, m, n0:n0 + nw],
                    op0=ALU.mult, op1=ALU.add), reads=["ps%d" % bks[0], "hT"], writes=["hT"])
            gemm([(wd, wbf[wd])], FT, 128, pan_d, X.aT, "aT", X.slots_d, ev_d, NSUB, [0, 1, 2, 3, 4, 5], "dn")

        def ffn_arena():
            A.reset(base_off)
            X = Ctx()
            X.hT = A.alloc([KT, NT], F32)
            X.uT = A.alloc([KT, NT], BF16)
            X.aT = A.alloc([FT, NT], BF16)
            X.sqs = [A.alloc([NT], F32) for _ in range(2)]
            X.rstd = A.alloc([NT], F32)
            X.tmp = [A.alloc([344], F32) for _ in range(2)]
            X.d_h = S.dsem("hT")
            slot_bytes = 2 * KT * 256 * 2
            X.slots_gu = []
            X.slots_d = []
            X.slots_in = []
            for i in range(2):
                off = A.off
                raw = A.alloc([slot_bytes // 2], BF16)
                ds = S.dsem("wslot%d_%d" % (i, len(S.dsems)))
                key = "wslot%d" % i
                X.slots_gu.append((raw[:, 0:2 * KT * 256].rearrange("p (w k m) -> p w k m", w=2, k=KT), key, ds))
                X.slots_d.append((raw[:, 0:FT * 128].rearrange("p (w k m) -> p w k m", w=1, k=FT), key, ds))
                X.slots_in.append((raw[:, 0:KT * 256].rearrange("p (w k m) -> p w k m", w=1, k=KT), key, ds))
            return X

        def phase1():
            X = ffn_arena()
            NST = 4
            stg = [A.alloc([NT], F32) for _ in range(NST)]
            dst = [S.dsem("stg%d_%d" % (i, len(S.dsems))) for i in range(NST)]
            chunks = [(c, 128) for c in range(0, C_XW, 128)]
            chunks += [(C_XW, 96), (C_XA, 96), (C_XG, 128), (C_XG + 128, 128)]
            chunks += [(c, 128) for c in range(C_CQ, C_KPA, 128)]
            chunks += [(C_KPA, 64), (C_KPB, 64)]
            chunks += [(c, 128) for c in range(C_GA, NIN, 128)]
            pan_in = mk_panels(chunks)
            ones_f = None
            for b in range(NB1):
                for ti in range(NTI1):
                    t0 = ti * NT
                    S.op("sp", lambda e, b=b, t0=t0: e.dma_start(
                        out=X.hT, in_=hT_in[b].rearrange("(c p) t -> p c t", p=128)[:, :, t0:t0 + NT]),
                        writes=["hT"], dsem=X.d_h)
                    rmsnorm(X.hT, X.uT, "ffn1_norm", "hT", "uT", X.sqs, X.rstd, 6, ones_all)
                    ffn(X, "wg1", "wu1", "wd1")
                    S.op("sp", lambda e, b=b, t0=t0: e.dma_start(
                        out=H1[b].rearrange("(c p) t -> p c t", p=128)[:, :, t0:t0 + NT], in_=X.hT),
                        reads=["hT"], writes=["H1"], dsem=X.d_h)
                    rmsnorm(X.hT, X.uT, "mix_norm", "hT", "uT", X.sqs, X.rstd, 6, ones_all)
                    cnt = [0]

                    def ev_in(cc0, cw, si, n0, nw, bks, b=b, t0=t0):
                        s = cnt[0] % NST
                        sk = "stg%d" % s
                        gate = cc0 >= C_GA
                        if gate:
                            S.op("act", lambda e: e.activation(out=stg[s][0:cw, n0:n0 + nw], in_=bank(bks[0], nw, cw),
                                                               func=AF.Sigmoid),
                                 reads=["ps%d" % bks[0]], writes=[sk])
                        else:
                            eng = "act" if (cnt[0] % 2 == 0) else "dve"
                            if eng == "act":
                                fn = lambda e: e.activation(out=stg[s][0:cw, n0:n0 + nw], in_=bank(bks[0], nw, cw),
                                                            func=AF.Copy)
                            else:
                                fn = lambda e: e.tensor_copy(out=stg[s][0:cw, n0:n0 + nw], in_=bank(bks[0], nw, cw))
                            S.op(eng, fn, reads=["ps%d" % bks[0]], writes=[sk])
                        if si == len(NSUB) - 1:
                            S.op("sp", lambda e: e.dma_start(out=PROJ[b, cc0:cc0 + cw, t0:t0 + NT],
                                                             in_=stg[s][0:cw, 0:NT]),
                                 reads=[sk], writes=["PROJ"], dsem=dst[s])
                            cnt[0] += 1
                    gemm([("win", wbf["win"])], KT, 128, pan_in, X.uT, "uT", X.slots_in, ev_in, NSUB,
                         [0, 1, 2, 3, 4, 5], "in")


        def phase2():
            A.reset(base_off)
            NP1 = NT + 1
            wup_s = A.alloc([D], BF16)
            aup_s = A.alloc([D], BF16)
            gup_s = A.alloc([2, D], BF16)
            d_l = S.dsem("lora")
            S.op("sp", lambda e: e.dma_start(out=wup_s[0:96, :], in_=wbf["wup"]), reads=["W_wup"], writes=["lw"], dsem=d_l)
            S.op("sp", lambda e: e.dma_start(out=aup_s[0:96, :], in_=wbf["aup"]), reads=["W_aup"], writes=["lw"], dsem=d_l)
            S.op("sp", lambda e: e.dma_start(out=gup_s, in_=wbf["gup"].rearrange("(c p) m -> p c m", p=128)),
                 reads=["W_gup"], writes=["lw"], dsem=d_l)
            raw = {n: [(A.alloc([NP1], F32), "raw_%s%d" % (n, i), S.dsem("raw_%s%d" % (n, i))) for i in range(2)]
                   for n in ("r", "k", "v")}
            xs = A.alloc([4, NP1], F32)
            d_xs = S.dsem("xs")
            txw = A.alloc([NT], BF16)
            txa = A.alloc([NT], BF16)
            tsg = A.alloc([2, NT], BF16)
            names = ["dtmp", "tmpf", "sh_r", "sh_k", "sh_v", "kk", "a_t", "wdec", "tmpA", "tmpB", "kn", "bt", "tq", "k2", "rk", "bon", "gsb"]
            W_ = {n: A.alloc([NT], F32) for n in names}
            DS = {n: S.dsem("o_" + n) for n in ("sh_r", "wdec", "k2", "sh_v", "kn", "bt", "bon", "gsb")}
            rr = [0]

            def nb_():
                b_ = [0, 1, 2, 3, 4, 5][rr[0] % 6]
                rr[0] += 1
                return b_

            def shift(dst, dkey, src, skey, mucol, parts=128):
                S.op("dve", lambda e: e.tensor_tensor(out=W_["dtmp"][0:parts, :], in0=src[0:parts, 0:NT], in1=src[0:parts, 1:NP1],
                                                      op=ALU.subtract), reads=[skey], writes=["dtmp"])
                S.op("dve", lambda e: e.scalar_tensor_tensor(out=dst, in0=W_["dtmp"][0:parts, :], scalar=mucol,
                                                             in1=src[0:parts, 1:NP1], op0=ALU.mult, op1=ALU.add),
                     reads=["dtmp", skey, "pvec"], writes=[dkey])

            def load_halo(dst, key, ds, b, row0, nrows, t0):
                if t0 == 0:
                    S.op("pool", lambda e: e.memset(dst[0:nrows, 0:1], 0.0), writes=[key])
                    S.op("sp", lambda e: e.dma_start(out=dst[0:nrows, 1:NP1], in_=PROJ[b, row0:row0 + nrows, 0:NT]),
                         reads=["PROJ"], writes=[key], dsem=ds)
                else:
                    S.op("sp", lambda e: e.dma_start(out=dst[0:nrows, 0:NP1], in_=PROJ[b, row0:row0 + nrows, t0 - 1:t0 + NT]),
                         reads=["PROJ"], writes=[key], dsem=ds)

            def mm_ev(mms, parts, evac):
                for (n0, nw) in NSUB:
                    bk = nb_()

                    def f(e, bk=bk, n0=n0, nw=nw):
                        i = None
                        for j, (l, rf) in enumerate(mms):
                            i = e.matmul(bank(bk, nw, parts), l, rf(n0, nw), start=(j == 0), stop=(j == len(mms) - 1))
                        return i
                    S.op("pe", f, reads=["lw", "txw", "txa", "tsg", "consts", "tmpB", "rk"], writes=["ps%d" % bk])
                    evac(bank(bk, nw, parts), n0, nw, "ps%d" % bk)

            def store(name, dst_ap):
                S.op("sp", lambda e: e.dma_start(out=dst_ap, in_=W_[name]), reads=[name], writes=["SCR"], dsem=DS[name])

            for b in range(NB1):
                for ti in range(NTI1):
                    t0 = ti * NT
                    for i, (r0, nr) in enumerate([(C_XW, 96), (C_XA, 96), (C_XG, 128), (C_XG + 128, 128)]):
                        load_halo(xs[:, i, :], "xs", d_xs, b, r0, nr, t0)
                    shift(W_["tmpf"][0:96, :], "tmpf", xs[:, 0, :], "xs", pcol("mu_xw", 0, 96), 96)
                    S.op("act", lambda e: e.activation(out=txw[0:96, :], in_=W_["tmpf"][0:96, :], func=AF.Tanh),
                         reads=["tmpf"], writes=["txw"])
                    shift(txa[0:96, :], "txa", xs[:, 1, :], "xs", pcol("mu_xa", 0, 96), 96)
                    for c in range(2):
                        shift(W_["tmpf"], "tmpf", xs[:, 2 + c, :], "xs", pcol("mu_xg", c))
                        S.op("act", lambda e, c=c: e.activation(out=tsg[:, c, :], in_=W_["tmpf"], func=AF.Sigmoid),
                             reads=["tmpf"], writes=["tsg"])
                    for m in range(KT):
                        ms = slice(m * 128, (m + 1) * 128)
                        for n, c0 in (("r", C_R), ("k", C_K), ("v", C_V)):
                            ap_, key, ds = raw[n][m % 2]
                            load_halo(ap_, key, ds, b, c0 + m * 128, 128, t0)
                            shift(W_["sh_" + n], "sh_" + n, ap_, key, pcol("mu_" + n, m))
                        mm_ev([(wup_s[0:96, ms], lambda n0, nw: txw[0:96, n0:n0 + nw])], 128,
                              lambda bp, n0, nw, bk, m=m: S.op("act", lambda e: e.activation(
                                  out=W_["tmpA"][:, n0:n0 + nw], in_=bp, func=AF.Sigmoid, bias=pcol("w0", m)),
                                  reads=[bk, "pvec"], writes=["tmpA"]))
                        S.op("act", lambda e: e.activation(out=W_["wdec"], in_=W_["tmpA"], func=AF.Exp, scale=-math.exp(-0.5)),
                             reads=["tmpA"], writes=["wdec"])
                        mm_ev([(aup_s[0:96, ms], lambda n0, nw: txa[0:96, n0:n0 + nw])], 128,
                              lambda bp, n0, nw, bk, m=m: S.op("act", lambda e: e.activation(
                                  out=W_["a_t"][:, n0:n0 + nw], in_=bp, func=AF.Sigmoid, bias=pcol("a0", m)),
                                  reads=[bk, "pvec"], writes=["a_t"]))
                        mm_ev([(gup_s[:, 0, ms], lambda n0, nw: tsg[:, 0, n0:n0 + nw]),
                               (gup_s[:, 1, ms], lambda n0, nw: tsg[:, 1, n0:n0 + nw])], 128,
                              lambda bp, n0, nw, bk: S.op("act", lambda e: e.activation(
                                  out=W_["gsb"][:, n0:n0 + nw], in_=bp, func=AF.Copy), reads=[bk], writes=["gsb"]))
                        S.op("dve", lambda e, m=m: e.tensor_scalar(out=W_["kk"], in0=W_["sh_k"], scalar1=pcol("k_k", m), scalar2=None,
                                                                   op0=ALU.mult), reads=["sh_k", "pvec"], writes=["kk"])
                        S.op("act", lambda e: e.activation(out=W_["tmpB"], in_=W_["kk"], func=AF.Square), reads=["kk"], writes=["tmpB"])
                        mm_ev([(blk1, lambda n0, nw: W_["tmpB"][:, n0:n0 + nw])], 128,
                              lambda bp, n0, nw, bk: S.op("dve", lambda e: e.tensor_scalar(
                                  out=W_["tq"][:, n0:n0 + nw], in0=bp, scalar1=1e-24, scalar2=None, op0=ALU.max),
                                  reads=[bk], writes=["tq"]))
                        S.op("act", lambda e: e.activation(out=W_["tq"], in_=W_["tq"], func=AF.Sqrt), reads=["tq"], writes=["tq"])
                        S.op("dve", lambda e: e.reciprocal(out=W_["tq"], in_=W_["tq"]), reads=["tq"], writes=["tq"])
                        S.op("dve", lambda e: e.scalar_tensor_tensor(out=W_["kn"], in0=W_["kk"], scalar=-1.0, in1=W_["tq"],
                                                                     op0=ALU.mult, op1=ALU.mult), reads=["kk", "tq"], writes=["kn"])
                        S.op("dve", lambda e: e.scalar_tensor_tensor(out=W_["bt"], in0=W_["kn"], scalar=-1.0, in1=W_["a_t"],
                                                                     op0=ALU.mult, op1=ALU.mult), reads=["kn", "a_t"], writes=["bt"])
                        S.op("dve", lambda e, m=m: e.tensor_scalar(out=W_["tq"], in0=W_["a_t"], scalar1=-1.0, scalar2=pcol("k_a", m),
                                                                   op0=ALU.add, op1=ALU.mult), reads=["a_t", "pvec"], writes=["tq"])
                        S.op("dve", lambda e: e.scalar_tensor_tensor(out=W_["k2"], in0=W_["tq"], scalar=1.0, in1=W_["sh_k"],
                                                                     op0=ALU.add, op1=ALU.mult), reads=["tq", "sh_k"], writes=["k2"])
                        S.op("dve", lambda e, m=m: e.scalar_tensor_tensor(out=W_["rk"], in0=W_["sh_r"], scalar=pcol("r_k", m), in1=W_["k2"],
                                                                          op0=ALU.mult, op1=ALU.mult), reads=["sh_r", "k2", "pvec"], writes=["rk"])
                        mm_ev([(blk1, lambda n0, nw: W_["rk"][:, n0:n0 + nw])], 128,
                              lambda bp, n0, nw, bk: S.op("dve", lambda e: e.tensor_tensor(
                                  out=W_["bon"][:, n0:n0 + nw], in0=bp, in1=W_["sh_v"][:, n0:n0 + nw], op=ALU.mult),
                                  reads=[bk, "sh_v"], writes=["bon"]))
                        tsl = slice(t0, t0 + NT)
                        store("sh_r", SC["r"][b, ms, tsl]); store("wdec", SC["w"][b, ms, tsl]); store("k2", SC["k"][b, ms, tsl])
                        store("sh_v", SC["v"][b, ms, tsl]); store("kn", SC["kn"][b, ms, tsl]); store("bt", SC["b"][b, ms, tsl])
                        store("bon", BONUS[b, ms, tsl]); store("gsb", GOUT[b, ms, tsl])

        def phase4():
            A.reset(base_off)
            TC = 32
            PH = NB1 * 32
            P2_ = 2 * PH
            sets = []
            for i in range(2):
                tl = {n: A.alloc([64, TC], F32) for n in ("r", "w", "k", "kn", "b")}
                tl["v"] = A.alloc([32, TC], F32)
                sets.append((tl, S.dsem("scin%d" % i), "scin%d" % i))
            ysets = [(A.alloc([32, TC], F32), S.dsem("yout%d" % i), "yout%d" % i) for i in range(2)]
            St = A.alloc([32, 64], F32)
            Sw = [A.alloc([32, 64], F32) for _ in range(2)]
            T1 = A.alloc([32, 64], F32)
            T2 = A.alloc([32, 64], F32)
            T3 = A.alloc([32, 64], F32)
            sa = A.alloc([32], F32)
            S.op("dve", lambda e: e.memset(St[0:P2_], 0.0), writes=["St"])
            nch = (T + TC - 1) // TC

            def load(c):
                tl, ds, key = sets[c % 2]
                t0 = c * TC
                tc = min(TC, T - t0)
                for n in ("r", "w", "k", "kn", "b"):
                    src = SC[n][0:NB1].rearrange("b (h j) t -> (b h) j t", j=64)
                    for half in range(2):
                        for jh in range(2):
                            S.op("sp", lambda e, n=n, half=half, jh=jh, src=src, tl=tl, t0=t0, tc=tc: e.dma_start(
                                out=tl[n][half * PH:(half + 1) * PH, jh * 32:(jh + 1) * 32, 0:tc],
                                in_=src[:, jh * 32:(jh + 1) * 32, t0:t0 + tc]), reads=["SCR"], writes=[key], dsem=ds)
                srcv = SC["v"][0:NB1].rearrange("b (h x i) t -> (b h) x i t", x=2, i=32)
                for half in range(2):
                    S.op("sp", lambda e, half=half, tl=tl, t0=t0, tc=tc: e.dma_start(
                        out=tl["v"][half * PH:(half + 1) * PH, :, 0:tc], in_=srcv[:, half, :, t0:t0 + tc]),
                        reads=["SCR"], writes=[key], dsem=ds)
            load(0)
            dsty = YA[0:NB1].rearrange("b (h x i) t -> (b h) x i t", x=2, i=32)
            step = 0
            for c in range(nch):
                if c + 1 < nch:
                    load(c + 1)
                tl, ds, key = sets[c % 2]
                yt, yds, ykey = ysets[c % 2]
                t0 = c * TC
                tc = min(TC, T - t0)
                for tt in range(tc):
                    def bj(n):
                        return tl[n][0:P2_, :, tt].unsqueeze(1).broadcast_to([P2_, 32, 64])
                    vb = tl["v"][0:P2_, :, tt].unsqueeze(2).broadcast_to([P2_, 32, 64])
                    sw = Sw[step % 2]
                    swk = "Sw%d" % (step % 2)
                    S.op("pool", lambda e, vb=vb, kb=bj("k"): e.tensor_tensor(out=T3[0:P2_], in0=vb, in1=kb, op=ALU.mult),
                         reads=[key], writes=["T3"])
                    S.op("pool", lambda e, sw=sw, wb=bj("w"): e.tensor_tensor(out=sw[0:P2_], in0=St[0:P2_], in1=wb, op=ALU.mult),
                         reads=[key, "St"], writes=[swk])
                    S.op("pool", lambda e, sw=sw: e.tensor_tensor(out=sw[0:P2_], in0=sw[0:P2_], in1=T3[0:P2_], op=ALU.add),
                         reads=[swk, "T3"], writes=[swk])
                    S.op("dve", lambda e, knb=bj("kn"): e.tensor_tensor(out=T1[0:P2_], in0=St[0:P2_], in1=knb, op=ALU.mult),
                         reads=[key, "St"], writes=["T1"])
                    S.op("dve", lambda e: e.tensor_reduce(out=sa[0:P2_], in_=T1[0:P2_], axis=AX.X, op=ALU.add),
                         reads=["T1"], writes=["sa"])
                    S.op("dve", lambda e, bb=bj("b"): e.tensor_tensor(
                        out=T2[0:P2_], in0=sa[0:P2_].unsqueeze(2).broadcast_to([P2_, 32, 64]), in1=bb, op=ALU.mult),
                        reads=[key, "sa"], writes=["T2"])
                    S.op("dve", lambda e, sw=sw: e.tensor_tensor(out=St[0:P2_], in0=sw[0:P2_], in1=T2[0:P2_], op=ALU.add),
                         reads=[swk, "T2"], writes=["St"])
                    S.op("dve", lambda e, rb=bj("r"): e.tensor_tensor(out=T1[0:P2_], in0=St[0:P2_], in1=rb, op=ALU.mult),
                         reads=[key, "St"], writes=["T1"])
                    S.op("dve", lambda e, yt=yt, tt=tt: e.tensor_reduce(out=yt[0:P2_, :, tt], in_=T1[0:P2_], axis=AX.X, op=ALU.add),
                         reads=["T1"], writes=[ykey])
                    step += 1
                for half in range(2):
                    S.op("sp", lambda e, half=half, yt=yt, t0=t0, tc=tc: e.dma_start(
                        out=dsty[:, half, :, t0:t0 + tc], in_=yt[half * PH:(half + 1) * PH, :, 0:tc]),
                        reads=[ykey], writes=["YA"], dsem=yds)

        def phase3():
            A.reset(base_off)
            SUBS = [(0, 512), (512, 512), (1024, 512), (1536, 512), (2048, 16)]
            SCALE = 192.0 ** -0.5
            cqn = A.alloc([4, T], BF16)
            ckvn = A.alloc([4, T], BF16)
            kpeT = A.alloc([T], BF16)
            ropeT = A.alloc([2, T], F32)
            wuq_s = A.alloc([4, 4096], BF16)
            wk_s = A.alloc([4, D], BF16)
            wv_s = A.alloc([4, D], BF16)
            d_w = S.dsem("mlaw")
            I32 = mybir.dt.int32
            rawck = A.alloc([6, 512], F32)
            rawc = rawck[:, 0:4, :]
            rawk = rawck[:, 4:6, :]
            ybh = A.alloc([T], F32)
            r_i = cqn[:, 0:2, :].rearrange("p a t -> p (a t)").bitcast(I32)
            r_x = ybh
            r_k = rawck.rearrange("p a t -> p (a t)")[:, 0:T]
            r_s = A.alloc([8], F32)
            r_si = r_s.bitcast(I32)
            TWO_PI = 2.0 * math.pi
            S.op("pool", lambda e: e.iota(r_si[0:64, 6:7], [[0, 1]], base=0, channel_multiplier=1), writes=["r_s"])
            S.op("dve", lambda e: e.tensor_copy(out=r_s[0:64, 0:1], in_=r_si[0:64, 6:7]), reads=["r_s"], writes=["r_s"])
            S.op("dve", lambda e: e.tensor_scalar(out=r_s[0:64, 1:2], in0=r_s[0:64, 0:1], scalar1=31.5, scalar2=None, op0=ALU.is_gt),
                 reads=["r_s"], writes=["r_s"])
            S.op("dve", lambda e: e.scalar_tensor_tensor(out=r_s[0:64, 2:3], in0=r_s[0:64, 1:2], scalar=-32.0, in1=r_s[0:64, 0:1],
                                                         op0=ALU.mult, op1=ALU.add), reads=["r_s"], writes=["r_s"])
            S.op("act", lambda e: e.activation(out=r_s[0:64, 3:4], in_=r_s[0:64, 2:3], func=AF.Exp, scale=-math.log(10000.0) / 32.0),
                 reads=["r_s"], writes=["r_s"])
            S.op("dve", lambda e: e.tensor_scalar(out=r_s[0:64, 4:5], in0=r_s[0:64, 1:2], scalar1=-math.pi / 2, scalar2=math.pi / 2,
                                                  op0=ALU.mult, op1=ALU.add), reads=["r_s"], writes=["r_s"])
            S.op("dve", lambda e: e.tensor_scalar(out=r_s[0:64, 5:6], in0=r_s[0:64, 4:5], scalar1=math.pi / 2, scalar2=None, op0=ALU.add),
                 reads=["r_s"], writes=["r_s"])
            S.op("pool", lambda e: e.iota(r_i[0:64, :], [[1, T]], base=0, channel_multiplier=0), writes=["cqn"])
            S.op("dve", lambda e: e.tensor_copy(out=r_x[0:64, :], in_=r_i[0:64, :]), reads=["cqn"], writes=["ybh"])
            S.op("dve", lambda e: e.tensor_scalar(out=r_x[0:64, :], in0=r_x[0:64, :], scalar1=r_s[0:64, 3:4], scalar2=None, op0=ALU.mult),
                 reads=["ybh", "r_s"], writes=["ybh"])
            for a_ in range(2):
                S.op("dve", lambda e, a_=a_: e.tensor_scalar(out=r_k[0:64, :], in0=r_x[0:64, :], scalar1=r_s[0:64, 4 + a_:5 + a_],
                                                             scalar2=None, op0=ALU.add), reads=["ybh", "r_s"], writes=["rawc", "rawk"])
                S.op("dve", lambda e: e.tensor_scalar(out=r_i[0:64, :], in0=r_k[0:64, :], scalar1=1.0 / TWO_PI, scalar2=None, op0=ALU.mult),
                     reads=["rawc", "rawk"], writes=["cqn"])
                S.op("dve", lambda e, a_=a_: e.tensor_copy(out=ropeT[0:64, a_, :], in_=r_i[0:64, :]), reads=["cqn"], writes=["rope"])
                S.op("dve", lambda e, a_=a_: e.scalar_tensor_tensor(out=r_k[0:64, :], in0=ropeT[0:64, a_, :], scalar=-TWO_PI, in1=r_k[0:64, :],
                                                                    op0=ALU.mult, op1=ALU.add), reads=["rope", "rawc", "rawk"], writes=["rawc", "rawk"])
                S.op("dve", lambda e, a_=a_: e.tensor_scalar(out=ropeT[0:64, a_, :], in0=r_k[0:64, :], scalar1=math.pi, scalar2=None, op0=ALU.is_gt),
                     reads=["rawc", "rawk"], writes=["rope"])
                S.op("dve", lambda e, a_=a_: e.scalar_tensor_tensor(out=r_k[0:64, :], in0=ropeT[0:64, a_, :], scalar=-TWO_PI, in1=r_k[0:64, :],
                                                                    op0=ALU.mult, op1=ALU.add), reads=["rope", "rawc", "rawk"], writes=["rawc", "rawk"])
                S.op("dve", lambda e, a_=a_: e.tensor_scalar(out=ropeT[0:64, a_, :], in0=r_k[0:64, :], scalar1=-math.pi, scalar2=None, op0=ALU.is_lt),
                     reads=["rawc", "rawk"], writes=["rope"])
                S.op("dve", lambda e, a_=a_: e.scalar_tensor_tensor(out=r_k[0:64, :], in0=ropeT[0:64, a_, :], scalar=TWO_PI, in1=r_k[0:64, :],
                                                                    op0=ALU.mult, op1=ALU.add), reads=["rope", "rawc", "rawk"], writes=["rawc", "rawk"])
                S.op("act", lambda e, a_=a_: e.activation(out=ropeT[0:64, a_, :], in_=r_k[0:64, :], func=AF.Sin), reads=["rawc", "rawk"], writes=["rope"])
            S.op("sp", lambda e: e.dma_start(out=wuq_s, in_=wbf["wuq"].rearrange("(c p) m -> p c m", p=128)),
                 reads=["W_wuq"], writes=["mlaw"], dsem=d_w)
            S.op("sp", lambda e: e.dma_start(out=wk_s, in_=wbf["wk"].rearrange("(c p) m -> p c m", p=128)),
                 reads=["W_wk"], writes=["mlaw"], dsem=d_w)
            S.op("sp", lambda e: e.dma_start(out=wv_s, in_=wbf["wv"].rearrange("(c p) m -> p c m", p=128)),
                 reads=["W_wv"], writes=["mlaw"], dsem=d_w)
            d_rc = S.dsem("rawc")
            d_rk = S.dsem("rawk")
            sq2 = [A.alloc([512], F32) for _ in range(2)]
            rs = A.alloc([512], F32)
            t1 = A.alloc([512], F32)
            t2 = A.alloc([512], F32)
            qnT = A.alloc([T], BF16)
            qpeT = A.alloc([T], BF16)
            knT = A.alloc([T], BF16)
            Vh = A.alloc([17, 128], BF16)
            Pts = [A.alloc([16 + 2048], BF16) for _ in range(2)]
            pti = [0]
            PTs = [A.alloc([128], BF16) for _ in range(4)]
            d_yb = S.dsem("ybh")
            st_ = A.alloc([16], F32)
            pt_bank = [psum_t[:, 5, 0:256].bitcast(BF16), psum_t[:, 7, 0:256].bitcast(BF16)]

            def norm_lat(b, row0, dstT, gname, dkey):
                for (n0, nw) in SUBS:
                    S.op("sp", lambda e, n0=n0, nw=nw: e.dma_start(
                        out=rawc[:, :, 0:nw], in_=PROJ[b, row0:row0 + 512, n0:n0 + nw].rearrange("(c p) t -> p c t", p=128)),
                        reads=["PROJ"], writes=["rawc"], dsem=d_rc)
                    for c in range(4):
                        S.op("act", lambda e, c=c, nw=nw: e.activation(out=sq2[c % 2][:, 0:nw], in_=rawc[:, c, 0:nw], func=AF.Square),
                             reads=["rawc"], writes=["sqm%d" % (c % 2)])
                        S.op("pe", lambda e, c=c, nw=nw: e.matmul(bank(6, nw), ones_all, sq2[c % 2][:, 0:nw], start=(c == 0), stop=(c == 3)),
                             reads=["sqm%d" % (c % 2), "ones"], writes=["ps6"])
                    S.op("dve", lambda e, nw=nw: e.tensor_scalar(out=rs[:, 0:nw], in0=bank(6, nw), scalar1=1.0 / 512, scalar2=EPS,
                                                                 op0=ALU.mult, op1=ALU.add), reads=["ps6"], writes=["rs"])
                    S.op("act", lambda e, nw=nw: e.activation(out=rs[:, 0:nw], in_=rs[:, 0:nw], func=AF.Sqrt), reads=["rs"], writes=["rs"])
                    S.op("dve", lambda e, nw=nw: e.reciprocal(out=rs[:, 0:nw], in_=rs[:, 0:nw]), reads=["rs"], writes=["rs"])
                    for c in range(4):
                        S.op("dve", lambda e, c=c, n0=n0, nw=nw: e.scalar_tensor_tensor(
                            out=dstT[:, c, n0:n0 + nw], in0=rawc[:, c, 0:nw], scalar=pcol(gname, c), in1=rs[:, 0:nw],
                            op0=ALU.mult, op1=ALU.mult), reads=["rawc", "rs", "pvec"], writes=[dkey])

            def rope_comb(dst, dkey, a_ap, b_ap, n0, nw, rkeys):
                S.op("dve", lambda e: e.tensor_tensor(out=t1[0:64, 0:nw], in0=a_ap, in1=ropeT[0:64, 0, n0:n0 + nw], op=ALU.mult),
                     reads=rkeys + ["rope"], writes=["t1"])
                S.op("dve", lambda e: e.tensor_tensor(out=t2[0:64, 0:nw], in0=b_ap, in1=ropeT[0:64, 1, n0:n0 + nw], op=ALU.mult),
                     reads=rkeys + ["rope"], writes=["t2"])
                S.op("dve", lambda e: e.tensor_tensor(out=dst[0:64, n0:n0 + nw], in0=t1[0:64, 0:nw], in1=t2[0:64, 0:nw], op=ALU.add),
                     reads=["t1", "t2"], writes=[dkey])

            brr = [0]

            def proj(lhs_fn, M, src, skey, evac):
                for (n0, nw) in SUBS:
                    bk = [0, 1, 2, 3][brr[0] % 4]
                    brr[0] += 1

                    def f(e, bk=bk, n0=n0, nw=nw):
                        i = None
                        for k in range(4):
                            i = e.matmul(bank(bk, nw, M), lhs_fn(k), src[:, k, n0:n0 + nw], start=(k == 0), stop=(k == 3))
                        return i
                    S.op("pe", f, reads=["mlaw", skey], writes=["ps%d" % bk])
                    evac(bank(bk, nw, M), "ps%d" % bk, n0, nw)

            for b in range(NB1):
                norm_lat(b, C_CQ, cqn, "q_norm", "cqn")
                norm_lat(b, C_CKV, ckvn, "kv_norm", "ckvn")
                for (n0, nw) in SUBS:
                    S.op("sp", lambda e, n0=n0, nw=nw, b=b: e.dma_start(out=rawk[0:64, 0, 0:nw], in_=PROJ[b, C_KPA:C_KPA + 64, n0:n0 + nw]),
                         reads=["PROJ"], writes=["rawk"], dsem=d_rk)
                    S.op("sp", lambda e, n0=n0, nw=nw, b=b: e.dma_start(out=rawk[0:64, 1, 0:nw], in_=PROJ[b, C_KPB:C_KPB + 64, n0:n0 + nw]),
                         reads=["PROJ"], writes=["rawk"], dsem=d_rk)
                    rope_comb(kpeT, "kpeT", rawk[0:64, 0, 0:nw], rawk[0:64, 1, 0:nw], n0, nw, ["rawk"])
                for h in range(MH):
                    proj(lambda k, h=h: wuq_s[:, k, h * 256:h * 256 + 128], 128, cqn, "cqn",
                         lambda bp, bk, n0, nw: S.op("act", lambda e: e.activation(out=qnT[:, n0:n0 + nw], in_=bp, func=AF.Copy),
                                                     reads=[bk], writes=["qnT"]))
                    for (n0, nw) in SUBS:
                        bka, bkb = 0, 1

                        def fa(e, n0=n0, nw=nw, h=h):
                            i = None
                            for k in range(4):
                                i = e.matmul(bank(0, nw, 64), wuq_s[:, k, h * 256 + 128:h * 256 + 192], cqn[:, k, n0:n0 + nw],
                                             start=(k == 0), stop=(k == 3))
                            for k in range(4):
                                i = e.matmul(bank(1, nw, 64), wuq_s[:, k, h * 256 + 192:h * 256 + 256], cqn[:, k, n0:n0 + nw],
                                             start=(k == 0), stop=(k == 3))
                            return i
                        S.op("pe", fa, reads=["mlaw", "cqn"], writes=["ps0", "ps1"])
                        rope_comb(qpeT, "qpeT", bank(0, nw, 64), bank(1, nw, 64), n0, nw, ["ps0", "ps1"])
                    proj(lambda k, h=h: wk_s[:, k, h * 128:(h + 1) * 128], 128, ckvn, "ckvn",
                         lambda bp, bk, n0, nw: S.op("act", lambda e: e.activation(out=knT[:, n0:n0 + nw], in_=bp, func=AF.Copy),
                                                     reads=[bk], writes=["knT"]))
                    for kb in range(17):
                        tk0, tw = (0, 16) if kb == 0 else (16 + 128 * (kb - 1), 128)
                        bk = [2, 3][kb % 2]

                        def fv(e, bk=bk, tk0=tk0, tw=tw, h=h):
                            i = None
                            for k in range(4):
                                i = e.matmul(bank(bk, 128, tw), ckvn[:, k, tk0:tk0 + tw], wv_s[:, k, h * 128:(h + 1) * 128],
                                             start=(k == 0), stop=(k == 3))
                            return i
                        S.op("pe", fv, reads=["mlaw", "ckvn"], writes=["ps%d" % bk])
                        S.op("dve", lambda e, bk=bk, kb=kb, tw=tw: e.tensor_copy(out=Vh[0:tw, kb, :], in_=bank(bk, 128, tw)),
                             reads=["ps%d" % bk], writes=["Vh"])
                    def qinfo(qb):
                        q0, qw = (0, 16) if qb == 0 else (16 + 128 * (qb - 1), 128)
                        return q0, qw, 128 * qb

                    def emit_S(qb):
                        q0, qw, nreal = qinfo(qb)
                        nkc = (nreal + 511) // 512

                        def fs(e, q0=q0, qw=qw, qb=qb, nreal=nreal, nkc=nkc):
                            i = e.matmul(bank(4, 16, qw), qnT[:, q0:q0 + qw], knT[:, 0:16], start=True, stop=False)
                            i = e.matmul(bank(4, 16, qw), qpeT[0:64, q0:q0 + qw], kpeT[0:64, 0:16], start=False, stop=(qb != 0))
                            if qb == 0:
                                i = e.matmul(bank(4, 16, 16), ident_b[0:16, 0:16], mask_b[0:16, 0:16], start=False, stop=True)
                            for kc in range(nkc):
                                k0 = 16 + 512 * kc
                                kw = min(512, nreal - 512 * kc)
                                last = (kc == nkc - 1)
                                i = e.matmul(bank(kc, kw, qw), qnT[:, q0:q0 + qw], knT[:, k0:k0 + kw], start=True, stop=False)
                                i = e.matmul(bank(kc, kw, qw), qpeT[0:64, q0:q0 + qw], kpeT[0:64, k0:k0 + kw], start=False, stop=not last)
                                if last:
                                    off = kw - 128
                                    i = e.matmul(psum_t[0:qw, kc, off:off + 128], ident_b, mask_b, start=False, stop=True)
                            return i
                        S.op("pe", fs, reads=["qnT", "qpeT", "knT", "kpeT", "cbf"], writes=["ps0", "ps1", "ps2", "ps3", "ps4"])

                    def emit_softmax(qb):
                        q0, qw, nreal = qinfo(qb)
                        par = qb % 2
                        P_ = Pts[par]
                        pk = "Pt%d" % par
                        sk_ = "st%d" % par
                        so = 8 * par
                        sc = lambda j: st_[0:qw, so + j:so + j + 1]
                        sreal = psum_t[0:qw, 0:4, :].rearrange("p a b -> p (a b)")[:, 0:max(nreal, 1)]
                        S.op("dve", lambda e: e.tensor_reduce(out=sc(1), in_=bank(4, 16, qw), axis=AX.X, op=ALU.max),
                             reads=["ps4"], writes=[sk_])
                        if qb > 0:
                            S.op("dve", lambda e: e.tensor_reduce(out=sc(0), in_=sreal, axis=AX.X, op=ALU.max),
                                 reads=["ps0", "ps1", "ps2", "ps3"], writes=[sk_])
                            S.op("dve", lambda e: e.tensor_tensor(out=sc(2), in0=sc(0), in1=sc(1), op=ALU.max),
                                 reads=[sk_], writes=[sk_])
                        else:
                            S.op("dve", lambda e: e.tensor_copy(out=sc(2), in_=sc(1)), reads=[sk_], writes=[sk_])
                        S.op("dve", lambda e: e.tensor_scalar(out=sc(3), in0=sc(2), scalar1=-SCALE, scalar2=None, op0=ALU.mult),
                             reads=[sk_], writes=[sk_])
                        S.op("act", lambda e: e.activation(out=P_[0:qw, 0:16], in_=bank(4, 16, qw), func=AF.Exp, bias=sc(3),
                                                           scale=SCALE, accum_out=sc(5)),
                             reads=["ps4", sk_], writes=[pk, sk_])
                        if qb > 0:
                            S.op("act", lambda e: e.activation(out=P_[0:qw, 16:16 + nreal], in_=sreal, func=AF.Exp, bias=sc(3),
                                                               scale=SCALE, accum_out=sc(4)),
                                 reads=["ps0", "ps1", "ps2", "ps3", sk_], writes=[pk, sk_])
                            S.op("dve", lambda e: e.tensor_tensor(out=sc(6), in0=sc(4), in1=sc(5), op=ALU.add),
                                 reads=[sk_], writes=[sk_])
                        else:
                            S.op("dve", lambda e: e.tensor_copy(out=sc(6), in_=sc(5)), reads=[sk_], writes=[sk_])
                        S.op("dve", lambda e: e.reciprocal(out=sc(7), in_=sc(6)), reads=[sk_], writes=[sk_])
                        S.op("dve", lambda e: e.tensor_scalar(out=P_[0:qw, 0:16 + nreal], in0=P_[0:qw, 0:16 + nreal],
                                                              scalar1=sc(7), scalar2=None, op0=ALU.mult),
                             reads=[pk, sk_], writes=[pk])

                    def emit_PV(qb):
                        q0, qw, nreal = qinfo(qb)
                        par = qb % 2
                        P_ = Pts[par]
                        pk = "Pt%d" % par
                        nkb = qb + 1
                        slots = {}

                        def emit_T(kb):
                            c0, kw = (0, 16) if kb == 0 else (16 + 128 * (kb - 1), 128)
                            s4 = pti[0] % 2
                            pti[0] += 1
                            slots[kb] = s4
                            ptp = pt_bank[s4][:, 0:128]
                            S.op("pe", lambda e: e.transpose(ptp[0:kw, 0:qw], P_[0:qw, c0:c0 + kw], ident_b[0:qw, 0:qw]),
                                 reads=[pk, "cbf"], writes=["ptp%d" % s4])
                            if kb % 2 == 0:
                                fn = lambda e: e.activation(out=PTs[s4][0:kw, 0:qw], in_=ptp[0:kw, 0:qw], func=AF.Copy)
                                eng = "act"
                            else:
                                fn = lambda e: e.tensor_copy(out=PTs[s4][0:kw, 0:qw], in_=ptp[0:kw, 0:qw])
                                eng = "dve"
                            S.op(eng, fn, reads=["ptp%d" % s4], writes=["PTs%d" % s4])
                        LA = 1
                        for kb in range(min(LA, nkb)):
                            emit_T(kb)
                        for kb in range(nkb):
                            if kb + LA < nkb:
                                emit_T(kb + LA)
                            kw = 16 if kb == 0 else 128
                            s4 = slots[kb]
                            S.op("pe", lambda e, kb=kb, kw=kw, s4=s4: e.matmul(
                                bank(6, qw), Vh[0:kw, kb, :], PTs[s4][0:kw, 0:qw], start=(kb == 0), stop=(kb == nkb - 1)),
                                reads=["Vh", "PTs%d" % s4], writes=["ps6"])
                        S.op("act", lambda e: e.activation(out=ybh[:, q0:q0 + qw], in_=bank(6, qw), func=AF.Copy),
                             reads=["ps6"], writes=["ybh"])

                    emit_S(0)
                    for qb in range(17):
                        emit_softmax(qb)
                        if qb + 1 < 17:
                            emit_S(qb + 1)
                        emit_PV(qb)
                    S.op("sp", lambda e, b=b, h=h: e.dma_start(out=YB[b, h * 128:(h + 1) * 128, :], in_=ybh),
                         reads=["ybh"], writes=["YB"], dsem=d_yb)
                if b == 0 and "DBG_kpe" in debug:
                    for nm, ap_, key_, parts in (("DBG_kpe", kpeT, "kpeT", 64), ("DBG_kn", knT, "knT", 128),
                                                 ("DBG_qpe", qpeT, "qpeT", 64), ("DBG_qn", qnT, "qnT", 128)):
                        dd = dscr(nm, [128, T], BF16)
                        S.op("sp", lambda e, dd=dd, ap_=ap_, parts=parts: e.dma_start(out=dd[0:parts, :], in_=ap_[0:parts, :]),
                             reads=[key_], writes=[nm], dsem=S.dsem(nm))

        def phase5():
            X = ffn_arena()
            NL = 6
            ld = [[(A.alloc([NT], F32), "ld%d_%d" % (i, j), S.dsem("ld%d_%d" % (i, j))) for j in range(NL)] for i in range(1)]
            wk_ = {"sq0": X.sqs[0], "sq1": X.sqs[1], "rstd": X.rstd}
            zT = X.aT[:, 0:KT, :]
            outf = X.aT[:, 0:32, :].rearrange("p a n -> p (a n)").bitcast(F32).rearrange("p (a n) -> p a n", a=KT)
            d_o = S.dsem("outf")
            pan_o = mk_panels([(c, 128) for c in range(0, D, 128)])
            rr = [0]
            for b in range(NB1):
                for ti in range(NTI1):
                    t0 = ti * NT
                    tsl = slice(t0, t0 + NT)
                    S.op("sp", lambda e, b=b, t0=t0: e.dma_start(
                        out=X.hT, in_=H1[b].rearrange("(c p) t -> p c t", p=128)[:, :, t0:t0 + NT]),
                        reads=["H1"], writes=["hT"], dsem=X.d_h)
                    for m in range(KT):
                        ms = slice(m * 128, (m + 1) * 128)
                        srcs = [YA[b, ms, tsl], BONUS[b, ms, tsl], GOUT[b, ms, tsl],
                                PROJ[b, C_GA + m * 128:C_GA + (m + 1) * 128, tsl], PROJ[b, C_GB + m * 128:C_GB + (m + 1) * 128, tsl],
                                YB[b, ms, tsl]]
                        L = ld[0]
                        for j, sap in enumerate(srcs):
                            S.op("sp", lambda e, j=j, sap=sap: e.dma_start(out=L[j][0], in_=sap),
                                 reads=["YA", "SCR", "PROJ", "YB"], writes=[L[j][1]], dsem=L[j][2])
                        y_, bo_, g_, ga_, gb_, yb_ = [L[j][0] for j in range(6)]
                        yk, bok, gk, gak, gbk, ybk = [L[j][1] for j in range(6)]

                        def blkmm(src_ap, skey, evac):
                            for (n0, nw) in NSUB:
                                bk = [6, 7][rr[0] % 2]
                                rr[0] += 1
                                S.op("pe", lambda e, bk=bk, n0=n0, nw=nw: e.matmul(bank(bk, nw), blk1, src_ap[:, n0:n0 + nw], start=True, stop=True),
                                     reads=[skey, "consts"], writes=["ps%d" % bk])
                                evac(bank(bk, nw), "ps%d" % bk, n0, nw)
                        blkmm(y_, yk, lambda bp, bk, n0, nw: S.op("dve", lambda e: e.scalar_tensor_tensor(
                            out=wk_["sq0"][:, n0:n0 + nw], in0=bp, scalar=-1.0 / 64, in1=y_[:, n0:n0 + nw], op0=ALU.mult, op1=ALU.add),
                            reads=[bk, yk], writes=["sq0"]))
                        S.op("act", lambda e: e.activation(out=wk_["sq1"], in_=wk_["sq0"], func=AF.Square), reads=["sq0"], writes=["sq1"])
                        blkmm(wk_["sq1"], "sq1", lambda bp, bk, n0, nw: S.op("dve", lambda e: e.tensor_scalar(
                            out=wk_["rstd"][:, n0:n0 + nw], in0=bp, scalar1=1.0 / 64, scalar2=GN_EPS, op0=ALU.mult, op1=ALU.add),
                            reads=[bk], writes=["rstd"]))
                        S.op("act", lambda e: e.activation(out=wk_["rstd"], in_=wk_["rstd"], func=AF.Sqrt), reads=["rstd"], writes=["rstd"])
                        S.op("dve", lambda e: e.reciprocal(out=wk_["rstd"], in_=wk_["rstd"]), reads=["rstd"], writes=["rstd"])
                        S.op("dve", lambda e: e.tensor_tensor(out=wk_["sq0"], in0=wk_["sq0"], in1=wk_["rstd"], op=ALU.mult),
                             reads=["sq0", "rstd"], writes=["sq0"])
                        S.op("dve", lambda e, m=m: e.tensor_scalar(out=wk_["sq0"], in0=wk_["sq0"], scalar1=pcol("gn_w", m), scalar2=pcol("gn_b", m),
                                                                   op0=ALU.mult, op1=ALU.add), reads=["sq0", "pvec"], writes=["sq0"])
                        S.op("pool", lambda e: e.tensor_tensor(out=wk_["sq0"], in0=wk_["sq0"], in1=bo_, op=ALU.add), reads=["sq0", bok], writes=["sq0"])
                        S.op("pool", lambda e: e.tensor_tensor(out=wk_["sq0"], in0=wk_["sq0"], in1=g_, op=ALU.mult), reads=["sq0", gk], writes=["sq0"])
                        S.op("pool", lambda e: e.tensor_tensor(out=wk_["sq0"], in0=wk_["sq0"], in1=ga_, op=ALU.mult), reads=["sq0", gak], writes=["sq0"])
                        S.op("pool", lambda e: e.tensor_tensor(out=wk_["sq1"], in0=yb_, in1=gb_, op=ALU.mult), reads=[ybk, gbk], writes=["sq1"])
                        S.op("dve", lambda e, m=m: e.tensor_tensor(out=zT[:, m, :], in0=wk_["sq0"], in1=wk_["sq1"], op=ALU.add),
                             reads=["sq0", "sq1"], writes=["aT"])

                    def ev_o(cc0, cw, si, n0, nw, bks):
                        m = cc0 // 128
                        S.op("dve", lambda e: e.tensor_tensor(out=X.hT[:, m, n0:n0 + nw], in0=bank(bks[0], nw), in1=X.hT[:, m, n0:n0 + nw],
                                                              op=ALU.add), reads=["ps%d" % bks[0], "hT"], writes=["hT"])
                    gemm([("wout", wbf["wout"])], KT, 128, pan_o, zT, "aT", X.slots_in, ev_o, NSUB, [0, 1, 2, 3, 4, 5], "wo")
                    rmsnorm(X.hT, X.uT, "ffn2_norm", "hT", "uT", X.sqs, X.rstd, 6, ones_all)
                    ffn(X, "wg2", "wu2", "wd2")
                    rmsnorm(X.hT, outf, "final_norm", "hT", "aT", X.sqs, X.rstd, 6, ones_all)
                    if ti == 0:
                        S.op("sp", lambda e, b=b: e.dma_start(out=out_T[b].rearrange("(c p) t -> p c t", p=128)[:, :, 0:NT - 16],
                                                              in_=outf[:, :, 16:NT]), reads=["aT"], writes=["OUT"], dsem=d_o)
                    else:
                        S.op("sp", lambda e, b=b, t0=t0: e.dma_start(out=out_T[b].rearrange("(c p) t -> p c t", p=128)[:, :, t0 - 16:t0 - 16 + NT],
                                                                     in_=outf), reads=["aT"], writes=["OUT"], dsem=d_o)

        ones_all = None
        ones_all = A.alloc([128], F32)
        base_off = A.off
        S.op("pool", lambda e: e.memset(ones_all, 1.0), writes=["ones"])

        phase0()
        S.barrier()
        if "stop0" not in debug:
            phase1()
            S.barrier()
        if "stop1" not in debug and "stop0" not in debug:
            phase2()
            S.barrier()
            if "no3" not in debug:
                phase3()
                S.barrier()
            if "no4" not in debug:
                phase4()
                S.barrier()
            if "no5" not in debug:
                phase5()
                S.barrier()

        S.final_wait("sp")

        semh = {}
        for e_ in ("pe", "act", "dve", "pool"):
            semh[e_] = es.enter_context(nc.semaphore("s_" + e_))
        for d in S.dsems:
            semh[d.name] = es.enter_context(nc.semaphore(d.name))
        block = es.enter_context(nc.Block())

        @block.tensor
        def _(e):
            S.emit("pe", e, semh)

        @block.scalar
        def _(e):
            S.emit("act", e, semh)

        @block.vector
        def _(e):
            S.emit("dve", e, semh)

        @block.gpsimd
        def _(e):
            S.emit("pool", e, semh)

        @block.sync
        def _(e):
            S.emit("sp", e, semh)
    return nc


def host_prep(inp):
    f = lambda a: np.ascontiguousarray(np.asarray(a, dtype=np.float32))
    x = f(inp["x"])
    meta = f(inp["meta_tokens"])
    w_in = f(inp["w_in"])[0]
    kpe = w_in[:, 7616:7680]
    win_ext = np.concatenate([w_in[:, :7616], kpe[:, :32], kpe[:, :32], kpe[:, 32:], kpe[:, 32:], w_in[:, 7680:]], axis=1)
    wuq = f(inp["w_uq"])[0].reshape(512, 16, 192)
    wuq_ext = np.concatenate([wuq[:, :, :128], wuq[:, :, 128:160], wuq[:, :, 128:160], wuq[:, :, 160:192],
                              wuq[:, :, 160:192]], axis=2).reshape(512, 4096)
    wukv = f(inp["w_ukv"])[0].reshape(512, 16, 256)
    wk = np.ascontiguousarray(wukv[:, :, :128].reshape(512, 2048))
    wv = np.ascontiguousarray(wukv[:, :, 128:].reshape(512, 2048))
    pv = np.zeros((128, NPV), np.float32)

    def put(name, vec, n):
        v = f(vec).reshape(-1)
        if v.size >= 128:
            pv[:, PV[name]:PV[name] + n] = v.reshape(n, 128).T
        else:
            pv[:v.size, PV[name]] = v
    mu = f(inp["tm_mu"])[0]
    put("ffn1_norm", inp["ffn1_norm"], 16); put("mix_norm", inp["mix_norm"], 16)
    put("mu_r", mu[0:2048], 16); put("mu_k", mu[2048:4096], 16); put("mu_v", mu[4096:6144], 16)
    put("w0", inp["w0"], 16); put("a0", inp["a0"], 16); put("k_k", inp["k_k"], 16); put("k_a", inp["k_a"], 16)
    put("r_k", inp["r_k"], 16); put("gn_w", inp["gn_w"], 16); put("gn_b", inp["gn_b"], 16)
    put("ffn2_norm", inp["ffn2_norm"], 16); put("final_norm", inp["final_norm"], 16)
    put("q_norm", inp["q_norm"], 4); put("kv_norm", inp["kv_norm"], 4)
    put("mu_xw", mu[6144:6240], 1); put("mu_xa", mu[6240:6336], 1); put("mu_xg", mu[6336:6592], 2)
    consts = np.zeros((128, 384), np.float32)
    consts[:64, :64] = 1.0
    consts[64:, 64:128] = 1.0
    consts[:, 128:256] = np.eye(128, dtype=np.float32)
    qi = np.arange(128)[:, None]
    ki = np.arange(128)[None, :]
    consts[:, 256:384] = np.where(ki <= qi, 0.0, -30000.0).astype(np.float32)
    shared = {
        "pvec": pv, "consts": consts,
        "wg1": f(inp["ffn1_w_gate"])[0], "wu1": f(inp["ffn1_w_up"])[0], "wd1": f(inp["ffn1_w_down"])[0],
        "win": np.ascontiguousarray(win_ext),
        "wup": f(inp["w_up"])[0], "aup": f(inp["a_up"])[0], "gup": f(inp["g_up"])[0],
        "wuq": np.ascontiguousarray(wuq_ext), "wk": wk, "wv": wv, "wout": f(inp["w_out"])[0],
        "wg2": f(inp["ffn2_w_gate"])[0], "wu2": f(inp["ffn2_w_up"])[0], "wd2": f(inp["ffn2_w_down"])[0],
    }
    in_maps = []
    for c in range(NCORES):
        hT = np.empty((NB, D, T), np.float32)
        for j in range(NB):
            b = c * NB + j
            hT[j, :, :16] = meta.T
            hT[j, :, 16:] = x[b].T
        m = dict(shared)
        m["hT"] = hT
        in_maps.append(m)
    return in_maps


def kernel(**inputs):
    in_maps = host_prep(inputs)
    nc = build_program()
    res = run_bass_kernel_spmd(nc, in_maps, core_ids=list(range(NCORES)))
    out = np.empty((NCORES * NB, T - 16, D), np.float32)
    for c in range(NCORES):
        oT = np.asarray(res.results[c]["outT"])
        for j in range(NB):
            out[c * NB + j] = oT[j].T
    return out
```

```python
import math
from contextlib import ExitStack
import numpy as np
import concourse.bass as bass
import concourse.mybir as mybir
from concourse.bass_utils import run_bass_kernel_spmd

F32 = mybir.dt.float32
BF16 = mybir.dt.bfloat16
U8 = mybir.dt.uint8
AF = mybir.ActivationFunctionType
ALU = mybir.AluOpType
AX = mybir.AxisListType

NCORES = 8
D = 2048
T = 2064
NB = 2
DFF = 5632
KT = D // 128
FT = DFF // 128
NT = 688
NTI = T // NT
NSUB = [(0, 344), (344, 344)]
NH = 32
MH = 16
EPS = 1e-6
GN_EPS = 64 * 1e-5
C_R, C_K, C_V = 0, 2048, 4096
C_XW, C_XA, C_XG = 6144, 6240, 6336
C_CQ, C_CKV, C_KPA, C_KPB, C_GA, C_GB = 6592, 7104, 7616, 7680, 7744, 9792
NIN = 11840
PV = {}
_o = 0
for _n, _w in [("ffn1_norm", 16), ("mix_norm", 16), ("mu_r", 16), ("mu_k", 16), ("mu_v", 16), ("w0", 16),
               ("a0", 16), ("k_k", 16), ("k_a", 16), ("r_k", 16), ("gn_w", 16), ("gn_b", 16),
               ("ffn2_norm", 16), ("final_norm", 16), ("q_norm", 4), ("kv_norm", 4),
               ("mu_xw", 1), ("mu_xa", 1), ("mu_xg", 2)]:
    PV[_n] = _o
    _o += _w
NPV = _o


class DSem:
    def __init__(self, name):
        self.name = name
        self.count = 0


class Sched:
    ENG = ("pe", "act", "dve", "pool", "sp")

    def __init__(self):
        self.q = {e: [] for e in self.ENG}
        self.cnt = {e: 0 for e in ("pe", "act", "dve", "pool")}
        self.bufs = {}
        self.waited = {e: {} for e in self.ENG}
        self.dsems = []
        self.barrier_tokens = []

    def dsem(self, name):
        d = DSem("d%d_%s" % (len(self.dsems), name))
        self.dsems.append(d)
        return d

    def barrier(self):
        toks = [(e, c) for e, c in self.cnt.items() if c > 0]
        toks += [(d.name, d.count) for d in self.dsems if d.count > 0]
        self.barrier_tokens = toks

    def op(self, eng, fn, reads=(), writes=(), dsem=None):
        deps = {}

        def add(tok):
            if tok is None:
                return
            s, v = tok
            if deps.get(s, 0) < v:
                deps[s] = v
        for k in reads:
            b = self.bufs.get(k)
            if b:
                add(b[0])
        for k in writes:
            b = self.bufs.get(k)
            if b:
                add(b[0])
                for s, v in b[1].items():
                    add((s, v))
        for tok in self.barrier_tokens:
            add(tok)
        if dsem is not None:
            dsem.count += 16
            token = (dsem.name, dsem.count)
            signal = (dsem.name, 16)
        else:
            self.cnt[eng] += 1
            token = (eng, self.cnt[eng])
            signal = (eng, 1)
        waits = []
        wd = self.waited[eng]
        for s, v in deps.items():
            if wd.get(s, 0) < v:
                wd[s] = v
                waits.append((s, v))
        for k in reads:
            b = self.bufs.setdefault(k, [None, {}])
            if b[1].get(token[0], 0) < token[1]:
                b[1][token[0]] = token[1]
        for k in writes:
            self.bufs[k] = [token, {}]
        self.q[eng].append((waits, fn, signal))
        return token

    def final_wait(self, eng="sp"):
        self.barrier()
        waits = []
        for s, v in self.barrier_tokens:
            if self.waited[eng].get(s, 0) < v:
                waits.append((s, v))
        self.q[eng].append((waits, None, None))

    def emit(self, eng, e, semh):
        for waits, fn, signal in self.q[eng]:
            for s, v in waits:
                e.wait_ge(semh[s], v)
            if fn is None:
                continue
            inst = fn(e)
            inst.then_inc(semh[signal[0]], signal[1])


class Arena:
    def __init__(self, ap, nbytes):
        self.ap = ap
        self.nbytes = nbytes
        self.off = 0

    def reset(self, off=0):
        self.off = off

    def alloc(self, shape, dtype, parts=128):
        esz = 4 if dtype == F32 else 2
        n = 1
        for s in shape:
            n *= s
        nb = (n * esz + 63) // 64 * 64
        assert self.off + nb <= self.nbytes, ("arena overflow", self.off, nb, self.nbytes)
        v = self.ap[0:parts, self.off:self.off + nb]
        self.off += nb
        v = v[:, 0:n * esz].bitcast(dtype)
        if len(shape) == 2:
            v = v.rearrange("p (a b) -> p a b", a=shape[0])
        elif len(shape) == 3:
            v = v.rearrange("p (a b c) -> p a b c", a=shape[0], b=shape[1])
        return v


def build_program(debug=()):
    nc = bass.Bass("TRN2", target_bir_lowering=False)
    S = Sched()
    NB1, NTI1 = (1, 1) if "one_tile" in debug else ((1, NTI) if "one_b" in debug else (NB, NTI))

    def din(name, shape, dt=F32):
        return nc.dram_tensor(name, list(shape), dt, kind="ExternalInput").ap()

    def dscr(name, shape, dt=F32):
        kind = "ExternalOutput" if name in debug else "Internal"
        return nc.dram_tensor(name, list(shape), dt, kind=kind).ap()

    hT_in = din("hT", [NB, D, T])
    pvec_in = din("pvec", [128, NPV])
    consts_in = din("consts", [128, 3 * 128])
    wsrc = {
        "wg1": din("wg1", [D, DFF]), "wu1": din("wu1", [D, DFF]), "wd1": din("wd1", [DFF, D]),
        "win": din("win", [D, NIN]),
        "wup": din("wup", [96, D]), "aup": din("aup", [96, D]), "gup": din("gup", [256, D]),
        "wuq": din("wuq", [512, 4096]), "wk": din("wk", [512, D]), "wv": din("wv", [512, D]),
        "wout": din("wout", [D, D]),
        "wg2": din("wg2", [D, DFF]), "wu2": din("wu2", [D, DFF]), "wd2": din("wd2", [DFF, D]),
    }
    out_T = nc.dram_tensor("outT", [NB, D, T - 16], F32, kind="ExternalOutput").ap()
    wbf = {k: dscr("b_" + k, v.shape, BF16) for k, v in wsrc.items()}
    H1 = dscr("H1", [NB, D, T])
    PROJ = dscr("PROJ", [NB, NIN, T])
    SC = {k: dscr("SC_" + k, [NB, D, T]) for k in ("r", "w", "k", "v", "kn", "b")}
    GOUT = dscr("GOUT", [NB, D, T])
    BONUS = dscr("BONUS", [NB, D, T])
    YA = dscr("YA", [NB, D, T])
    YB = dscr("YB", [NB, D, T])

    ARENA_BYTES = 190 * 1024
    with ExitStack() as es:
        arena_t = es.enter_context(nc.sbuf_tensor("arena", [128, ARENA_BYTES], U8))
        psum_t = es.enter_context(nc.psum_tensor("psum", [128, 8, 512], F32))
        A = Arena(arena_t, ARENA_BYTES)

        def bank(i, n=512, parts=128):
            return psum_t[0:parts, i, 0:n]

        pvec = A.alloc([NPV], F32)
        consts = A.alloc([3 * 128], F32)
        cbf = A.alloc([2 * 128], BF16)
        base_off = A.off
        blk1 = consts[:, 0:128]
        ident_f = consts[:, 128:256]
        ident_b = cbf[:, 0:128]
        mask_b = cbf[:, 128:256]

        def pcol(name, c=0, parts=128):
            return pvec[0:parts, PV[name] + c:PV[name] + c + 1]

        d_pv = S.dsem("pvec")
        S.op("sp", lambda e: e.dma_start(out=pvec, in_=pvec_in), writes=["pvec"], dsem=d_pv)
        d_cs = S.dsem("consts")
        S.op("sp", lambda e: e.dma_start(out=consts, in_=consts_in), writes=["consts"], dsem=d_cs)
        S.op("dve", lambda e: e.tensor_copy(out=cbf, in_=consts[:, 128:384]), reads=["consts"], writes=["cbf"])

        LATE = ("wout", "wg2", "wu2", "wd2")

        def conv_items(names, CH, NS, engs, tag):
            st_f = [A.alloc([CH], F32) for _ in range(NS)]
            st_b = [A.alloc([CH], BF16) for _ in range(NS)]
            ds = [S.dsem("cv%s%d" % (tag, i)) for i in range(NS)]
            ds2 = [S.dsem("cvb%s%d" % (tag, i)) for i in range(NS)]
            items = []
            it = 0
            for name in names:
                src = wsrc[name]
                R, C = src.shape
                dst = wbf[name]
                nchunk = (C + CH - 1) // CH
                cw = (C + nchunk - 1) // nchunk
                for r0 in range(0, R, 128):
                    rp = min(128, R - r0)
                    for c0 in range(0, C, cw):
                        w = min(cw, C - c0)
                        s_ = it % NS
                        eng = engs[it % len(engs)]
                        it += 1

                        def emit(name=name, src=src, dst=dst, r0=r0, rp=rp, c0=c0, w=w, s_=s_, eng=eng):
                            f_ap = st_f[s_][0:rp, 0:w]
                            b_ap = st_b[s_][0:rp, 0:w]
                            fk = "cvf%s%d" % (tag, s_)
                            bk = "cvb%s%d" % (tag, s_)
                            S.op("sp", lambda e: e.dma_start(out=f_ap, in_=src[r0:r0 + rp, c0:c0 + w]),
                                 writes=[fk], dsem=ds[s_])
                            if eng == "act":
                                fn = lambda e: e.activation(out=b_ap, in_=f_ap, func=AF.Copy)
                            else:
                                fn = lambda e: e.tensor_copy(out=b_ap, in_=f_ap)
                            S.op(eng, fn, reads=[fk], writes=[bk])
                            S.op("sp", lambda e: e.dma_start(out=dst[r0:r0 + rp, c0:c0 + w], in_=b_ap),
                                 reads=[bk], writes=["W_" + name], dsem=ds2[s_])
                        items.append(emit)
            return items

        def phase0():
            A.reset(base_off)
            for emit in conv_items([n for n in wsrc if n not in LATE], 4096, 3, ["act", "dve", "pool"], "a"):
                emit()

        class Ctx:
            pass

        def rmsnorm(hT, uT, gname, hkey, ukey, sqs, rstd, pb0, ones_ap, nchunk=KT, dim=D, nt=NT, nsub=NSUB):
            for c in range(nchunk):
                sq = sqs[c % 2]
                sk = "sq%d" % (c % 2)
                S.op("act", lambda e, c=c, sq=sq: e.activation(out=sq[:, 0:nt], in_=hT[:, c, 0:nt], func=AF.Square),
                     reads=[hkey], writes=[sk])

                def mm(e, c=c, sq=sq):
                    i = None
                    for si, (n0, nw) in enumerate(nsub):
                        i = e.matmul(bank(pb0 + si, nw), ones_ap, sq[:, n0:n0 + nw],
                                     start=(c == 0), stop=(c == nchunk - 1))
                    return i
                S.op("pe", mm, reads=[sk, "consts", "ones"], writes=["ps%d" % (pb0 + si) for si in range(len(nsub))])
            for si, (n0, nw) in enumerate(nsub):
                S.op("dve", lambda e, si=si, n0=n0, nw=nw: e.tensor_scalar(
                    out=rstd[:, n0:n0 + nw], in0=bank(pb0 + si, nw), scalar1=1.0 / dim, scalar2=EPS,
                    op0=ALU.mult, op1=ALU.add), reads=["ps%d" % (pb0 + si)], writes=["rstd"])
            S.op("act", lambda e: e.activation(out=rstd[:, 0:nt], in_=rstd[:, 0:nt], func=AF.Sqrt),
                 reads=["rstd"], writes=["rstd"])
            S.op("dve", lambda e: e.reciprocal(out=rstd[:, 0:nt], in_=rstd[:, 0:nt]), reads=["rstd"], writes=["rstd"])
            for c in range(nchunk):
                S.op("dve", lambda e, c=c: e.scalar_tensor_tensor(
                    out=uT[:, c, 0:nt], in0=hT[:, c, 0:nt], scalar=pcol(gname, c), in1=rstd[:, 0:nt],
                    op0=ALU.mult, op1=ALU.mult), reads=[hkey, "rstd", "pvec"], writes=[ukey])

        psrr = [0]

        def gemm(wlist, kt, kp, panels, act, actkey, slots, evac, nsub, banks, tag):
            nw_ = len(wlist)
            npan = len(panels)

            def load(pi):
                c0, pw, _ = panels[pi]
                sl_ap, sl_key, sl_ds = slots[pi % len(slots)]
                for wi, (wname, wap) in enumerate(wlist):
                    if kt > 1:
                        src = wap.rearrange("(c p) m -> p c m", p=kp)[:, :, c0:c0 + pw]
                    else:
                        src = wap[:, c0:c0 + pw].unsqueeze(1)
                    S.op("sp", lambda e, sl_ap=sl_ap, wi=wi, src=src, pw=pw: e.dma_start(
                        out=sl_ap[0:kp, wi, 0:kt, 0:pw], in_=src),
                        reads=["W_" + wname], writes=[sl_key], dsem=sl_ds)
            load(0)
            for pi in range(npan):
                if pi + 1 < npan:
                    load(pi + 1)
                c0, pw, chunks = panels[pi]
                sl_ap, sl_key, sl_ds = slots[pi % len(slots)]
                for (cc0, cw) in chunks:
                    for si, (n0, nw) in enumerate(nsub):
                        bks = []
                        for wi in range(nw_):
                            bk = banks[psrr[0] % len(banks)]
                            psrr[0] += 1
                            bks.append(bk)

                            def mm(e, wi=wi, bk=bk, cc0=cc0, cw=cw, n0=n0, nw=nw, sl_ap=sl_ap, c0=c0):
                                i = None
                                for k in range(kt):
                                    i = e.matmul(bank(bk, nw, cw), sl_ap[0:kp, wi, k, cc0 - c0:cc0 - c0 + cw],
                                                 act[0:kp, k, n0:n0 + nw], start=(k == 0), stop=(k == kt - 1))
                                return i
                            S.op("pe", mm, reads=[sl_key, actkey], writes=["ps%d" % bk])
                        evac(cc0, cw, si, n0, nw, bks)

        def mk_panels(chunks, pw=256):
            panels = []
            cur = []
            for (c0, w) in chunks:
                if cur and (c0 + w - cur[0][0] > pw or cur[-1][0] + cur[-1][1] != c0):
                    panels.append((cur[0][0], cur[-1][0] + cur[-1][1] - cur[0][0], cur))
                    cur = []
                cur.append((c0, w))
            if cur:
                panels.append((cur[0][0], cur[-1][0] + cur[-1][1] - cur[0][0], cur))
            return panels

        def ffn(X, wg, wu, wd):
            pan = mk_panels([(c, 128) for c in range(0, DFF, 128)])

            def ev_gu(cc0, cw, si, n0, nw, bks):
                f = cc0 // 128
                tm = X.tmp[si % 2]
                tk = "tmp%d" % (si % 2)
                S.op("act", lambda e: e.activation(out=tm[:, 0:nw], in_=bank(bks[0], nw), func=AF.Silu),
                     reads=["ps%d" % bks[0]], writes=[tk])
                S.op("dve", lambda e: e.tensor_tensor(out=X.aT[:, f, n0:n0 + nw], in0=bank(bks[1], nw),
                                                      in1=tm[:, 0:nw], op=ALU.mult),
                     reads=["ps%d" % bks[1], tk], writes=["aT"])
            gemm([(wg, wbf[wg]), (wu, wbf[wu])], KT, 128, pan, X.uT, "uT", X.slots_gu, ev_gu, NSUB,
                 [0, 1, 2, 3, 4, 5], "gu")
            pan_d = mk_panels([(c, 128) for c in range(0, D, 128)], pw=128)

            def ev_d(cc0, cw, si, n0, nw, bks):
                m = cc0 // 128
                S.op("dve", lambda e: e.scalar_tensor_tensor(
                    out=X.hT[:, m, n0:n0 + nw], in0=bank(bks[0], nw), scalar=0.5, in1=X.hT[:, m, n0:n0 + nw],
                    op0=ALU.mult, op1=ALU.add), reads=["ps%d" % bks[0], "hT"], writes=["hT"])
            gemm([(wd, wbf[wd])], FT, 128, pan_d, X.aT, "aT", X.slots_d, ev_d, NSUB, [0, 1, 2, 3, 4, 5], "dn")

        def ffn_arena():
            A.reset(base_off)
            X = Ctx()
            X.hT = A.alloc([KT, NT], F32)
            X.uT = A.alloc([KT, NT], BF16)
            X.aT = A.alloc([FT, NT], BF16)
            X.sqs = [A.alloc([NT], F32) for _ in range(2)]
            X.rstd = A.alloc([NT], F32)
            X.tmp = [A.alloc([344], F32) for _ in range(2)]
            X.d_h = S.dsem("hT")
            slot_bytes = 2 * KT * 256 * 2
            X.slots_gu = []
            X.slots_d = []
            X.slots_in = []
            for i in range(2):
                off = A.off
                raw = A.alloc([slot_bytes // 2], BF16)
                ds = S.dsem("wslot%d_%d" % (i, len(S.dsems)))
                key = "wslot%d" % i
                X.slots_gu.append((raw[:, 0:2 * KT * 256].rearrange("p (w k m) -> p w k m", w=2, k=KT), key, ds))
                X.slots_d.append((raw[:, 0:FT * 128].rearrange("p (w k m) -> p w k m", w=1, k=FT), key, ds))
                X.slots_in.append((raw[:, 0:KT * 256].rearrange("p (w k m) -> p w k m", w=1, k=KT), key, ds))
            return X

        def phase1():
            X = ffn_arena()
            NST = 4
            stg = [A.alloc([NT], F32) for _ in range(NST)]
            dst = [S.dsem("stg%d_%d" % (i, len(S.dsems))) for i in range(NST)]
            chunks = [(c, 128) for c in range(0, C_XW, 128)]
            chunks += [(C_XW, 96), (C_XA, 96), (C_XG, 128), (C_XG + 128, 128)]
            chunks += [(c, 128) for c in range(C_CQ, C_KPA, 128)]
            chunks += [(C_KPA, 64), (C_KPB, 64)]
            chunks += [(c, 128) for c in range(C_GA, NIN, 128)]
            pan_in = mk_panels(chunks)
            ones_f = None
            for b in range(NB1):
                for ti in range(NTI1):
                    t0 = ti * NT
                    S.op("sp", lambda e, b=b, t0=t0: e.dma_start(
                        out=X.hT, in_=hT_in[b].rearrange("(c p) t -> p c t", p=128)[:, :, t0:t0 + NT]),
                        writes=["hT"], dsem=X.d_h)
                    rmsnorm(X.hT, X.uT, "ffn1_norm", "hT", "uT", X.sqs, X.rstd, 6, ones_all)
                    ffn(X, "wg1", "wu1", "wd1")
                    S.op("sp", lambda e, b=b, t0=t0: e.dma_start(
                        out=H1[b].rearrange("(c p) t -> p c t", p=128)[:, :, t0:t0 + NT], in_=X.hT),
                        reads=["hT"], writes=["H1"], dsem=X.d_h)
                    rmsnorm(X.hT, X.uT, "mix_norm", "hT", "uT", X.sqs, X.rstd, 6, ones_all)
                    cnt = [0]

                    def ev_in(cc0, cw, si, n0, nw, bks, b=b, t0=t0):
                        s = cnt[0] % NST
                        sk = "stg%d" % s
                        gate = cc0 >= C_GA
                        if gate:
                            S.op("act", lambda e: e.activation(out=stg[s][0:cw, n0:n0 + nw], in_=bank(bks[0], nw, cw),
                                                               func=AF.Sigmoid),
                                 reads=["ps%d" % bks[0]], writes=[sk])
                        else:
                            eng = "act" if (cnt[0] % 2 == 0) else "dve"
                            if eng == "act":
                                fn = lambda e: e.activation(out=stg[s][0:cw, n0:n0 + nw], in_=bank(bks[0], nw, cw),
                                                            func=AF.Copy)
                            else:
                                fn = lambda e: e.tensor_copy(out=stg[s][0:cw, n0:n0 + nw], in_=bank(bks[0], nw, cw))
                            S.op(eng, fn, reads=["ps%d" % bks[0]], writes=[sk])
                        if si == len(NSUB) - 1:
                            S.op("sp", lambda e: e.dma_start(out=PROJ[b, cc0:cc0 + cw, t0:t0 + NT],
                                                             in_=stg[s][0:cw, 0:NT]),
                                 reads=[sk], writes=["PROJ"], dsem=dst[s])
                            cnt[0] += 1
                    gemm([("win", wbf["win"])], KT, 128, pan_in, X.uT, "uT", X.slots_in, ev_in, NSUB,
                         [0, 1, 2, 3, 4, 5], "in")


        def phase2():
            A.reset(base_off)
            NP1 = NT + 1
            wup_s = A.alloc([D], BF16)
            aup_s = A.alloc([D], BF16)
            gup_s = A.alloc([2, D], BF16)
            d_l = S.dsem("lora")
            S.op("sp", lambda e: e.dma_start(out=wup_s[0:96, :], in_=wbf["wup"]), reads=["W_wup"], writes=["lw"], dsem=d_l)
            S.op("sp", lambda e: e.dma_start(out=aup_s[0:96, :], in_=wbf["aup"]), reads=["W_aup"], writes=["lw"], dsem=d_l)
            S.op("sp", lambda e: e.dma_start(out=gup_s, in_=wbf["gup"].rearrange("(c p) m -> p c m", p=128)),
                 reads=["W_gup"], writes=["lw"], dsem=d_l)
            raw = {n: [(A.alloc([NP1], F32), "raw_%s%d" % (n, i), S.dsem("raw_%s%d" % (n, i))) for i in range(2)]
                   for n in ("r", "k", "v")}
            xs = A.alloc([4, NP1], F32)
            d_xs = S.dsem("xs")
            txw = A.alloc([NT], BF16)
            txa = A.alloc([NT], BF16)
            tsg = A.alloc([2, NT], BF16)
            names = ["dtmp", "tmpf", "sh_r", "sh_k", "sh_v", "kk", "a_t", "wdec", "tmpA", "tmpB", "kn", "bt", "tq", "k2", "rk", "bon", "gsb"]
            W_ = {n: A.alloc([NT], F32) for n in names}
            DS = {n: S.dsem("o_" + n) for n in ("sh_r", "wdec", "k2", "sh_v", "kn", "bt", "bon", "gsb")}
            rr = [0]

            def nb_():
                b_ = [0, 1, 2, 3, 4, 5][rr[0] % 6]
                rr[0] += 1
                return b_

            def shift(dst, dkey, src, skey, mucol, parts=128):
                S.op("dve", lambda e: e.tensor_tensor(out=W_["dtmp"][0:parts, :], in0=src[0:parts, 0:NT], in1=src[0:parts, 1:NP1],
                                                      op=ALU.subtract), reads=[skey], writes=["dtmp"])
                S.op("dve", lambda e: e.scalar_tensor_tensor(out=dst, in0=W_["dtmp"][0:parts, :], scalar=mucol,
                                                             in1=src[0:parts, 1:NP1], op0=ALU.mult, op1=ALU.add),
                     reads=["dtmp", skey, "pvec"], writes=[dkey])

            def load_halo(dst, key, ds, b, row0, nrows, t0):
                if t0 == 0:
                    S.op("pool", lambda e: e.memset(dst[0:nrows, 0:1], 0.0), writes=[key])
                    S.op("sp", lambda e: e.dma_start(out=dst[0:nrows, 1:NP1], in_=PROJ[b, row0:row0 + nrows, 0:NT]),
                         reads=["PROJ"], writes=[key], dsem=ds)
                else:
                    S.op("sp", lambda e: e.dma_start(out=dst[0:nrows, 0:NP1], in_=PROJ[b, row0:row0 + nrows, t0 - 1:t0 + NT]),
                         reads=["PROJ"], writes=[key], dsem=ds)

            def mm_ev(mms, parts, evac):
                for (n0, nw) in NSUB:
                    bk = nb_()

                    def f(e, bk=bk, n0=n0, nw=nw):
                        i = None
                        for j, (l, rf) in enumerate(mms):
                            i = e.matmul(bank(bk, nw, parts), l, rf(n0, nw), start=(j == 0), stop=(j == len(mms) - 1))
                        return i
                    S.op("pe", f, reads=["lw", "txw", "txa", "tsg", "consts", "tmpB", "rk"], writes=["ps%d" % bk])
                    evac(bank(bk, nw, parts), n0, nw, "ps%d" % bk)

            def store(name, dst_ap):
                S.op("sp", lambda e: e.dma_start(out=dst_ap, in_=W_[name]), reads=[name], writes=["SCR"], dsem=DS[name])

            for b in range(NB1):
                for ti in range(NTI1):
                    t0 = ti * NT
                    for i, (r0, nr) in enumerate([(C_XW, 96), (C_XA, 96), (C_XG, 128), (C_XG + 128, 128)]):
                        load_halo(xs[:, i, :], "xs", d_xs, b, r0, nr, t0)
                    shift(W_["tmpf"][0:96, :], "tmpf", xs[:, 0, :], "xs", pcol("mu_xw", 0, 96), 96)
                    S.op("act", lambda e: e.activation(out=txw[0:96, :], in_=W_["tmpf"][0:96, :], func=AF.Tanh),
                         reads=["tmpf"], writes=["txw"])
                    shift(txa[0:96, :], "txa", xs[:, 1, :], "xs", pcol("mu_xa", 0, 96), 96)
                    for c in range(2):
                        shift(W_["tmpf"], "tmpf", xs[:, 2 + c, :], "xs", pcol("mu_xg", c))
                        S.op("act", lambda e, c=c: e.activation(out=tsg[:, c, :], in_=W_["tmpf"], func=AF.Sigmoid),
                             reads=["tmpf"], writes=["tsg"])
                    for m in range(KT):
                        ms = slice(m * 128, (m + 1) * 128)
                        for n, c0 in (("r", C_R), ("k", C_K), ("v", C_V)):
                            ap_, key, ds = raw[n][m % 2]
                            load_halo(ap_, key, ds, b, c0 + m * 128, 128, t0)
                            shift(W_["sh_" + n], "sh_" + n, ap_, key, pcol("mu_" + n, m))
                        mm_ev([(wup_s[0:96, ms], lambda n0, nw: txw[0:96, n0:n0 + nw])], 128,
                              lambda bp, n0, nw, bk, m=m: S.op("act", lambda e: e.activation(
                                  out=W_["tmpA"][:, n0:n0 + nw], in_=bp, func=AF.Sigmoid, bias=pcol("w0", m)),
                                  reads=[bk, "pvec"], writes=["tmpA"]))
                        S.op("act", lambda e: e.activation(out=W_["wdec"], in_=W_["tmpA"], func=AF.Exp, scale=-math.exp(-0.5)),
                             reads=["tmpA"], writes=["wdec"])
                        mm_ev([(aup_s[0:96, ms], lambda n0, nw: txa[0:96, n0:n0 + nw])], 128,
                              lambda bp, n0, nw, bk, m=m: S.op("act", lambda e: e.activation(
                                  out=W_["a_t"][:, n0:n0 + nw], in_=bp, func=AF.Sigmoid, bias=pcol("a0", m)),
                                  reads=[bk, "pvec"], writes=["a_t"]))
                        mm_ev([(gup_s[:, 0, ms], lambda n0, nw: tsg[:, 0, n0:n0 + nw]),
                               (gup_s[:, 1, ms], lambda n0, nw: tsg[:, 1, n0:n0 + nw])], 128,
                              lambda bp, n0, nw, bk: S.op("act", lambda e: e.activation(
                                  out=W_["gsb"][:, n0:n0 + nw], in_=bp, func=AF.Copy), reads=[bk], writes=["gsb"]))
                        S.op("dve", lambda e, m=m: e.tensor_scalar(out=W_["kk"], in0=W_["sh_k"], scalar1=pcol("k_k", m), scalar2=None,
                                                                   op0=ALU.mult), reads=["sh_k", "pvec"], writes=["kk"])
                        S.op("act", lambda e: e.activation(out=W_["tmpB"], in_=W_["kk"], func=AF.Square), reads=["kk"], writes=["tmpB"])
                        mm_ev([(blk1, lambda n0, nw: W_["tmpB"][:, n0:n0 + nw])], 128,
                              lambda bp, n0, nw, bk: S.op("dve", lambda e: e.tensor_scalar(
                                  out=W_["tq"][:, n0:n0 + nw], in0=bp, scalar1=1e-24, scalar2=None, op0=ALU.max),
                                  reads=[bk], writes=["tq"]))
                        S.op("act", lambda e: e.activation(out=W_["tq"], in_=W_["tq"], func=AF.Sqrt), reads=["tq"], writes=["tq"])
                        S.op("dve", lambda e: e.reciprocal(out=W_["tq"], in_=W_["tq"]), reads=["tq"], writes=["tq"])
                        S.op("dve", lambda e: e.scalar_tensor_tensor(out=W_["kn"], in0=W_["kk"], scalar=-1.0, in1=W_["tq"],
                                                                     op0=ALU.mult, op1=ALU.mult), reads=["kk", "tq"], writes=["kn"])
                        S.op("dve", lambda e: e.scalar_tensor_tensor(out=W_["bt"], in0=W_["kn"], scalar=-1.0, in1=W_["a_t"],
                                                                     op0=ALU.mult, op1=ALU.mult), reads=["kn", "a_t"], writes=["bt"])
                        S.op("dve", lambda e, m=m: e.tensor_scalar(out=W_["tq"], in0=W_["a_t"], scalar1=-1.0, scalar2=pcol("k_a", m),
                                                                   op0=ALU.add, op1=ALU.mult), reads=["a_t", "pvec"], writes=["tq"])
                        S.op("dve", lambda e: e.scalar_tensor_tensor(out=W_["k2"], in0=W_["tq"], scalar=1.0, in1=W_["sh_k"],
                                                                     op0=ALU.add, op1=ALU.mult), reads=["tq", "sh_k"], writes=["k2"])
                        S.op("dve", lambda e, m=m: e.scalar_tensor_tensor(out=W_["rk"], in0=W_["sh_r"], scalar=pcol("r_k", m), in1=W_["k2"],
                                                                          op0=ALU.mult, op1=ALU.mult), reads=["sh_r", "k2", "pvec"], writes=["rk"])
                        mm_ev([(blk1, lambda n0, nw: W_["rk"][:, n0:n0 + nw])], 128,
                              lambda bp, n0, nw, bk: S.op("dve", lambda e: e.tensor_tensor(
                                  out=W_["bon"][:, n0:n0 + nw], in0=bp, in1=W_["sh_v"][:, n0:n0 + nw], op=ALU.mult),
                                  reads=[bk, "sh_v"], writes=["bon"]))
                        tsl = slice(t0, t0 + NT)
                        store("sh_r", SC["r"][b, ms, tsl]); store("wdec", SC["w"][b, ms, tsl]); store("k2", SC["k"][b, ms, tsl])
                        store("sh_v", SC["v"][b, ms, tsl]); store("kn", SC["kn"][b, ms, tsl]); store("bt", SC["b"][b, ms, tsl])
                        store("bon", BONUS[b, ms, tsl]); store("gsb", GOUT[b, ms, tsl])

        def phase4():
            A.reset(base_off)
            TC = 32
            PH = NB1 * 32
            P2_ = 2 * PH
            sets = []
            for i in range(2):
                tl = {n: A.alloc([64, TC], F32) for n in ("r", "w", "k", "kn", "b")}
                tl["v"] = A.alloc([32, TC], F32)
                sets.append((tl, S.dsem("scin%d" % i), "scin%d" % i))
            ysets = [(A.alloc([32, TC], F32), S.dsem("yout%d" % i), "yout%d" % i) for i in range(2)]
            St = A.alloc([32, 64], F32)
            Sw = [A.alloc([32, 64], F32) for _ in range(2)]
            T1 = A.alloc([32, 64], F32)
            T2 = A.alloc([32, 64], F32)
            T3 = A.alloc([32, 64], F32)
            sa = A.alloc([32], F32)
            S.op("dve", lambda e: e.memset(St[0:P2_], 0.0), writes=["St"])
            nch = (T + TC - 1) // TC
            late_items = conv_items(list(LATE), 2048, 2, ["act"], "l")
            late_pos = [0]

            def load(c):
                tl, ds, key = sets[c % 2]
                t0 = c * TC
                tc = min(TC, T - t0)
                for n in ("r", "w", "k", "kn", "b"):
                    src = SC[n][0:NB1].rearrange("b (h j) t -> (b h) j t", j=64)
                    for half in range(2):
                        for jh in range(2):
                            S.op("sp", lambda e, n=n, half=half, jh=jh, src=src, tl=tl, t0=t0, tc=tc: e.dma_start(
                                out=tl[n][half * PH:(half + 1) * PH, jh * 32:(jh + 1) * 32, 0:tc],
                                in_=src[:, jh * 32:(jh + 1) * 32, t0:t0 + tc]), reads=["SCR"], writes=[key], dsem=ds)
                srcv = SC["v"][0:NB1].rearrange("b (h x i) t -> (b h) x i t", x=2, i=32)
                for half in range(2):
                    S.op("sp", lambda e, half=half, tl=tl, t0=t0, tc=tc: e.dma_start(
                        out=tl["v"][half * PH:(half + 1) * PH, :, 0:tc], in_=srcv[:, half, :, t0:t0 + tc]),
                        reads=["SCR"], writes=[key], dsem=ds)
            load(0)
            dsty = YA[0:NB1].rearrange("b (h x i) t -> (b h) x i t", x=2, i=32)
            step = 0
            for c in range(nch):
                if c + 1 < nch:
                    load(c + 1)
                for _ in range(3):
                    if late_pos[0] < len(late_items):
                        late_items[late_pos[0]]()
                        late_pos[0] += 1
                tl, ds, key = sets[c % 2]
                yt, yds, ykey = ysets[c % 2]
                t0 = c * TC
                tc = min(TC, T - t0)
                for tt in range(tc):
                    def bj(n):
                        return tl[n][0:P2_, :, tt].unsqueeze(1).broadcast_to([P2_, 32, 64])
                    vb = tl["v"][0:P2_, :, tt].unsqueeze(2).broadcast_to([P2_, 32, 64])
                    sw = Sw[step % 2]
                    swk = "Sw%d" % (step % 2)
                    S.op("pool", lambda e, vb=vb, kb=bj("k"): e.tensor_tensor(out=T3[0:P2_], in0=vb, in1=kb, op=ALU.mult),
                         reads=[key], writes=["T3"])
                    S.op("pool", lambda e, sw=sw, wb=bj("w"): e.tensor_tensor(out=sw[0:P2_], in0=St[0:P2_], in1=wb, op=ALU.mult),
                         reads=[key, "St"], writes=[swk])
                    S.op("pool", lambda e, sw=sw: e.tensor_tensor(out=sw[0:P2_], in0=sw[0:P2_], in1=T3[0:P2_], op=ALU.add),
                         reads=[swk, "T3"], writes=[swk])
                    S.op("dve", lambda e, knb=bj("kn"): e.tensor_tensor(out=T1[0:P2_], in0=St[0:P2_], in1=knb, op=ALU.mult),
                         reads=[key, "St"], writes=["T1"])
                    S.op("dve", lambda e: e.tensor_reduce(out=sa[0:P2_], in_=T1[0:P2_], axis=AX.X, op=ALU.add),
                         reads=["T1"], writes=["sa"])
                    S.op("dve", lambda e, bb=bj("b"): e.tensor_tensor(
                        out=T2[0:P2_], in0=sa[0:P2_].unsqueeze(2).broadcast_to([P2_, 32, 64]), in1=bb, op=ALU.mult),
                        reads=[key, "sa"], writes=["T2"])
                    S.op("dve", lambda e, sw=sw: e.tensor_tensor(out=St[0:P2_], in0=sw[0:P2_], in1=T2[0:P2_], op=ALU.add),
                         reads=[swk, "T2"], writes=["St"])
                    S.op("dve", lambda e, rb=bj("r"): e.tensor_tensor(out=T1[0:P2_], in0=St[0:P2_], in1=rb, op=ALU.mult),
                         reads=[key, "St"], writes=["T1"])
                    S.op("dve", lambda e, yt=yt, tt=tt: e.tensor_reduce(out=yt[0:P2_, :, tt], in_=T1[0:P2_], axis=AX.X, op=ALU.add),
                         reads=["T1"], writes=[ykey])
                    step += 1
                for half in range(2):
                    S.op("sp", lambda e, half=half, yt=yt, t0=t0, tc=tc: e.dma_start(
                        out=dsty[:, half, :, t0:t0 + tc], in_=yt[half * PH:(half + 1) * PH, :, 0:tc]),
                        reads=[ykey], writes=["YA"], dsem=yds)

            while late_pos[0] < len(late_items):
                late_items[late_pos[0]]()
                late_pos[0] += 1

        def phase3():
            A.reset(base_off)
            SUBS = [(0, 512), (512, 512), (1024, 512), (1536, 512), (2048, 16)]
            SCALE = 192.0 ** -0.5
            cqn = A.alloc([4, T], BF16)
            ckvn = A.alloc([4, T], BF16)
            kpeT = A.alloc([T], BF16)
            ropeT = A.alloc([2, T], F32)
            wuq_s = A.alloc([4, 4096], BF16)
            wk_s = A.alloc([4, D], BF16)
            wv_s = A.alloc([4, D], BF16)
            d_w = S.dsem("mlaw")
            I32 = mybir.dt.int32
            rawck = A.alloc([6, 512], F32)
            rawc = rawck[:, 0:4, :]
            rawk = rawck[:, 4:6, :]
            ybh = A.alloc([T], F32)
            r_i = cqn[:, 0:2, :].rearrange("p a t -> p (a t)").bitcast(I32)
            r_x = ybh
            r_k = rawck.rearrange("p a t -> p (a t)")[:, 0:T]
            r_s = A.alloc([8], F32)
            r_si = r_s.bitcast(I32)
            TWO_PI = 2.0 * math.pi
            S.op("pool", lambda e: e.iota(r_si[0:64, 6:7], [[0, 1]], base=0, channel_multiplier=1), writes=["r_s"])
            S.op("dve", lambda e: e.tensor_copy(out=r_s[0:64, 0:1], in_=r_si[0:64, 6:7]), reads=["r_s"], writes=["r_s"])
            S.op("dve", lambda e: e.tensor_scalar(out=r_s[0:64, 1:2], in0=r_s[0:64, 0:1], scalar1=31.5, scalar2=None, op0=ALU.is_gt),
                 reads=["r_s"], writes=["r_s"])
            S.op("dve", lambda e: e.scalar_tensor_tensor(out=r_s[0:64, 2:3], in0=r_s[0:64, 1:2], scalar=-32.0, in1=r_s[0:64, 0:1],
                                                         op0=ALU.mult, op1=ALU.add), reads=["r_s"], writes=["r_s"])
            S.op("act", lambda e: e.activation(out=r_s[0:64, 3:4], in_=r_s[0:64, 2:3], func=AF.Exp, scale=-math.log(10000.0) / 32.0),
                 reads=["r_s"], writes=["r_s"])
            S.op("dve", lambda e: e.tensor_scalar(out=r_s[0:64, 4:5], in0=r_s[0:64, 1:2], scalar1=-math.pi / 2, scalar2=math.pi / 2,
                                                  op0=ALU.mult, op1=ALU.add), reads=["r_s"], writes=["r_s"])
            S.op("dve", lambda e: e.tensor_scalar(out=r_s[0:64, 5:6], in0=r_s[0:64, 4:5], scalar1=math.pi / 2, scalar2=None, op0=ALU.add),
                 reads=["r_s"], writes=["r_s"])
            S.op("pool", lambda e: e.iota(r_i[0:64, :], [[1, T]], base=0, channel_multiplier=0), writes=["cqn"])
            S.op("dve", lambda e: e.tensor_copy(out=r_x[0:64, :], in_=r_i[0:64, :]), reads=["cqn"], writes=["ybh"])
            S.op("dve", lambda e: e.tensor_scalar(out=r_x[0:64, :], in0=r_x[0:64, :], scalar1=r_s[0:64, 3:4], scalar2=None, op0=ALU.mult),
                 reads=["ybh", "r_s"], writes=["ybh"])
            for a_ in range(2):
                S.op("dve", lambda e, a_=a_: e.tensor_scalar(out=r_k[0:64, :], in0=r_x[0:64, :], scalar1=r_s[0:64, 4 + a_:5 + a_],
                                                             scalar2=None, op0=ALU.add), reads=["ybh", "r_s"], writes=["rawc", "rawk"])
                S.op("dve", lambda e: e.tensor_scalar(out=r_i[0:64, :], in0=r_k[0:64, :], scalar1=1.0 / TWO_PI, scalar2=None, op0=ALU.mult),
                     reads=["rawc", "rawk"], writes=["cqn"])
                S.op("dve", lambda e, a_=a_: e.tensor_copy(out=ropeT[0:64, a_, :], in_=r_i[0:64, :]), reads=["cqn"], writes=["rope"])
                S.op("dve", lambda e, a_=a_: e.scalar_tensor_tensor(out=r_k[0:64, :], in0=ropeT[0:64, a_, :], scalar=-TWO_PI, in1=r_k[0:64, :],
                                                                    op0=ALU.mult, op1=ALU.add), reads=["rope", "rawc", "rawk"], writes=["rawc", "rawk"])
                S.op("dve", lambda e, a_=a_: e.tensor_scalar(out=ropeT[0:64, a_, :], in0=r_k[0:64, :], scalar1=math.pi, scalar2=None, op0=ALU.is_gt),
                     reads=["rawc", "rawk"], writes=["rope"])
                S.op("dve", lambda e, a_=a_: e.scalar_tensor_tensor(out=r_k[0:64, :], in0=ropeT[0:64, a_, :], scalar=-TWO_PI, in1=r_k[0:64, :],
                                                                    op0=ALU.mult, op1=ALU.add), reads=["rope", "rawc", "rawk"], writes=["rawc", "rawk"])
                S.op("dve", lambda e, a_=a_: e.tensor_scalar(out=ropeT[0:64, a_, :], in0=r_k[0:64, :], scalar1=-math.pi, scalar2=None, op0=ALU.is_lt),
                     reads=["rawc", "rawk"], writes=["rope"])
                S.op("dve", lambda e, a_=a_: e.scalar_tensor_tensor(out=r_k[0:64, :], in0=ropeT[0:64, a_, :], scalar=TWO_PI, in1=r_k[0:64, :],
                                                                    op0=ALU.mult, op1=ALU.add), reads=["rope", "rawc", "rawk"], writes=["rawc", "rawk"])
                S.op("act", lambda e, a_=a_: e.activation(out=ropeT[0:64, a_, :], in_=r_k[0:64, :], func=AF.Sin), reads=["rawc", "rawk"], writes=["rope"])
            S.op("sp", lambda e: e.dma_start(out=wuq_s, in_=wbf["wuq"].rearrange("(c p) m -> p c m", p=128)),
                 reads=["W_wuq"], writes=["mlaw"], dsem=d_w)
            S.op("sp", lambda e: e.dma_start(out=wk_s, in_=wbf["wk"].rearrange("(c p) m -> p c m", p=128)),
                 reads=["W_wk"], writes=["mlaw"], dsem=d_w)
            S.op("sp", lambda e: e.dma_start(out=wv_s, in_=wbf["wv"].rearrange("(c p) m -> p c m", p=128)),
                 reads=["W_wv"], writes=["mlaw"], dsem=d_w)
            d_rc = S.dsem("rawc")
            d_rk = S.dsem("rawk")
            sq2 = [A.alloc([512], F32) for _ in range(2)]
            rs = A.alloc([512], F32)
            t1 = A.alloc([512], F32)
            t2 = A.alloc([512], F32)
            qnT = A.alloc([T], BF16)
            qpeT = A.alloc([T], BF16)
            knT = A.alloc([T], BF16)
            Vh = A.alloc([17, 128], BF16)
            Pts = [A.alloc([16 + 2048], BF16) for _ in range(2)]
            pti = [0]
            PTs = [A.alloc([128], BF16) for _ in range(4)]
            d_yb = S.dsem("ybh")
            st_ = A.alloc([16], F32)
            pt_bank = [psum_t[:, 5, 0:256].bitcast(BF16), psum_t[:, 7, 0:256].bitcast(BF16)]

            def norm_lat(b, row0, dstT, gname, dkey):
                for (n0, nw) in SUBS:
                    S.op("sp", lambda e, n0=n0, nw=nw: e.dma_start(
                        out=rawc[:, :, 0:nw], in_=PROJ[b, row0:row0 + 512, n0:n0 + nw].rearrange("(c p) t -> p c t", p=128)),
                        reads=["PROJ"], writes=["rawc"], dsem=d_rc)
                    for c in range(4):
                        S.op("act", lambda e, c=c, nw=nw: e.activation(out=sq2[c % 2][:, 0:nw], in_=rawc[:, c, 0:nw], func=AF.Square),
                             reads=["rawc"], writes=["sqm%d" % (c % 2)])
                        S.op("pe", lambda e, c=c, nw=nw: e.matmul(bank(6, nw), ones_all, sq2[c % 2][:, 0:nw], start=(c == 0), stop=(c == 3)),
                             reads=["sqm%d" % (c % 2), "ones"], writes=["ps6"])
                    S.op("dve", lambda e, nw=nw: e.tensor_scalar(out=rs[:, 0:nw], in0=bank(6, nw), scalar1=1.0 / 512, scalar2=EPS,
                                                                 op0=ALU.mult, op1=ALU.add), reads=["ps6"], writes=["rs"])
                    S.op("act", lambda e, nw=nw: e.activation(out=rs[:, 0:nw], in_=rs[:, 0:nw], func=AF.Sqrt), reads=["rs"], writes=["rs"])
                    S.op("dve", lambda e, nw=nw: e.reciprocal(out=rs[:, 0:nw], in_=rs[:, 0:nw]), reads=["rs"], writes=["rs"])
                    for c in range(4):
                        S.op("dve", lambda e, c=c, n0=n0, nw=nw: e.scalar_tensor_tensor(
                            out=dstT[:, c, n0:n0 + nw], in0=rawc[:, c, 0:nw], scalar=pcol(gname, c), in1=rs[:, 0:nw],
                            op0=ALU.mult, op1=ALU.mult), reads=["rawc", "rs", "pvec"], writes=[dkey])

            def rope_comb(dst, dkey, a_ap, b_ap, n0, nw, rkeys):
                S.op("dve", lambda e: e.tensor_tensor(out=t1[0:64, 0:nw], in0=a_ap, in1=ropeT[0:64, 0, n0:n0 + nw], op=ALU.mult),
                     reads=rkeys + ["rope"], writes=["t1"])
                S.op("dve", lambda e: e.tensor_tensor(out=t2[0:64, 0:nw], in0=b_ap, in1=ropeT[0:64, 1, n0:n0 + nw], op=ALU.mult),
                     reads=rkeys + ["rope"], writes=["t2"])
                S.op("dve", lambda e: e.tensor_tensor(out=dst[0:64, n0:n0 + nw], in0=t1[0:64, 0:nw], in1=t2[0:64, 0:nw], op=ALU.add),
                     reads=["t1", "t2"], writes=[dkey])

            brr = [0]

            def proj(lhs_fn, M, src, skey, evac):
                for (n0, nw) in SUBS:
                    bk = [0, 1, 2, 3][brr[0] % 4]
                    brr[0] += 1

                    def f(e, bk=bk, n0=n0, nw=nw):
                        i = None
                        for k in range(4):
                            i = e.matmul(bank(bk, nw, M), lhs_fn(k), src[:, k, n0:n0 + nw], start=(k == 0), stop=(k == 3))
                        return i
                    S.op("pe", f, reads=["mlaw", skey], writes=["ps%d" % bk])
                    evac(bank(bk, nw, M), "ps%d" % bk, n0, nw)

            for b in range(NB1):
                norm_lat(b, C_CQ, cqn, "q_norm", "cqn")
                norm_lat(b, C_CKV, ckvn, "kv_norm", "ckvn")
                for (n0, nw) in SUBS:
                    S.op("sp", lambda e, n0=n0, nw=nw, b=b: e.dma_start(out=rawk[0:64, 0, 0:nw], in_=PROJ[b, C_KPA:C_KPA + 64, n0:n0 + nw]),
                         reads=["PROJ"], writes=["rawk"], dsem=d_rk)
                    S.op("sp", lambda e, n0=n0, nw=nw, b=b: e.dma_start(out=rawk[0:64, 1, 0:nw], in_=PROJ[b, C_KPB:C_KPB + 64, n0:n0 + nw]),
                         reads=["PROJ"], writes=["rawk"], dsem=d_rk)
                    rope_comb(kpeT, "kpeT", rawk[0:64, 0, 0:nw], rawk[0:64, 1, 0:nw], n0, nw, ["rawk"])
                for h in range(MH):
                    proj(lambda k, h=h: wuq_s[:, k, h * 256:h * 256 + 128], 128, cqn, "cqn",
                         lambda bp, bk, n0, nw: S.op("act", lambda e: e.activation(out=qnT[:, n0:n0 + nw], in_=bp, func=AF.Copy),
                                                     reads=[bk], writes=["qnT"]))
                    for (n0, nw) in SUBS:
                        bka, bkb = 0, 1

                        def fa(e, n0=n0, nw=nw, h=h):
                            i = None
                            for k in range(4):
                                i = e.matmul(bank(0, nw, 64), wuq_s[:, k, h * 256 + 128:h * 256 + 192], cqn[:, k, n0:n0 + nw],
                                             start=(k == 0), stop=(k == 3))
                            for k in range(4):
                                i = e.matmul(bank(1, nw, 64), wuq_s[:, k, h * 256 + 192:h * 256 + 256], cqn[:, k, n0:n0 + nw],
                                             start=(k == 0), stop=(k == 3))
                            return i
                        S.op("pe", fa, reads=["mlaw", "cqn"], writes=["ps0", "ps1"])
                        rope_comb(qpeT, "qpeT", bank(0, nw, 64), bank(1, nw, 64), n0, nw, ["ps0", "ps1"])
                    proj(lambda k, h=h: wk_s[:, k, h * 128:(h + 1) * 128], 128, ckvn, "ckvn",
                         lambda bp, bk, n0, nw: S.op("act", lambda e: e.activation(out=knT[:, n0:n0 + nw], in_=bp, func=AF.Copy),
                                                     reads=[bk], writes=["knT"]))
                    for kb in range(17):
                        tk0, tw = (0, 16) if kb == 0 else (16 + 128 * (kb - 1), 128)
                        bk = [2, 3][kb % 2]

                        def fv(e, bk=bk, tk0=tk0, tw=tw, h=h):
                            i = None
                            for k in range(4):
                                i = e.matmul(bank(bk, 128, tw), ckvn[:, k, tk0:tk0 + tw], wv_s[:, k, h * 128:(h + 1) * 128],
                                             start=(k == 0), stop=(k == 3))
                            return i
                        S.op("pe", fv, reads=["mlaw", "ckvn"], writes=["ps%d" % bk])
                        S.op("dve", lambda e, bk=bk, kb=kb, tw=tw: e.tensor_copy(out=Vh[0:tw, kb, :], in_=bank(bk, 128, tw)),
                             reads=["ps%d" % bk], writes=["Vh"])
                    def qinfo(qb):
                        q0, qw = (0, 16) if qb == 0 else (16 + 128 * (qb - 1), 128)
                        return q0, qw, 128 * qb

                    def emit_S(qb):
                        q0, qw, nreal = qinfo(qb)
                        nkc = (nreal + 511) // 512

                        def fs(e, q0=q0, qw=qw, qb=qb, nreal=nreal, nkc=nkc):
                            i = e.matmul(bank(4, 16, qw), qnT[:, q0:q0 + qw], knT[:, 0:16], start=True, stop=False)
                            i = e.matmul(bank(4, 16, qw), qpeT[0:64, q0:q0 + qw], kpeT[0:64, 0:16], start=False, stop=(qb != 0))
                            if qb == 0:
                                i = e.matmul(bank(4, 16, 16), ident_b[0:16, 0:16], mask_b[0:16, 0:16], start=False, stop=True)
                            for kc in range(nkc):
                                k0 = 16 + 512 * kc
                                kw = min(512, nreal - 512 * kc)
                                last = (kc == nkc - 1)
                                i = e.matmul(bank(kc, kw, qw), qnT[:, q0:q0 + qw], knT[:, k0:k0 + kw], start=True, stop=False)
                                i = e.matmul(bank(kc, kw, qw), qpeT[0:64, q0:q0 + qw], kpeT[0:64, k0:k0 + kw], start=False, stop=not last)
                                if last:
                                    off = kw - 128
                                    i = e.matmul(psum_t[0:qw, kc, off:off + 128], ident_b, mask_b, start=False, stop=True)
                            return i
                        S.op("pe", fs, reads=["qnT", "qpeT", "knT", "kpeT", "cbf"], writes=["ps0", "ps1", "ps2", "ps3", "ps4"])

                    def emit_softmax(qb):
                        q0, qw, nreal = qinfo(qb)
                        par = qb % 2
                        P_ = Pts[par]
                        pk = "Pt%d" % par
                        sk_ = "st%d" % par
                        so = 8 * par
                        sc = lambda j: st_[0:qw, so + j:so + j + 1]
                        sreal = psum_t[0:qw, 0:4, :].rearrange("p a b -> p (a b)")[:, 0:max(nreal, 1)]
                        S.op("dve", lambda e: e.tensor_reduce(out=sc(1), in_=bank(4, 16, qw), axis=AX.X, op=ALU.max),
                             reads=["ps4"], writes=[sk_])
                        if qb > 0:
                            S.op("dve", lambda e: e.tensor_reduce(out=sc(0), in_=sreal, axis=AX.X, op=ALU.max),
                                 reads=["ps0", "ps1", "ps2", "ps3"], writes=[sk_])
                            S.op("dve", lambda e: e.tensor_tensor(out=sc(2), in0=sc(0), in1=sc(1), op=ALU.max),
                                 reads=[sk_], writes=[sk_])
                        else:
                            S.op("dve", lambda e: e.tensor_copy(out=sc(2), in_=sc(1)), reads=[sk_], writes=[sk_])
                        S.op("dve", lambda e: e.tensor_scalar(out=sc(3), in0=sc(2), scalar1=-SCALE, scalar2=None, op0=ALU.mult),
                             reads=[sk_], writes=[sk_])
                        S.op("act", lambda e: e.activation(out=P_[0:qw, 0:16], in_=bank(4, 16, qw), func=AF.Exp, bias=sc(3),
                                                           scale=SCALE, accum_out=sc(5)),
                             reads=["ps4", sk_], writes=[pk, sk_])
                        if qb > 0:
                            S.op("act", lambda e: e.activation(out=P_[0:qw, 16:16 + nreal], in_=sreal, func=AF.Exp, bias=sc(3),
                                                               scale=SCALE, accum_out=sc(4)),
                                 reads=["ps0", "ps1", "ps2", "ps3", sk_], writes=[pk, sk_])
                            S.op("dve", lambda e: e.tensor_tensor(out=sc(6), in0=sc(4), in1=sc(5), op=ALU.add),
                                 reads=[sk_], writes=[sk_])
                        else:
                            S.op("dve", lambda e: e.tensor_copy(out=sc(6), in_=sc(5)), reads=[sk_], writes=[sk_])
                        S.op("dve", lambda e: e.reciprocal(out=sc(7), in_=sc(6)), reads=[sk_], writes=[sk_])
                        S.op("dve", lambda e: e.tensor_scalar(out=P_[0:qw, 0:16 + nreal], in0=P_[0:qw, 0:16 + nreal],
                                                              scalar1=sc(7), scalar2=None, op0=ALU.mult),
                             reads=[pk, sk_], writes=[pk])

                    def emit_PV(qb):
                        q0, qw, nreal = qinfo(qb)
                        par = qb % 2
                        P_ = Pts[par]
                        pk = "Pt%d" % par
                        nkb = qb + 1
                        slots = {}

                        def emit_T(kb):
                            c0, kw = (0, 16) if kb == 0 else (16 + 128 * (kb - 1), 128)
                            s4 = pti[0] % 2
                            pti[0] += 1
                            slots[kb] = s4
                            ptp = pt_bank[s4][:, 0:128]
                            S.op("pe", lambda e: e.transpose(ptp[0:kw, 0:qw], P_[0:qw, c0:c0 + kw], ident_b[0:qw, 0:qw]),
                                 reads=[pk, "cbf"], writes=["ptp%d" % s4])
                            if kb % 2 == 0:
                                fn = lambda e: e.activation(out=PTs[s4][0:kw, 0:qw], in_=ptp[0:kw, 0:qw], func=AF.Copy)
                                eng = "act"
                            else:
                                fn = lambda e: e.tensor_copy(out=PTs[s4][0:kw, 0:qw], in_=ptp[0:kw, 0:qw])
                                eng = "dve"
                            S.op(eng, fn, reads=["ptp%d" % s4], writes=["PTs%d" % s4])
                        LA = 1
                        for kb in range(min(LA, nkb)):
                            emit_T(kb)
                        for kb in range(nkb):
                            if kb + LA < nkb:
                                emit_T(kb + LA)
                            kw = 16 if kb == 0 else 128
                            s4 = slots[kb]
                            S.op("pe", lambda e, kb=kb, kw=kw, s4=s4: e.matmul(
                                bank(6, qw), Vh[0:kw, kb, :], PTs[s4][0:kw, 0:qw], start=(kb == 0), stop=(kb == nkb - 1)),
                                reads=["Vh", "PTs%d" % s4], writes=["ps6"])
                        S.op("act", lambda e: e.activation(out=ybh[:, q0:q0 + qw], in_=bank(6, qw), func=AF.Copy),
                             reads=["ps6"], writes=["ybh"])

                    emit_S(0)
                    for qb in range(17):
                        emit_softmax(qb)
                        if qb + 1 < 17:
                            emit_S(qb + 1)
                        emit_PV(qb)
                    S.op("sp", lambda e, b=b, h=h: e.dma_start(out=YB[b, h * 128:(h + 1) * 128, :], in_=ybh),
                         reads=["ybh"], writes=["YB"], dsem=d_yb)
                if b == 0 and "DBG_kpe" in debug:
                    for nm, ap_, key_, parts in (("DBG_kpe", kpeT, "kpeT", 64), ("DBG_kn", knT, "knT", 128),
                                                 ("DBG_qpe", qpeT, "qpeT", 64), ("DBG_qn", qnT, "qnT", 128)):
                        dd = dscr(nm, [128, T], BF16)
                        S.op("sp", lambda e, dd=dd, ap_=ap_, parts=parts: e.dma_start(out=dd[0:parts, :], in_=ap_[0:parts, :]),
                             reads=[key_], writes=[nm], dsem=S.dsem(nm))

        def phase5():
            X = ffn_arena()
            NL = 6
            ld = [[(A.alloc([NT], F32), "ld%d_%d" % (i, j), S.dsem("ld%d_%d" % (i, j))) for j in range(NL)] for i in range(1)]
            wk_ = {"sq0": X.sqs[0], "sq1": X.sqs[1], "rstd": X.rstd}
            zT = X.aT[:, 0:KT, :]
            outf = X.aT[:, 0:32, :].rearrange("p a n -> p (a n)").bitcast(F32).rearrange("p (a n) -> p a n", a=KT)
            d_o = S.dsem("outf")
            pan_o = mk_panels([(c, 128) for c in range(0, D, 128)])
            rr = [0]
            for b in range(NB1):
                for ti in range(NTI1):
                    t0 = ti * NT
                    tsl = slice(t0, t0 + NT)
                    S.op("sp", lambda e, b=b, t0=t0: e.dma_start(
                        out=X.hT, in_=H1[b].rearrange("(c p) t -> p c t", p=128)[:, :, t0:t0 + NT]),
                        reads=["H1"], writes=["hT"], dsem=X.d_h)
                    for m in range(KT):
                        ms = slice(m * 128, (m + 1) * 128)
                        srcs = [YA[b, ms, tsl], BONUS[b, ms, tsl], GOUT[b, ms, tsl],
                                PROJ[b, C_GA + m * 128:C_GA + (m + 1) * 128, tsl], PROJ[b, C_GB + m * 128:C_GB + (m + 1) * 128, tsl],
                                YB[b, ms, tsl]]
                        L = ld[0]
                        for j, sap in enumerate(srcs):
                            S.op("sp", lambda e, j=j, sap=sap: e.dma_start(out=L[j][0], in_=sap),
                                 reads=["YA", "SCR", "PROJ", "YB"], writes=[L[j][1]], dsem=L[j][2])
                        y_, bo_, g_, ga_, gb_, yb_ = [L[j][0] for j in range(6)]
                        yk, bok, gk, gak, gbk, ybk = [L[j][1] for j in range(6)]

                        def blkmm(src_ap, skey, evac):
                            for (n0, nw) in NSUB:
                                bk = [6, 7][rr[0] % 2]
                                rr[0] += 1
                                S.op("pe", lambda e, bk=bk, n0=n0, nw=nw: e.matmul(bank(bk, nw), blk1, src_ap[:, n0:n0 + nw], start=True, stop=True),
                                     reads=[skey, "consts"], writes=["ps%d" % bk])
                                evac(bank(bk, nw), "ps%d" % bk, n0, nw)
                        blkmm(y_, yk, lambda bp, bk, n0, nw: S.op("dve", lambda e: e.scalar_tensor_tensor(
                            out=wk_["sq0"][:, n0:n0 + nw], in0=bp, scalar=-1.0 / 64, in1=y_[:, n0:n0 + nw], op0=ALU.mult, op1=ALU.add),
                            reads=[bk, yk], writes=["sq0"]))
                        S.op("act", lambda e: e.activation(out=wk_["sq1"], in_=wk_["sq0"], func=AF.Square), reads=["sq0"], writes=["sq1"])
                        blkmm(wk_["sq1"], "sq1", lambda bp, bk, n0, nw: S.op("dve", lambda e: e.tensor_scalar(
                            out=wk_["rstd"][:, n0:n0 + nw], in0=bp, scalar1=1.0 / 64, scalar2=GN_EPS, op0=ALU.mult, op1=ALU.add),
                            reads=[bk], writes=["rstd"]))
                        S.op("act", lambda e: e.activation(out=wk_["rstd"], in_=wk_["rstd"], func=AF.Sqrt), reads=["rstd"], writes=["rstd"])
                        S.op("dve", lambda e: e.reciprocal(out=wk_["rstd"], in_=wk_["rstd"]), reads=["rstd"], writes=["rstd"])
                        S.op("dve", lambda e: e.tensor_tensor(out=wk_["sq0"], in0=wk_["sq0"], in1=wk_["rstd"], op=ALU.mult),
                             reads=["sq0", "rstd"], writes=["sq0"])
                        S.op("dve", lambda e, m=m: e.tensor_scalar(out=wk_["sq0"], in0=wk_["sq0"], scalar1=pcol("gn_w", m), scalar2=pcol("gn_b", m),
                                                                   op0=ALU.mult, op1=ALU.add), reads=["sq0", "pvec"], writes=["sq0"])
                        S.op("pool", lambda e: e.tensor_tensor(out=wk_["sq0"], in0=wk_["sq0"], in1=bo_, op=ALU.add), reads=["sq0", bok], writes=["sq0"])
                        S.op("pool", lambda e: e.tensor_tensor(out=wk_["sq0"], in0=wk_["sq0"], in1=g_, op=ALU.mult), reads=["sq0", gk], writes=["sq0"])
                        S.op("pool", lambda e: e.tensor_tensor(out=wk_["sq0"], in0=wk_["sq0"], in1=ga_, op=ALU.mult), reads=["sq0", gak], writes=["sq0"])
                        S.op("pool", lambda e: e.tensor_tensor(out=wk_["sq1"], in0=yb_, in1=gb_, op=ALU.mult), reads=[ybk, gbk], writes=["sq1"])
                        S.op("dve", lambda e, m=m: e.tensor_tensor(out=zT[:, m, :], in0=wk_["sq0"], in1=wk_["sq1"], op=ALU.add),
                             reads=["sq0", "sq1"], writes=["aT"])

                    def ev_o(cc0, cw, si, n0, nw, bks):
                        m = cc0 // 128
                        S.op("dve", lambda e: e.tensor_tensor(out=X.hT[:, m, n0:n0 + nw], in0=bank(bks[0], nw), in1=X.hT[:, m, n0:n0 + nw],
                                                              op=ALU.add), reads=["ps%d" % bks[0], "hT"], writes=["hT"])
                    gemm([("wout", wbf["wout"])], KT, 128, pan_o, zT, "aT", X.slots_in, ev_o, NSUB, [0, 1, 2, 3, 4, 5], "wo")
                    rmsnorm(X.hT, X.uT, "ffn2_norm", "hT", "uT", X.sqs, X.rstd, 6, ones_all)
                    ffn(X, "wg2", "wu2", "wd2")
                    rmsnorm(X.hT, outf, "final_norm", "hT", "aT", X.sqs, X.rstd, 6, ones_all)
                    if ti == 0:
                        S.op("sp", lambda e, b=b: e.dma_start(out=out_T[b].rearrange("(c p) t -> p c t", p=128)[:, :, 0:NT - 16],
                                                              in_=outf[:, :, 16:NT]), reads=["aT"], writes=["OUT"], dsem=d_o)
                    else:
                        S.op("sp", lambda e, b=b, t0=t0: e.dma_start(out=out_T[b].rearrange("(c p) t -> p c t", p=128)[:, :, t0 - 16:t0 - 16 + NT],
                                                                     in_=outf), reads=["aT"], writes=["OUT"], dsem=d_o)

        ones_all = None
        ones_all = A.alloc([128], F32)
        base_off = A.off
        S.op("pool", lambda e: e.memset(ones_all, 1.0), writes=["ones"])

        phase0()
        S.barrier()
        if "stop0" not in debug:
            phase1()
            S.barrier()
        if "stop1" not in debug and "stop0" not in debug:
            phase2()
            S.barrier()
            if "no3" not in debug:
                phase3()
                S.barrier()
            if "no4" not in debug:
                phase4()
                S.barrier()
            if "no5" not in debug:
                phase5()
                S.barrier()

        S.final_wait("sp")

        semh = {}
        for e_ in ("pe", "act", "dve", "pool"):
            semh[e_] = es.enter_context(nc.semaphore("s_" + e_))
        for d in S.dsems:
            semh[d.name] = es.enter_context(nc.semaphore(d.name))
        block = es.enter_context(nc.Block())

        @block.tensor
        def _(e):
            S.emit("pe", e, semh)

        @block.scalar
        def _(e):
            S.emit("act", e, semh)

        @block.vector
        def _(e):
            S.emit("dve", e, semh)

        @block.gpsimd
        def _(e):
            S.emit("pool", e, semh)

        @block.sync
        def _(e):
            S.emit("sp", e, semh)
    return nc


def host_prep(inp):
    f = lambda a: np.ascontiguousarray(np.asarray(a, dtype=np.float32))
    x = f(inp["x"])
    meta = f(inp["meta_tokens"])
    w_in = f(inp["w_in"])[0]
    kpe = w_in[:, 7616:7680]
    win_ext = np.concatenate([w_in[:, :7616], kpe[:, :32], kpe[:, :32], kpe[:, 32:], kpe[:, 32:], w_in[:, 7680:]], axis=1)
    wuq = f(inp["w_uq"])[0].reshape(512, 16, 192)
    wuq_ext = np.concatenate([wuq[:, :, :128], wuq[:, :, 128:160], wuq[:, :, 128:160], wuq[:, :, 160:192],
                              wuq[:, :, 160:192]], axis=2).reshape(512, 4096)
    wukv = f(inp["w_ukv"])[0].reshape(512, 16, 256)
    wk = np.ascontiguousarray(wukv[:, :, :128].reshape(512, 2048))
    wv = np.ascontiguousarray(wukv[:, :, 128:].reshape(512, 2048))
    pv = np.zeros((128, NPV), np.float32)

    def put(name, vec, n):
        v = f(vec).reshape(-1)
        if v.size >= 128:
            pv[:, PV[name]:PV[name] + n] = v.reshape(n, 128).T
        else:
            pv[:v.size, PV[name]] = v
    mu = f(inp["tm_mu"])[0]
    put("ffn1_norm", inp["ffn1_norm"], 16); put("mix_norm", inp["mix_norm"], 16)
    put("mu_r", mu[0:2048], 16); put("mu_k", mu[2048:4096], 16); put("mu_v", mu[4096:6144], 16)
    put("w0", inp["w0"], 16); put("a0", inp["a0"], 16); put("k_k", inp["k_k"], 16); put("k_a", inp["k_a"], 16)
    put("r_k", inp["r_k"], 16); put("gn_w", inp["gn_w"], 16); put("gn_b", inp["gn_b"], 16)
    put("ffn2_norm", inp["ffn2_norm"], 16); put("final_norm", inp["final_norm"], 16)
    put("q_norm", inp["q_norm"], 4); put("kv_norm", inp["kv_norm"], 4)
    put("mu_xw", mu[6144:6240], 1); put("mu_xa", mu[6240:6336], 1); put("mu_xg", mu[6336:6592], 2)
    consts = np.zeros((128, 384), np.float32)
    consts[:64, :64] = 1.0
    consts[64:, 64:128] = 1.0
    consts[:, 128:256] = np.eye(128, dtype=np.float32)
    qi = np.arange(128)[:, None]
    ki = np.arange(128)[None, :]
    consts[:, 256:384] = np.where(ki <= qi, 0.0, -30000.0).astype(np.float32)
    shared = {
        "pvec": pv, "consts": consts,
        "wg1": f(inp["ffn1_w_gate"])[0], "wu1": f(inp["ffn1_w_up"])[0], "wd1": f(inp["ffn1_w_down"])[0],
        "win": np.ascontiguousarray(win_ext),
        "wup": f(inp["w_up"])[0], "aup": f(inp["a_up"])[0], "gup": f(inp["g_up"])[0],
        "wuq": np.ascontiguousarray(wuq_ext), "wk": wk, "wv": wv, "wout": f(inp["w_out"])[0],
        "wg2": f(inp["ffn2_w_gate"])[0], "wu2": f(inp["ffn2_w_up"])[0], "wd2": f(inp["ffn2_w_down"])[0],
    }
    in_maps = []
    for c in range(NCORES):
        hT = np.empty((NB, D, T), np.float32)
        for j in range(NB):
            b = c * NB + j
            hT[j, :, :16] = meta.T
            hT[j, :, 16:] = x[b].T
        m = dict(shared)
        m["hT"] = hT
        in_maps.append(m)
    return in_maps


def kernel(**inputs):
    in_maps = host_prep(inputs)
    nc = build_program()
    res = run_bass_kernel_spmd(nc, in_maps, core_ids=list(range(NCORES)))
    out = np.empty((NCORES * NB, T - 16, D), np.float32)
    for c in range(NCORES):
        oT = np.asarray(res.results[c]["outT"])
        for j in range(NB):
            out[c * NB + j] = oT[j].T
    return out
```
